# Optimizing a Trainium2 kernel written in Bass

```python
import math
import jax, jax.numpy as jnp
from jax import lax
import numpy as np

D_MODEL = 2048
BATCH = 4
SEQ = 4096
DEPTH = 4

GRID_W = 64
EPS = 1e-6
ROPE_THETA = 10000.0
MIX_HALF = D_MODEL // 2
GLA_DV = 128
GLA_DK = GLA_DV // 2
GLA_HEADS = MIX_HALF // GLA_DV
GLA_LOWRANK = 16
GLA_TAU = 16.0
GLA_CHUNK = 64
NA_DH = 128
NA_HEADS = MIX_HALF // NA_DH
NA_KH = 8
NA_KW = 16
SSD_INNER = MIX_HALF
SSD_HEADDIM = 64
SSD_HEADS = SSD_INNER // SSD_HEADDIM
SSD_GROUPS = 2
SSD_STATE = 128
SSD_CONV = 5
SSD_CHUNK = 128
SSD_CONV_DIM = SSD_INNER + 2 * SSD_GROUPS * SSD_STATE
SWA_DH = 128
SWA_HEADS = MIX_HALF // SWA_DH
SWA_KV_HEADS = SWA_HEADS // 4
SWA_WINDOW = 128
SWA_BLOCK = 128
FFN_HIDDEN = -(-8 * D_MODEL // (3 * 256)) * 256

N_EVEN = (DEPTH + 1) // 2
N_ODD = DEPTH // 2
EVEN_SPLITS = (GLA_HEADS * GLA_DK, GLA_HEADS * GLA_DK, GLA_HEADS * GLA_DV, GLA_HEADS * GLA_DV,
               2 * GLA_LOWRANK, NA_HEADS * NA_DH, NA_HEADS * NA_DH, NA_HEADS * NA_DH)
EVEN_IN = sum(EVEN_SPLITS)
EVEN_MIX = GLA_HEADS * GLA_DV + NA_HEADS * NA_DH
ODD_SPLITS = (SSD_INNER, SSD_CONV_DIM, 2 * SSD_HEADS, SWA_HEADS * SWA_DH,
              SWA_KV_HEADS * SWA_DH, SWA_KV_HEADS * SWA_DH)
ODD_IN = sum(ODD_SPLITS)
ODD_MIX = SSD_INNER + SWA_HEADS * SWA_DH

kernel_name = 'bidir_hybrid_gla_na_ssd_swa'


def split_cols(t, sizes):
    idx = [int(i) for i in np.cumsum(sizes)[:-1]]
    return jnp.split(t, idx, axis=-1)


def rms_norm(x, g):
    xf = x.astype(jnp.float32)
    y = xf * lax.rsqrt(jnp.mean(xf * xf, axis=-1, keepdims=True) + EPS)
    return (y * g.astype(jnp.float32)).astype(x.dtype)


def rope(x, pos):
    half = x.shape[-1] // 2
    inv = ROPE_THETA ** (-jnp.arange(half, dtype=jnp.float32) / half)
    ang = pos.astype(jnp.float32)[:, None] * inv[None, :]
    cos, sin = jnp.cos(ang), jnp.sin(ang)
    xf = x.astype(jnp.float32)
    x1, x2 = xf[..., :half], xf[..., half:]
    return jnp.concatenate([x1 * cos - x2 * sin, x2 * cos + x1 * sin], axis=-1).astype(x.dtype)


def flip_t(t, axis):
    return jnp.flip(t, axis=axis)


def gla_scan(q, k, v, log_a):
    bn, h, t, dk = q.shape
    dv = v.shape[-1]
    c = GLA_CHUNK
    n = t // c

    def to_chunks(a):
        return jnp.moveaxis(a.astype(jnp.float32).reshape(bn, h, n, c, a.shape[-1]), 2, 0)

    qc, kc, vc, ac = to_chunks(q), to_chunks(k), to_chunks(v), to_chunks(log_a)
    causal = jnp.tril(jnp.ones((c, c), dtype=bool))

    def step(s_prev, inp):
        qi, ki, vi, ai = inp
        b = jnp.cumsum(ai, axis=2)
        diff = b[:, :, :, None, :] - b[:, :, None, :, :]
        decay = jnp.exp(jnp.where(causal[:, :, None], diff, -jnp.inf))
        attn = jnp.einsum('bhtc,bhsc,bhtsc->bhts', qi, ki, decay)
        o = (jnp.einsum('bhts,bhsv->bhtv', attn, vi)
             + jnp.einsum('bhtc,bhcv->bhtv', qi * jnp.exp(b), s_prev))
        b_last = b[:, :, -1:, :]
        s_new = (jnp.exp(b_last[:, :, 0, :])[..., None] * s_prev
                 + jnp.einsum('bhsc,bhsv->bhcv', ki * jnp.exp(b_last - b), vi))
        return s_new, o

    s0 = jnp.zeros((bn, h, dk, dv), jnp.float32)
    _, o = lax.scan(step, s0, (qc, kc, vc, ac))
    return jnp.moveaxis(o, 0, 2).reshape(bn, h, t, dv)


def gla_mixer(q, k, v, g, lr, w_decay, b_decay, norm_g):
    bn, t, _ = q.shape

    def heads(a, d):
        return a.reshape(bn, t, GLA_HEADS, d).transpose(0, 2, 1, 3)

    qh = heads(q, GLA_DK) * (GLA_DK ** -0.5)
    kh = heads(k, GLA_DK)
    vh = heads(v, GLA_DV)
    lr = lr.reshape(bn, t, 2, GLA_LOWRANK)
    z = jnp.einsum('btdr,drk->dbtk', lr, w_decay) + b_decay[:, None, None, :]
    log_a = jax.nn.log_sigmoid(z.astype(jnp.float32)) / GLA_TAU
    o_f = gla_scan(qh, kh, vh, heads(log_a[0], GLA_DK))
    o_b = flip_t(gla_scan(flip_t(qh, 2), flip_t(kh, 2), flip_t(vh, 2),
                          flip_t(heads(log_a[1], GLA_DK), 2)), 2)
    o = o_f + o_b
    o = o * lax.rsqrt(jnp.mean(o * o, axis=-1, keepdims=True) + EPS)
    o = o.transpose(0, 2, 1, 3).reshape(bn, t, GLA_HEADS * GLA_DV) * norm_g.astype(jnp.float32)
    return (o * jax.nn.silu(g.astype(jnp.float32))).astype(q.dtype)


def na_mixer(q, k, v, rpb):
    bn, t, _ = q.shape
    rows = t // GRID_W
    kh_ = min(NA_KH, rows)

    def heads(a):
        return a.reshape(bn, t, NA_HEADS, NA_DH).transpose(0, 2, 1, 3)

    qh = heads(q) * (NA_DH ** -0.5)
    khd = heads(k)
    vhd = heads(v)
    cols = jnp.arange(GRID_W)
    col_start = jnp.clip(cols - NA_KW // 2, 0, GRID_W - NA_KW)
    col_idx = col_start[:, None] + jnp.arange(NA_KW)[None, :]
    dc = col_idx - cols[:, None] + (NA_KW - 1)

    def row_block(r):
        row_start = jnp.clip(r - kh_ // 2, 0, rows - kh_)
        row_idx = row_start + jnp.arange(kh_)
        dr = row_idx - r + (NA_KH - 1)
        tok = (row_idx[None, :, None] * GRID_W + col_idx[:, None, :]).reshape(GRID_W, kh_ * NA_KW)
        qb = lax.dynamic_slice_in_dim(qh, r * GRID_W, GRID_W, axis=2)
        kb = khd[:, :, tok]
        vb = vhd[:, :, tok]
        bias = rpb[:, dr[None, :, None], dc[:, None, :]].reshape(NA_HEADS, GRID_W, kh_ * NA_KW)
        s = jnp.einsum('bhqd,bhqkd->bhqk', qb, kb).astype(jnp.float32) + bias[None].astype(jnp.float32)
        p = jax.nn.softmax(s, axis=-1)
        return jnp.einsum('bhqk,bhqkd->bhqd', p.astype(vb.dtype), vb)

    o = lax.map(row_block, jnp.arange(rows))
    return o.transpose(1, 0, 3, 2, 4).reshape(bn, t, NA_HEADS * NA_DH)


def depthwise_conv(x, w, b):
    kw, cc = w.shape
    y = lax.conv_general_dilated(x, w[:, None, :].astype(x.dtype), window_strides=(1,),
                                 padding=[(kw // 2, kw // 2)],
                                 dimension_numbers=('NWC', 'WIO', 'NWC'),
                                 feature_group_count=cc)
    return y + b.astype(x.dtype)


def ssd_scan(xdt, da, bm, cm):
    bn, t, g, j, p = xdt.shape
    s = bm.shape[-1]
    l = SSD_CHUNK
    n = t // l
    x = xdt.reshape(bn, n, l, g, j, p)
    bc = bm.reshape(bn, n, l, g, s)
    cc = cm.reshape(bn, n, l, g, s)
    a_cum = jnp.cumsum(da.reshape(bn, n, l, g, j), axis=2)
    tri = jnp.tril(jnp.ones((l, l), dtype=bool))
    seg = a_cum[:, :, :, None] - a_cum[:, :, None]
    lmat = jnp.exp(jnp.where(tri[:, :, None, None], seg, -jnp.inf))
    cb = jnp.einsum('bnlgs,bnmgs->bnlmg', cc, bc)
    y_diag = jnp.einsum('bnlmg,bnlmgj,bnmgjp->bnlgjp', cb, lmat, x)
    decay_states = jnp.exp(a_cum[:, :, -1:] - a_cum)
    states = jnp.einsum('bnlgs,bnlgj,bnlgjp->bngjps', bc, decay_states, x)
    a_tot = a_cum[:, :, -1]
    a_tot_cum = jnp.cumsum(a_tot, axis=1)
    excl = a_tot_cum - a_tot
    tri_n = jnp.tril(jnp.ones((n, n), dtype=bool), k=-1)
    cdiff = excl[:, :, None] - a_tot_cum[:, None]
    cdec = jnp.exp(jnp.where(tri_n[:, :, None, None], cdiff, -jnp.inf))
    prev = jnp.einsum('bnmgj,bmgjps->bngjps', cdec, states)
    y_off = jnp.einsum('bnlgs,bngjps,bnlgj->bnlgjp', cc, prev, jnp.exp(a_cum))
    return (y_diag + y_off).reshape(bn, t, g, j, p)


def ssd_mixer(z, xbc, dt_raw, conv_w, conv_b, dt_bias, a_log, d_skip, norm_g):
    bn, t, _ = z.shape
    jh = SSD_HEADS // SSD_GROUPS
    xbc = jax.nn.silu(depthwise_conv(xbc, conv_w, conv_b)).astype(jnp.float32)
    xs, bm, cm = split_cols(xbc, (SSD_INNER, SSD_GROUPS * SSD_STATE, SSD_GROUPS * SSD_STATE))
    xh = xs.reshape(bn, t, SSD_GROUPS, jh, SSD_HEADDIM)
    bm = bm.reshape(bn, t, SSD_GROUPS, SSD_STATE)
    cm = cm.reshape(bn, t, SSD_GROUPS, SSD_STATE)
    dt = jax.nn.softplus(dt_raw.astype(jnp.float32).reshape(bn, t, 2, SSD_HEADS)
                         + dt_bias.astype(jnp.float32))
    a = -jnp.exp(a_log.astype(jnp.float32))
    dt_f = dt[:, :, 0].reshape(bn, t, SSD_GROUPS, jh)
    dt_b = dt[:, :, 1].reshape(bn, t, SSD_GROUPS, jh)
    da_f = dt_f * a[0].reshape(SSD_GROUPS, jh)
    da_b = dt_b * a[1].reshape(SSD_GROUPS, jh)
    y_f = ssd_scan(xh * dt_f[..., None], da_f, bm, cm)
    y_b = flip_t(ssd_scan(flip_t(xh * dt_b[..., None], 1), flip_t(da_b, 1),
                          flip_t(bm, 1), flip_t(cm, 1)), 1)
    y = y_f + y_b + xh * d_skip.astype(jnp.float32).reshape(SSD_GROUPS, jh)[:, :, None]
    y = y.reshape(bn, t, SSD_INNER) * jax.nn.silu(z.astype(jnp.float32))
    return rms_norm(y, norm_g).astype(z.dtype)


def swa_mixer(q, k, v, sink):
    bn, t, _ = q.shape
    gq = SWA_HEADS // SWA_KV_HEADS
    wb = SWA_BLOCK
    n = t // wb
    pos = jnp.arange(t)
    qh = rope(q.reshape(bn, t, SWA_HEADS, SWA_DH).transpose(0, 2, 1, 3), pos) * (SWA_DH ** -0.5)
    kh = rope(k.reshape(bn, t, SWA_KV_HEADS, SWA_DH).transpose(0, 2, 1, 3), pos)
    vh = v.reshape(bn, t, SWA_KV_HEADS, SWA_DH).transpose(0, 2, 1, 3)
    qb = qh.reshape(bn, SWA_KV_HEADS, gq, n, wb, SWA_DH)
    pad = ((0, 0), (0, 0), (wb, wb), (0, 0))

    def band(a):
        ap = jnp.pad(a, pad).reshape(bn, SWA_KV_HEADS, n + 2, wb, SWA_DH)
        return jnp.concatenate([ap[:, :, :-2], ap[:, :, 1:-1], ap[:, :, 2:]], axis=3)

    kb, vb = band(kh), band(vh)
    qi = pos.reshape(n, wb)
    kj = (jnp.arange(n)[:, None] - 1) * wb + jnp.arange(3 * wb)[None, :]
    valid = ((kj[:, None, :] >= 0) & (kj[:, None, :] < t)
             & (jnp.abs(qi[:, :, None] - kj[:, None, :]) <= SWA_WINDOW))
    s = jnp.einsum('bkgnqd,bkncd->bkgnqc', qb, kb).astype(jnp.float32)
    s = jnp.where(valid, s, -jnp.inf)
    sink_l = jnp.broadcast_to(sink.astype(jnp.float32).reshape(SWA_KV_HEADS, gq)[None, :, :, None, None, None],
                              s.shape[:-1] + (1,))
    p = jax.nn.softmax(jnp.concatenate([s, sink_l], axis=-1), axis=-1)[..., :-1]
    o = jnp.einsum('bkgnqc,bkncd->bkgnqd', p.astype(vb.dtype), vb)
    return o.transpose(0, 3, 4, 1, 2, 5).reshape(bn, t, SWA_HEADS * SWA_DH)


def setup_inputs(seed: int = 0) -> dict:
    key = jax.random.key(seed)
    ks = jax.random.split(key, 24)
    f32 = jnp.float32

    def nrm(k, shape, scale):
        return jax.random.normal(k, shape, f32) * scale

    ne, no = N_EVEN, N_ODD
    dt0 = jnp.exp(jax.random.uniform(ks[17], (no, 2, SSD_HEADS), f32,
                                     minval=math.log(1e-3), maxval=math.log(1e-1)))
    return {
        'x': nrm(ks[0], (BATCH, SEQ, D_MODEL), 1.0),
        'norm_gains': 1.0 + nrm(ks[1], (DEPTH, 4, D_MODEL), 0.05),
        'ffn_w_gate': nrm(ks[2], (DEPTH, D_MODEL, FFN_HIDDEN), D_MODEL ** -0.5),
        'ffn_w_up': nrm(ks[3], (DEPTH, D_MODEL, FFN_HIDDEN), D_MODEL ** -0.5),
        'ffn_w_down': nrm(ks[4], (DEPTH, FFN_HIDDEN, D_MODEL), FFN_HIDDEN ** -0.5),
        'even_w_in': nrm(ks[5], (ne, D_MODEL, EVEN_IN), D_MODEL ** -0.5),
        'even_w_out': nrm(ks[6], (ne, EVEN_MIX, D_MODEL), EVEN_MIX ** -0.5),
        'gla_w_decay': nrm(ks[7], (ne, 2, GLA_LOWRANK, GLA_HEADS * GLA_DK), GLA_LOWRANK ** -0.5),
        'gla_b_decay': nrm(ks[8], (ne, 2, GLA_HEADS * GLA_DK), 0.1),
        'gla_norm': 1.0 + nrm(ks[9], (ne, GLA_HEADS * GLA_DV), 0.05),
        'na_rpb': nrm(ks[10], (ne, NA_HEADS, 2 * NA_KH - 1, 2 * NA_KW - 1), 0.1),
        'odd_w_in': nrm(ks[11], (no, D_MODEL, ODD_IN), D_MODEL ** -0.5),
        'odd_w_out': nrm(ks[12], (no, ODD_MIX, D_MODEL), ODD_MIX ** -0.5),
        'ssd_conv_w': nrm(ks[13], (no, SSD_CONV, SSD_CONV_DIM), SSD_CONV ** -0.5),
        'ssd_conv_b': nrm(ks[14], (no, SSD_CONV_DIM), 0.02),
        'ssd_dt_bias': dt0 + jnp.log(-jnp.expm1(-dt0)),
        'ssd_a_log': jnp.log(jax.random.uniform(ks[15], (no, 2, SSD_HEADS), f32, minval=1.0, maxval=16.0)),
        'ssd_d': 1.0 + nrm(ks[16], (no, SSD_HEADS), 0.1),
        'ssd_norm': 1.0 + nrm(ks[18], (no, SSD_INNER), 0.05),
        'swa_sink': nrm(ks[19], (no, SWA_HEADS), 0.5),
    }


def reference(x, norm_gains, ffn_w_gate, ffn_w_up, ffn_w_down,
              even_w_in, even_w_out, gla_w_decay, gla_b_decay, gla_norm, na_rpb,
              odd_w_in, odd_w_out, ssd_conv_w, ssd_conv_b, ssd_dt_bias, ssd_a_log,
              ssd_d, ssd_norm, swa_sink):
    for layer in range(DEPTH):
        g = norm_gains[layer]
        h = rms_norm(x, g[0])
        i = layer // 2
        if layer % 2 == 0:
            proj = h @ even_w_in[i]
            gq, gk, gv, gg, glr, nq, nk, nv = split_cols(proj, EVEN_SPLITS)
            o_a = gla_mixer(gq, gk, gv, gg, glr, gla_w_decay[i], gla_b_decay[i], gla_norm[i])
            o_b = na_mixer(nq, nk, nv, na_rpb[i])
            mix = jnp.concatenate([o_a, o_b], axis=-1) @ even_w_out[i]
        else:
            proj = h @ odd_w_in[i]
            sz, sxbc, sdt, wq, wk, wv = split_cols(proj, ODD_SPLITS)
            o_c = ssd_mixer(sz, sxbc, sdt, ssd_conv_w[i], ssd_conv_b[i], ssd_dt_bias[i],
                            ssd_a_log[i], ssd_d[i], ssd_norm[i])
            o_d = swa_mixer(wq, wk, wv, swa_sink[i])
            mix = jnp.concatenate([o_c, o_d], axis=-1) @ odd_w_out[i]
        x = x + rms_norm(mix, g[1])
        h = rms_norm(x, g[2])
        f = (jax.nn.silu(h @ ffn_w_gate[layer]) * (h @ ffn_w_up[layer])) @ ffn_w_down[layer]
        x = x + rms_norm(f, g[3])
    return x
```

```python
import contextlib
import numpy as np
import concourse.bass as bass
import concourse.mybir as mybir
from concourse.bass_utils import run_bass_kernel_spmd

F32 = mybir.dt.float32
BF16 = mybir.dt.bfloat16
AF = mybir.ActivationFunctionType
ALU = mybir.AluOpType

D = 2048
T_FULL = 4096
HID = 5632
EPS = 1e-6
EVEN_IN = 6176
ODD_IN = 4128

ENGS = ["pe", "act", "dve", "pool", "sp"]
DMA_ENGS = ["sp", "pool", "act"]
NDMASEM = 12


class Prog:
    def __init__(self, nc):
        self.nc = nc
        self.outer = contextlib.ExitStack()
        self.sems = {}
        for e in ENGS:
            self.sems[e] = self.outer.enter_context(nc.semaphore("s_" + e))
        for e in DMA_ENGS:
            for j in range(NDMASEM):
                self.sems[("dma", e, j)] = self.outer.enter_context(nc.semaphore(f"d_{e}_{j}"))
        self.cnt = {e: 0 for e in ENGS}
        self.dma_cnt = {}
        self.dma_rr = {e: 0 for e in ENGS}
        self.known = {e: {} for e in ENGS}
        self.n_ops = 0
        self._reset()

    def _reset(self):
        self.ops = {e: [] for e in ENGS}
        self.last_w = {}
        self.readers = {}

    def op(self, eng, fn, reads=(), writes=(), dma=False):
        deps = {}

        def note(t):
            sk, v, src = t
            cur = deps.get(sk)
            if cur is None or cur[0] < v:
                deps[sk] = (v, src)

        for k in reads:
            t = self.last_w.get(k)
            if t is not None:
                note(t)
        for k in writes:
            t = self.last_w.get(k)
            if t is not None:
                note(t)
            for t in self.readers.get(k, ()):
                note(t)
        if dma:
            j = self.dma_rr[eng]
            self.dma_rr[eng] = (j + 1) % NDMASEM
            sk = ("dma", eng, j)
            c = self.dma_cnt.get(sk, 0)
            if c > 0:
                note((sk, 16 * c, "dma"))
            self.dma_cnt[sk] = c + 1
            tok = (sk, 16 * (c + 1), "dma")
            inc = (sk, 16)
        else:
            self.cnt[eng] += 1
            tok = (eng, self.cnt[eng], eng)
            inc = (eng, 1)
        waits = []
        kn = self.known[eng]
        for sk, (v, src) in deps.items():
            if src == "pe" and eng == "pe" and not dma:
                continue
            if kn.get(sk, 0) >= v:
                continue
            kn[sk] = v
            waits.append((sk, v))
        self.ops[eng].append((fn, waits, inc))
        self.n_ops += 1
        for k in reads:
            self.readers.setdefault(k, []).append(tok)
        for k in writes:
            self.last_w[k] = tok
            self.readers[k] = []
        return tok

    @contextlib.contextmanager
    def phase(self):
        st = contextlib.ExitStack()
        self._reset()
        try:
            yield st
            self._flush()
        finally:
            st.close()

    def _flush(self):
        nc = self.nc
        bar = [(e, self.cnt[e]) for e in ENGS if self.cnt[e] > 0]
        bar += [(sk, 16 * c) for sk, c in self.dma_cnt.items() if c > 0]
        sems = self.sems
        with nc.Block() as block:
            def run(e):
                def body(eng):
                    for (fn, waits, inc) in self.ops[e]:
                        for sk, v in waits:
                            eng.wait_ge(sems[sk], v)
                        fn(eng).then_inc(sems[inc[0]], inc[1])
                    kn = self.known[e]
                    for sk, v in bar:
                        if kn.get(sk, 0) < v:
                            eng.wait_ge(sems[sk], v)
                            kn[sk] = v
                return body

            block.tensor(run("pe"))
            block.scalar(run("act"))
            block.vector(run("dve"))
            block.gpsimd(run("pool"))
            block.sync(run("sp"))
        self._reset()

    def close(self):
        self.outer.close()


_UID = [0]


def sb(st, nc, name, shape, dtype):
    _UID[0] += 1
    return st.enter_context(nc.sbuf_tensor(f"sb{_UID[0]}_{name}", list(shape), dtype))


def pst(st, nc, name, shape, dtype):
    _UID[0] += 1
    return st.enter_context(nc.psum_tensor(f"ps{_UID[0]}_{name}", list(shape), dtype))


def MM(P, out, lhsT, rhs, start, stop, r, w):
    P.op("pe", lambda e: e.matmul(out, lhsT=lhsT, rhs=rhs, start=start, stop=stop), r, w)


def TR(P, out, in_, ident, r, w):
    P.op("pe", lambda e: e.transpose(out=out, in_=in_, identity=ident), r, w)


def ACT(P, out, in_, func, r, w, bias=None, scale=None, accum=None):
    kw = {}
    if bias is not None:
        kw["bias"] = bias
    if scale is not None:
        kw["scale"] = scale
    if accum is not None:
        kw["accum_out"] = accum
    P.op("act", lambda e: e.activation(out=out, in_=in_, func=func, **kw), r, w)


def TT(P, eng, out, in0, in1, op, r, w):
    P.op(eng, lambda e: e.tensor_tensor(out=out, in0=in0, in1=in1, op=op), r, w)


def TS(P, eng, out, in0, s1, s2, op0, op1, r, w):
    if s2 is None:
        P.op(eng, lambda e: e.tensor_scalar(out=out, in0=in0, scalar1=s1, scalar2=None, op0=op0), r, w)
    else:
        P.op(eng, lambda e: e.tensor_scalar(out=out, in0=in0, scalar1=s1, scalar2=s2, op0=op0, op1=op1), r, w)


def STT(P, out, in0, scalar, in1, op0, op1, r, w):
    P.op("dve", lambda e: e.scalar_tensor_tensor(out=out, in0=in0, scalar=scalar, in1=in1, op0=op0, op1=op1), r, w)


def CP(P, eng, out, in_, r, w):
    if eng == "act":
        P.op("act", lambda e: e.copy(out=out, in_=in_), r, w)
    else:
        P.op(eng, lambda e: e.tensor_copy(out=out, in_=in_), r, w)


def RECIP(P, out, in_, r, w):
    P.op("dve", lambda e: e.reciprocal(out=out, in_=in_), r, w)


def DMA(P, eng, out, in_, r, w):
    P.op(eng, lambda e: e.dma_start(out=out, in_=in_), r, w, dma=True)


def MEMSET(P, eng, ap, val, r, w):
    P.op(eng, lambda e: e.memset(ap, val), r, w)


def rstd_from_ss(P, col, n, key):
    ACT(P, col[:, 1:2], col[:, 0:1], AF.Sqrt, [key], [key], bias=EPS, scale=1.0 / n)
    RECIP(P, col[:, 2:3], col[:, 1:2], [key], [key])


def phase_A(P, C, x_src, gT, W, blocks, T):
    nc = P.nc
    TA = 1024 if T >= 1024 else T
    NSUB = TA // 128
    NH = TA // 512
    with P.phase() as st:
        xa = sb(st, nc, "xa", [128, 2, D], F32)
        hbf = sb(st, nc, "hbfA", [128, 2, D], BF16)
        hT = sb(st, nc, "hT", [128, 16, TA], BF16)
        wsl = sb(st, nc, "wslA", [128, 4, 8192], BF16)
        stg = sb(st, nc, "stgA", [128, 4, 512], F32)
        stb = sb(st, nc, "stbA", [128, 4, 512], BF16)
        col = sb(st, nc, "colA", [128, 2, 4], F32)
        ps = pst(st, nc, "psA", [128, 4096], F32)
        slot_i = 0
        stg_i = 0
        bank_i = 0
        ev_i = 0
        for tile in range(T // TA):
            for sub in range(NSUB):
                j = sub % 2
                t0 = tile * TA + sub * 128
                kx, kh, kc = ("xa", j), ("hbf", j), ("col", j)
                DMA(P, "sp", xa[:, j, :], x_src[t0:t0 + 128, :], [], [kx])
                ACT(P, hbf[:, j, :], xa[:, j, :], AF.Square, [kx], [kh, kc], accum=col[:, j, 0:1])
                rstd_from_ss(P, col[:, j, :], D, kc)
                TS(P, "dve", hbf[:, j, :], xa[:, j, :], col[:, j, 2:3], None, ALU.mult, None, [kx, kc], [kh])
                for kg in range(4):
                    bank = 6 + (kg % 2)
                    kb = ("ps", bank)
                    ptb = ps[:, bank * 512:(bank + 1) * 512].bitcast(BF16)
                    for kk in range(4):
                        k = kg * 4 + kk
                        TR(P, ptb[:, kk * 128:(kk + 1) * 128], hbf[:, j, k * 128:(k + 1) * 128], C["ident"], [kh], [kb])
                    TT(P, "dve", hT[:, kg * 4:(kg + 1) * 4, sub * 128:(sub + 1) * 128],
                       ptb[:, 0:512].rearrange("p (a b) -> p a b", a=4),
                       gT[:, kg * 4:(kg + 1) * 4].unsqueeze(2).to_broadcast([128, 4, 128]), ALU.mult,
                       [], [kb, ("hT", sub // 4)])
            for blk in blocks:
                for g0 in range(0, blk["width"], 512):
                    gw = min(512, blk["width"] - g0)
                    s = slot_i % 4
                    slot_i += 1
                    ks = ("wsl", s)
                    wv = wsl[:, s, 0:16 * gw].rearrange("p (k n) -> p k n", k=16)
                    c0w = blk["col0"] + g0
                    DMA(P, "pool", wv, W[:, c0w:c0w + gw].rearrange("(k p) n -> p k n", p=128), [], [ks])
                    if blk["mode"] == "fm":
                        for c0 in range(0, gw, 128):
                            cw = min(128, gw - c0)
                            for half in range(NH):
                                bank = bank_i % 6
                                bank_i += 1
                                kb = ("ps", bank)
                                pb = ps[0:cw, bank * 512:(bank + 1) * 512]
                                for k in range(16):
                                    MM(P, pb, wv[:, k, c0:c0 + cw], hT[:, k, half * 512:(half + 1) * 512],
                                       k == 0, k == 15, [ks, ("hT", half)], [kb])
                                si = stg_i % 4
                                stg_i += 1
                                kst = ("stg", si)
                                dsb = stb[0:cw, si, :] if blk["dt"] == BF16 else stg[0:cw, si, :]
                                if blk.get("silu"):
                                    ACT(P, dsb, pb, AF.Silu, [], [kb, kst])
                                else:
                                    ev_i += 1
                                    CP(P, "act" if ev_i % 2 else "dve", dsb, pb, [], [kb, kst])
                                r0 = blk["off"] + g0 + c0
                                tt0 = tile * TA + half * 512
                                DMA(P, "sp", blk["dst"][r0:r0 + cw, tt0:tt0 + 512], dsb, [kst], [])
                    else:
                        for sub in range(NSUB):
                            bank = bank_i % 6
                            bank_i += 1
                            kb = ("ps", bank)
                            pb = ps[:, bank * 512:bank * 512 + gw]
                            for k in range(16):
                                MM(P, pb, hT[:, k, sub * 128:(sub + 1) * 128], wv[:, k, 0:gw],
                                   k == 0, k == 15, [ks, ("hT", sub // 4)], [kb])
                            si = stg_i % 4
                            stg_i += 1
                            kst = ("stg", si)
                            dsb = stb[:, si, 0:gw] if blk["dt"] == BF16 else stg[:, si, 0:gw]
                            ev_i += 1
                            CP(P, "act" if ev_i % 2 else "dve", dsb, pb, [], [kb, kst])
                            tt0 = tile * TA + sub * 128
                            cc0 = blk["off"] + g0
                            DMA(P, "sp", blk["dst"][tt0:tt0 + 128, cc0:cc0 + gw], dsb, [kst], [])


def phase_CD(P, C, MIXT, x_src, x_dst, g1row, g2T, g3row, Wout, Wg, Wu, Wd, T, do_ffn=True):
    nc = P.nc
    NKH = HID // 128
    with P.phase() as st:
        aT = sb(st, nc, "aT", [128, 16, 512], BF16)
        actT = sb(st, nc, "actT", [128, NKH, 512], BF16)
        xt = sb(st, nc, "xt", [128, 2, D], F32)
        f = sb(st, nc, "f", [128, 4, D], F32)
        gb = sb(st, nc, "gb", [128, D], F32)
        hbf = sb(st, nc, "hbfC", [128, 2, D], BF16)
        wsl = sb(st, nc, "wslC", [128, 4, 8192], BF16)
        sg = sb(st, nc, "sg", [128, 2, 512], F32)
        ss = sb(st, nc, "ss", [128, 4, 8], F32)
        col = sb(st, nc, "colC", [128, 2, 4], F32)
        ps = pst(st, nc, "psC", [128, 4096], F32)
        state = {"slot": 0}

        junk = sb(st, nc, "junkC", [128, 512], BF16)

        def tok_proj(src, nk, Wdram, srckey):
            for cg in range(4):
                for kg0 in range(0, nk, 16):
                    kn = min(16, nk - kg0)
                    s = state["slot"] % 4
                    state["slot"] += 1
                    ks = ("wsl", s)
                    wv = wsl[:, s, 0:kn * 512].rearrange("p (k n) -> p k n", k=kn)
                    DMA(P, "pool", wv,
                        Wdram[kg0 * 128:(kg0 + kn) * 128, cg * 512:(cg + 1) * 512].rearrange("(k p) n -> p k n", p=128),
                        [], [ks])
                    for k in range(kn):
                        kk = kg0 + k
                        for sub in range(4):
                            MM(P, ps[:, sub * 512:(sub + 1) * 512], src[:, kk, sub * 128:(sub + 1) * 128], wv[:, k, :],
                               kk == 0, kk == nk - 1, [ks, srckey], [("ps", sub)])
                for sub in range(4):
                    pb = ps[:, sub * 512:(sub + 1) * 512]
                    ACT(P, junk[:], pb, AF.Square, [], [("ps", sub), ("ss", sub)],
                        accum=ss[:, sub, cg:cg + 1])
                    TT(P, "dve", f[:, sub, cg * 512:(cg + 1) * 512], pb, gb[:, cg * 512:(cg + 1) * 512], ALU.mult,
                       ["gb"], [("ps", sub), ("f", sub)])

        def finish_rstd(sub):
            k = ("ss", sub)
            P.op("dve", lambda e: e.tensor_reduce(out=ss[:, sub, 4:5], in_=ss[:, sub, 0:4], axis=mybir.AxisListType.X,
                                                   op=ALU.add), [k], [k])
            ACT(P, ss[:, sub, 5:6], ss[:, sub, 4:5], AF.Sqrt, [k], [k], bias=EPS, scale=1.0 / D)
            RECIP(P, ss[:, sub, 6:7], ss[:, sub, 5:6], [k], [k])

        for tile in range(T // 512):
            t0 = tile * 512
            DMA(P, "sp", aT[:], MIXT[:, t0:t0 + 512].rearrange("(k p) t -> p k t", p=128), [], ["aT"])
            DMA(P, "sp", gb[:], g1row.partition_broadcast(128), [], ["gb"])
            tok_proj(aT, 16, Wout, "aT")
            for sub in range(4):
                j = sub % 2
                kx, kh, kc = ("xt", j), ("hbf", j), ("col", j)
                tt0 = t0 + sub * 128
                kd = ("xd", tile, sub)
                finish_rstd(sub)
                DMA(P, "sp", xt[:, j, :], x_src[tt0:tt0 + 128, :], [kd], [kx])
                STT(P, xt[:, j, :], f[:, sub, :], ss[:, sub, 6:7], xt[:, j, :], ALU.mult, ALU.add,
                    [("f", sub), ("ss", sub), kx], [kx])
                DMA(P, "sp", x_dst[tt0:tt0 + 128, :], xt[:, j, :], [kx], [kd])
                if not do_ffn:
                    continue
                ACT(P, hbf[:, j, :], xt[:, j, :], AF.Square, [kx], [kh, kc], accum=col[:, j, 0:1])
                rstd_from_ss(P, col[:, j, :], D, kc)
                TS(P, "dve", hbf[:, j, :], xt[:, j, :], col[:, j, 2:3], None, ALU.mult, None, [kx, kc], [kh])
                for kg in range(4):
                    bank = 4 + 2 * (kg % 2)
                    kb = ("ps", bank)
                    ptb = ps[:, bank * 512:(bank + 1) * 512].bitcast(BF16)
                    for kk in range(4):
                        k = kg * 4 + kk
                        TR(P, ptb[:, kk * 128:(kk + 1) * 128], hbf[:, j, k * 128:(k + 1) * 128], C["ident"], [kh], [kb])
                    TT(P, "dve", aT[:, kg * 4:(kg + 1) * 4, sub * 128:(sub + 1) * 128],
                       ptb[:, 0:512].rearrange("p (a b) -> p a b", a=4),
                       g2T[:, kg * 4:(kg + 1) * 4].unsqueeze(2).to_broadcast([128, 4, 128]), ALU.mult,
                       [], [kb, "aT"])
            if not do_ffn:
                continue
            DMA(P, "sp", gb[:], g3row.partition_broadcast(128), [], ["gb"])
            for hg in range(HID // 512):
                sl = []
                for Wm in (Wg, Wu):
                    s = state["slot"] % 4
                    state["slot"] += 1
                    wv = wsl[:, s, 0:8192].rearrange("p (k n) -> p k n", k=16)
                    DMA(P, "pool", wv, Wm[:, hg * 512:(hg + 1) * 512].rearrange("(k p) n -> p k n", p=128), [], [("wsl", s)])
                    sl.append((wv, ("wsl", s)))
                for c in range(4):
                    hc = hg * 4 + c
                    bG = 4 + 2 * (hc % 2)
                    bU = bG + 1
                    for (wv, ks), b in zip(sl, (bG, bU)):
                        for k in range(16):
                            MM(P, ps[:, b * 512:(b + 1) * 512], wv[:, k, c * 128:(c + 1) * 128], aT[:, k, :],
                               k == 0, k == 15, [ks, "aT"], [("ps", b)])
                    ACT(P, sg[:, hc % 2, :], ps[:, bG * 512:(bG + 1) * 512], AF.Silu, [], [("ps", bG), ("sg", hc % 2)])
                    TT(P, "dve", actT[:, hc, :], sg[:, hc % 2, :], ps[:, bU * 512:(bU + 1) * 512], ALU.mult,
                       [("sg", hc % 2)], [("ps", bU), "actT"])
            tok_proj(actT, NKH, Wd, "actT")
            for sub in range(4):
                j = sub % 2
                kx = ("xt", j)
                tt0 = t0 + sub * 128
                kd = ("xd", tile, sub)
                finish_rstd(sub)
                DMA(P, "sp", xt[:, j, :], x_dst[tt0:tt0 + 128, :], [kd], [kx])
                STT(P, xt[:, j, :], f[:, sub, :], ss[:, sub, 6:7], xt[:, j, :], ALU.mult, ALU.add,
                    [("f", sub), ("ss", sub), kx], [kx])
                DMA(P, "sp", x_dst[tt0:tt0 + 128, :], xt[:, j, :], [kx], [kd])


def phase_SWA(P, C, OFM, OTM, MIXT, cosT_d, sinT_d, sink_row, T):
    nc = P.nc
    NB = T // 128
    SC = 128 ** -0.5
    with P.phase() as st:
        cosT = sb(st, nc, "cosT", [128, T], F32)
        sinT = sb(st, nc, "sinT", [128, T], F32)
        raw = sb(st, nc, "raw", [128, 2, 512], BF16)
        t1 = sb(st, nc, "t1", [128, 2, 512], F32)
        t2 = sb(st, nc, "t2", [128, 2, 512], F32)
        qrot = sb(st, nc, "qrot", [128, 4, T], BF16)
        krot = sb(st, nc, "krot", [128, T], BF16)
        V = sb(st, nc, "Vswa", [128, NB, 128], BF16)
        pT = sb(st, nc, "pT", [128, 4, 512], BF16)
        es = sb(st, nc, "es", [128, 8], F32)
        den = sb(st, nc, "den", [128, 2, 512], F32)
        ost = sb(st, nc, "ost", [128, 4, 512], BF16)
        ps = pst(st, nc, "psW", [128, 4096], F32)
        DMA(P, "sp", cosT[:], cosT_d[:, 0:T], [], ["cos"])
        DMA(P, "sp", sinT[:], sinT_d[:, 0:T], [], ["sin"])
        DMA(P, "sp", es[:], sink_row.partition_broadcast(128), [], ["es"])
        ACT(P, es[:], es[:], AF.Exp, ["es"], ["es"])
        ri = 0
        pi = 0

        def rope(src_rows, dst_ap_fn, dkey):
            nonlocal ri
            for tb in range(T // 512):
                j = ri % 2
                ri += 1
                bank = 6 + j
                kb = ("ps", bank)
                pb = ps[:, bank * 512:(bank + 1) * 512]
                tsl = slice(tb * 512, (tb + 1) * 512)
                DMA(P, "sp", raw[:, j, :], OFM[src_rows:src_rows + 128, tsl], [], [("raw", j)])
                MM(P, pb, C["rotm"], raw[:, j, :], True, True, [("raw", j)], [kb])
                TT(P, "dve", t1[:, j, :], raw[:, j, :], cosT[:, tsl], ALU.mult, [("raw", j), "cos"], [("t1", j)])
                TT(P, "dve", t2[:, j, :], pb, sinT[:, tsl], ALU.mult, ["sin"], [kb, ("t2", j)])
                TT(P, "pool", dst_ap_fn(tsl), t1[:, j, :], t2[:, j, :], ALU.add, [("t1", j), ("t2", j)], [dkey])

        for g in range(2):
            rope(3584 + g * 128, lambda tsl: krot[:, tsl], "krot")
            for h in range(4):
                rope(2560 + (g * 4 + h) * 128, (lambda hh: (lambda tsl: qrot[:, hh, tsl]))(h), "qrot")
            DMA(P, "sp", V[:], OTM[:, g * 128:(g + 1) * 128].rearrange("(n p) c -> p n c", p=128), [], ["V"])
            for n in range(NB):
                bO = 2 + (n % 2)
                bD = 4 + (n % 2)
                pO = ps[:, bO * 512:(bO + 1) * 512]
                pD = ps[:, bD * 512:(bD + 1) * 512]
                js = [j for j in (n - 1, n, n + 1) if 0 <= j < NB]
                for idx, jb in enumerate(js):
                    bS = pi % 2
                    pslot = pi % 4
                    pi += 1
                    pS = ps[:, bS * 512:(bS + 1) * 512]
                    kp = ("pT", pslot)
                    MM(P, pS, krot[:, jb * 128:(jb + 1) * 128], qrot[:, :, n * 128:(n + 1) * 128], True, True,
                       ["krot", "qrot"], [("ps", bS)])
                    ACT(P, pT[:, pslot, :], pS, AF.Exp, [], [("ps", bS), kp], scale=SC)
                    if jb != n:
                        mk = C["mprev"] if jb < n else C["mnext"]
                        TT(P, "dve", pT[:, pslot, :].rearrange("p (h q) -> p h q", h=4),
                           pT[:, pslot, :].rearrange("p (h q) -> p h q", h=4),
                           mk.unsqueeze(1).to_broadcast([128, 4, 128]), ALU.mult, [], [kp])
                    MM(P, pO, V[:, jb, :], pT[:, pslot, :], idx == 0, idx == len(js) - 1, ["V", kp], [("ps", bO)])
                    MM(P, pD, C["ones"], pT[:, pslot, :], idx == 0, idx == len(js) - 1, [kp], [("ps", bD)])
                dj = n % 2
                TT(P, "dve", den[:, dj, :].rearrange("p (h q) -> p h q", h=4), pD.rearrange("p (h q) -> p h q", h=4),
                   es[:, g * 4:(g + 1) * 4].unsqueeze(2).to_broadcast([128, 4, 128]), ALU.add,
                   ["es"], [("ps", bD), ("den", dj)])
                RECIP(P, den[:, dj, :], den[:, dj, :], [], [("den", dj)])
                TT(P, "dve", ost[:, :, (n % 4) * 128:(n % 4 + 1) * 128], pO.rearrange("p (h q) -> p h q", h=4),
                   den[:, dj, :].rearrange("p (h q) -> p h q", h=4), ALU.mult, [("den", dj)], [("ps", bO), "ost"])
                if n % 4 == 3:
                    n0 = n - 3
                    r0 = 1024 + g * 512
                    DMA(P, "sp", MIXT[r0:r0 + 512, n0 * 128:n0 * 128 + 512].rearrange("(h d) t -> d h t", d=128),
                        ost[:], ["ost"], [])


def na_pattern(m):
    return 0 if m == 0 else 1 if m == 1 else 3 if m == 30 else 4 if m == 31 else 2


def phase_NA(P, C, EFM, ETM, MIXT, nab, T):
    nc = P.nc
    NB = T // 128
    SC = 128 ** -0.5
    assert T == 4096
    with P.phase() as st:
        qT = sb(st, nc, "naq", [128, 2, T], BF16)
        kT = sb(st, nc, "nak", [128, 2, T], BF16)
        V = sb(st, nc, "nav", [128, 2, NB * 128], BF16)
        bias = sb(st, nc, "nabias", [128, 2, 5 * 640], F32)
        sbf = sb(st, nc, "nasb", [128, 2, 640], F32)
        pT = sb(st, nc, "napT", [128, 2, 640], BF16)
        rden = sb(st, nc, "narden", [128, 2, 128], F32)
        ost = sb(st, nc, "naost", [128, 2, 512], BF16)
        ps = pst(st, nc, "psN", [128, 4096], F32)
        it = 0
        for h in range(8):
            hj = h % 2
            kq, kk, kv, kbz = ("q", hj), ("k", hj), ("v", hj), ("b", hj)
            DMA(P, "sp", qT[:, hj, :], EFM[2048 + h * 128:2048 + (h + 1) * 128, :], [], [kq])
            DMA(P, "sp", kT[:, hj, :], EFM[3072 + h * 128:3072 + (h + 1) * 128, :], [], [kk])
            DMA(P, "sp", V[:, hj, :].rearrange("p (n c) -> p n c", c=128),
                ETM[:, 1024 + h * 128:1024 + (h + 1) * 128].rearrange("(n p) c -> p n c", p=128), [], [kv])
            DMA(P, "sp", bias[:, hj, :], nab[h], [], [kbz])
            for m in range(NB):
                u0 = min(max(m - 2, 0), NB - 5)
                pat = na_pattern(m)
                sj = it % 2
                it += 1
                b0 = 2 * sj
                kS = ("ps", b0)
                pS = ps[:, b0 * 512:b0 * 512 + 640]
                for j in range(5):
                    MM(P, pS[:, j * 128:(j + 1) * 128], kT[:, hj, (u0 + j) * 128:(u0 + j + 1) * 128],
                       qT[:, hj, m * 128:(m + 1) * 128], True, True, [kq, kk], [kS if j < 4 else ("ps", b0 + 1)])
                STT(P, sbf[:, sj, 0:512], pS[:, 0:512], SC, bias[:, hj, pat * 640:pat * 640 + 512], ALU.mult, ALU.add,
                    [kbz], [kS, ("sbf", sj)])
                STT(P, sbf[:, sj, 512:640], pS[:, 512:640], SC, bias[:, hj, pat * 640 + 512:(pat + 1) * 640], ALU.mult, ALU.add,
                    [kbz], [("ps", b0 + 1), ("sbf", sj)])
                ACT(P, pT[:, sj, :], sbf[:, sj, :], AF.Exp, [("sbf", sj)], [("pT", sj)])
                bO = 4 + sj
                pO = ps[:, bO * 512:bO * 512 + 128]
                pD = ps[:, bO * 512 + 128:bO * 512 + 256]
                for j in range(5):
                    MM(P, pO, V[:, hj, (u0 + j) * 128:(u0 + j + 1) * 128], pT[:, sj, j * 128:(j + 1) * 128],
                       j == 0, j == 4, [kv, ("pT", sj)], [("ps", bO)])
                for j in range(5):
                    MM(P, pD, C["ones"], pT[:, sj, j * 128:(j + 1) * 128], j == 0, j == 4, [("pT", sj)], [("ps", bO)])
                RECIP(P, rden[:, sj, :], pD, [], [("ps", bO), ("rden", sj)])
                oj = (m // 4) % 2
                TT(P, "dve", ost[:, oj, (m % 4) * 128:(m % 4 + 1) * 128], pO, rden[:, sj, :], ALU.mult,
                   [("rden", sj)], [("ps", bO), ("ost", oj)])
                if m % 4 == 3:
                    n0 = m - 3
                    DMA(P, "sp", MIXT[1024 + h * 128:1024 + (h + 1) * 128, n0 * 128:n0 * 128 + 512], ost[:, oj, :],
                        [("ost", oj)], [])


def make_consts(T=T_FULL):
    c = {}
    c["ident"] = np.eye(128, dtype=np.float32)
    c["ones"] = np.ones((128, 128), np.float32)
    rot = np.zeros((128, 128), np.float32)
    for m in range(64):
        rot[m + 64, m] = -1.0
        rot[m, m + 64] = 1.0
    c["rotm"] = rot
    jj = np.arange(128)[:, None]
    ii = np.arange(128)[None, :]
    c["mprev"] = (ii <= jj).astype(np.float32)
    c["mnext"] = (jj <= ii).astype(np.float32)
    half = 64
    inv = (10000.0 ** (-np.arange(half, dtype=np.float32) / half)).astype(np.float32)
    ang = np.arange(T, dtype=np.float32)[None, :] * np.concatenate([inv, inv])[:, None]
    c["cosT"] = np.cos(ang).astype(np.float32)
    c["sinT"] = np.sin(ang).astype(np.float32)
    return c


def na_bias_tables(rpb):
    H = rpb.shape[0]
    out = np.full((H, 128, 5, 5, 128), -30000.0, np.float32)
    reps = [0, 1, 5, 30, 31]
    for p, m in enumerate(reps):
        u0 = min(max(m - 2, 0), 27)
        ktok = (u0 * 128 + np.arange(640)).reshape(5, 128)
        kr, kc = ktok // 64, ktok % 64
        qtok = m * 128 + np.arange(128)
        r, cq = qtok // 64, qtok % 64
        rs = np.clip(r - 4, 0, 56)
        cs = np.clip(cq - 8, 0, 48)
        KR = kr[:, :, None]
        KC = kc[:, :, None]
        valid = (KR >= rs[None, None, :]) & (KR < rs[None, None, :] + 8) & (KC >= cs[None, None, :]) & (KC < cs[None, None, :] + 16)
        dr = np.clip(KR - r[None, None, :] + 7, 0, 14)
        dc = np.clip(KC - cq[None, None, :] + 15, 0, 30)
        for h in range(H):
            vals = rpb[h][dr, dc]
            tab = np.where(valid, vals, np.float32(-30000.0)).astype(np.float32)
            out[h, :, p, :, :] = tab.transpose(1, 0, 2)
    return out.reshape(H, 128, 5 * 640)


def phase_GLA(P, C, EFM, ETM, LRT, MIXT, wdec_d, bdecT_d, gnT_d, T):
    nc = P.nc
    NB = T // 128
    NT5 = T // 512
    with P.phase() as st:
        lrT = sb(st, nc, "lrT", [32, T], F32)
        wdec = sb(st, nc, "wdec", [32, 2 * 512], F32)
        negb = sb(st, nc, "negb", [128, 8], F32)
        gnT = sb(st, nc, "gnT", [128, 8], F32)
        rmask = sb(st, nc, "rmask", [128, 512], F32)
        mk2 = sb(st, nc, "mk2", [128, 2, 128], BF16)
        cs = sb(st, nc, "cs", [128, 2, T], F32)
        tmpe = sb(st, nc, "tmpe", [128, 2, 512], F32)
        tmps = sb(st, nc, "tmps", [128, 2, 512], F32)
        tmpp = sb(st, nc, "tmpp", [128, 2, 512], F32)
        qk = sb(st, nc, "qkraw", [128, 2, T], BF16)
        qkt = sb(st, nc, "qkt", [128, 4, T], BF16)
        ktok = sb(st, nc, "ktok", [128, 2, NB * 128], BF16)
        Vp = sb(st, nc, "Vp", [128, NB, 256], BF16)
        Sprev = sb(st, nc, "Sprev", [128, 2, NB * 128], BF16)
        S = sb(st, nc, "Sst", [128, 2, 128], F32)
        tS = sb(st, nc, "tS", [128, 2, 128], F32)
        dl = sb(st, nc, "dl", [128, 2, NB], F32)
        ATm = sb(st, nc, "ATm", [128, 2, 512], BF16)
        sq = sb(st, nc, "sq", [128, 2, 512], BF16)
        rt = sb(st, nc, "rt", [128, 2, 512], F32)
        o1 = sb(st, nc, "o1", [128, 2, 512], F32)
        sgt = sb(st, nc, "sgt", [128, 2, 512], BF16)
        ostg = sb(st, nc, "ostg", [128, 2, 512], BF16)
        ps = pst(st, nc, "psG", [128, 4096], F32)

        def bank(i, w=512):
            return ps[:, i * 512:i * 512 + w]

        DMA(P, "sp", lrT[:], LRT, [], ["lrT"])
        DMA(P, "sp", wdec[:], wdec_d.rearrange("r d c -> r (d c)"), [], ["wdec"])
        DMA(P, "sp", negb[:], bdecT_d, [], ["negb"])
        TS(P, "dve", negb[:], negb[:], -1.0, None, ALU.mult, None, ["negb"], ["negb"])
        DMA(P, "sp", gnT[:], gnT_d, [], ["gnT"])
        MEMSET(P, "dve", rmask[:], 1.0, [], ["rmask"])
        for c in range(4):
            MEMSET(P, "dve", rmask[:, c * 128:c * 128 + 1], 0.0, [], ["rmask"])
        CP(P, "dve", mk2[:, 0, :], C["mnext"], [], ["mk2"])
        CP(P, "dve", mk2[:, 1, :], C["mprev"], [], ["mk2"])
        ti = 0
        ei = 0
        for pr in range(4):
            DMA(P, "sp", qk[:, 0, :], EFM[pr * 128:(pr + 1) * 128, :], [], ["qraw"])
            DMA(P, "sp", qk[:, 1, :], EFM[512 + pr * 128:512 + (pr + 1) * 128, :], [], ["kraw"])
            DMA(P, "sp", Vp[:], ETM[:, pr * 256:(pr + 1) * 256].rearrange("(n p) c -> p n c", p=128), [], ["Vp"])
            for dr in range(2):
                for tb in range(NT5):
                    j = ti % 2
                    ti += 1
                    tsl = slice(tb * 512, (tb + 1) * 512)
                    MM(P, bank(0), wdec[0:32, dr * 512 + pr * 128:dr * 512 + (pr + 1) * 128], lrT[0:32, tsl], True, True,
                       ["wdec", "lrT"], [("ps", 0)])
                    ACT(P, tmpe[:, j, :], bank(0), AF.Exp, ["negb"], [("ps", 0), ("tmpe", j)],
                        bias=negb[:, dr * 4 + pr:dr * 4 + pr + 1], scale=-1.0)
                    ACT(P, tmps[:, j, :], tmpe[:, j, :], AF.Ln, [("tmpe", j)], [("tmps", j)], bias=1.0)
                    if dr == 0:
                        P.op("dve", (lambda o, d1: (lambda e: e.tensor_tensor_scan(out=o, data0=rmask[:], data1=d1, initial=0.0,
                                                                                    op0=ALU.mult, op1=ALU.add)))(cs[:, 0, tsl], tmps[:, j, :]),
                             [("tmps", j), "rmask"], [("cs", 0)])
                    else:
                        P.op("dve", (lambda o, d1: (lambda e: e.tensor_tensor_scan(out=o, data0=rmask[:], data1=d1, initial=0.0,
                                                                                    op0=ALU.mult, op1=ALU.add)))(tmpp[:, j, :], tmps[:, j, :]),
                             [("tmps", j), "rmask"], [("tmpp", j)])
                        TT(P, "dve", tmps[:, j, :], tmps[:, j, :], tmpp[:, j, :], ALU.subtract, [("tmpp", j)], [("tmps", j)])
                        TT(P, "dve", cs[:, 1, tsl].rearrange("p (c t) -> p c t", c=4),
                           tmps[:, j, :].rearrange("p (c t) -> p c t", c=4),
                           tmpp[:, j, :].rearrange("p (c t) -> p c t", c=4)[:, :, 127:128].to_broadcast([128, 4, 128]),
                           ALU.add, [("tmps", j), ("tmpp", j)], [("cs", 1)])
            for dr in range(2):
                for tb in range(NT5):
                    tsl = slice(tb * 512, (tb + 1) * 512)
                    j = ei % 2
                    ei += 1
                    ACT(P, tmpe[:, j, :], cs[:, dr, tsl], AF.Exp, [("cs", dr)], [("tmpe", j)], scale=-1.0 / 16)
                    STT(P, qkt[:, 2 * dr, tsl], qk[:, 0, tsl], 0.125, tmpe[:, j, :], ALU.mult, ALU.mult,
                        ["qraw", ("tmpe", j)], [("qkt", 2 * dr)])
                    ACT(P, tmpp[:, j, :], cs[:, dr, tsl], AF.Exp, [("cs", dr)], [("tmpp", j)], scale=1.0 / 16)
                    TT(P, "dve", qkt[:, 2 * dr + 1, tsl], qk[:, 1, tsl], tmpp[:, j, :], ALU.mult,
                       ["kraw", ("tmpp", j)], [("qkt", 2 * dr + 1)])
            cs3f = cs[:, 0, :].rearrange("p (n t) -> p n t", t=128)
            cs3b = cs[:, 1, :].rearrange("p (n t) -> p n t", t=128)
            ACT(P, dl[:, 0, :], cs3f[:, :, 127], AF.Exp, [("cs", 0)], ["dl"], scale=-1.0 / 16)
            ACT(P, dl[:, 1, :], cs3b[:, :, 0], AF.Exp, [("cs", 1)], ["dl"], scale=-1.0 / 16)
            for dr in range(2):
                for n0 in range(0, NB, 4):
                    ptb = bank(1).bitcast(BF16)
                    for c in range(4):
                        n = n0 + c
                        TR(P, ptb[:, c * 128:(c + 1) * 128], qkt[:, 2 * dr + 1, n * 128:(n + 1) * 128], C["ident"],
                           [("qkt", 2 * dr + 1)], [("ps", 1)])
                    CP(P, "act" if (n0 // 4) % 2 else "dve", ktok[:, dr, n0 * 128:(n0 + 4) * 128], ptb[:, 0:512], [],
                       [("ps", 1), ("ktok", dr)])
            for dr in range(2):
                kS = ("S", dr)
                MEMSET(P, "dve", S[:, dr, :], 0.0, [], [kS])
                order = range(NB) if dr == 0 else range(NB - 1, -1, -1)
                for idx, n in enumerate(order):
                    CP(P, "act", Sprev[:, dr, n * 128:(n + 1) * 128], S[:, dr, :], [kS], [("Sprev", dr)])
                    if idx == NB - 1:
                        break
                    pU = bank(2, 128)
                    for hh in range(2):
                        MM(P, pU[hh * 64:(hh + 1) * 64, :], ktok[:, dr, n * 128 + hh * 64:n * 128 + (hh + 1) * 64],
                           Vp[:, n, hh * 128:(hh + 1) * 128], True, True, [("ktok", dr), "Vp"], [("ps", 2)])
                    TT(P, "dve", tS[:, dr, :], pU, S[:, dr, :], ALU.add, [kS], [("ps", 2), ("tS", dr)])
                    ACT(P, S[:, dr, :], tS[:, dr, :], AF.Copy, [("tS", dr), "dl"], [kS], scale=dl[:, dr, n:n + 1])
            for n in range(NB):
                bAs = (3, 4) if n % 2 == 0 else (0, 1)
                aj = n % 2
                for dr in range(2):
                    for hh in range(2):
                        pA = bank(bAs[hh])
                        MM(P, pA[:, dr * 128:(dr + 1) * 128], qkt[hh * 64:(hh + 1) * 64, 2 * dr + 1, n * 128:(n + 1) * 128],
                           qkt[hh * 64:(hh + 1) * 64, 2 * dr, n * 128:(n + 1) * 128], True, True,
                           [("qkt", 2 * dr + 1), ("qkt", 2 * dr)], [("ps", bAs[hh])])
                for hh in range(2):
                    TT(P, "dve", ATm[:, aj, hh * 256:(hh + 1) * 256].rearrange("p (d t) -> p d t", d=2),
                       bank(bAs[hh])[:, 0:256].rearrange("p (d t) -> p d t", d=2),
                       mk2[:], ALU.mult, ["mk2"], [("ps", bAs[hh]), ("ATm", aj)])
                for hh in range(2):
                    pO = bank(5 + hh)[:, (n % 4) * 128:(n % 4 + 1) * 128]
                    kO = ("ps", 5 + hh)
                    for dr in range(2):
                        MM(P, pO, Vp[:, n, hh * 128:(hh + 1) * 128], ATm[:, aj, (hh * 2 + dr) * 128:(hh * 2 + dr + 1) * 128],
                           dr == 0, False, ["Vp", ("ATm", aj)], [kO])
                        MM(P, pO, Sprev[hh * 64:(hh + 1) * 64, dr, n * 128:(n + 1) * 128],
                           qkt[hh * 64:(hh + 1) * 64, 2 * dr, n * 128:(n + 1) * 128], False, dr == 1,
                           [("Sprev", dr), ("qkt", 2 * dr)], [kO])
                if n % 4 == 3:
                    tsl = slice((n - 3) * 128, (n + 1) * 128)
                    for hh in range(2):
                        h = pr * 2 + hh
                        kO = ("ps", 5 + hh)
                        pO = bank(5 + hh)
                        DMA(P, "sp", sgt[:, hh, :], EFM[1024 + h * 128:1024 + (h + 1) * 128, tsl], [], [("sgt", hh)])
                        ACT(P, sq[:, hh, :], pO, AF.Square, [], [kO, ("sq", hh)])
                        MM(P, bank(7), C["ones"], sq[:, hh, :], True, True, [("sq", hh)], [("ps", 7)])
                        ACT(P, rt[:, hh, :], bank(7), AF.Sqrt, [], [("ps", 7), ("rt", hh)], bias=EPS, scale=1.0 / 128)
                        RECIP(P, rt[:, hh, :], rt[:, hh, :], [], [("rt", hh)])
                        TT(P, "dve", o1[:, hh, :], pO, rt[:, hh, :], ALU.mult, [("rt", hh)], [kO, ("o1", hh)])
                        STT(P, ostg[:, hh, :], o1[:, hh, :], gnT[:, h:h + 1], sgt[:, hh, :], ALU.mult, ALU.mult,
                            [("o1", hh), "gnT", ("sgt", hh)], [("ostg", hh)])
                        DMA(P, "sp", MIXT[h * 128:(h + 1) * 128, tsl], ostg[:, hh, :], [("ostg", hh)], [])


def phase_SSD(P, C, OFM, DTR, MIXT, XBCS, STATES, PREV, prm, T):
    nc = P.nc
    NB = T // 128
    NT5 = T // 512
    with contextlib.ExitStack() as sst:
        _phase_SSD(P, C, OFM, DTR, MIXT, XBCS, STATES, PREV, prm, T, sst)


def _phase_SSD(P, C, OFM, DTR, MIXT, XBCS, STATES, PREV, prm, T, sst):
    nc = P.nc
    NB = T // 128
    NT5 = T // 512
    dt = sb(sst, nc, "ssd_dt", [128, NB, 32], F32)
    acum = sb(sst, nc, "ssd_acum", [128, NB, 32], F32)
    nacum = sb(sst, nc, "ssd_nacum", [128, NB, 32], F32)
    EA = sb(sst, nc, "ssd_EA", [128, NB, 32], F32)
    acT = sb(sst, nc, "ssd_acT", [16, 2, T], F32)
    with P.phase() as st:
        cw = sb(st, nc, "convw", [128, 60], F32)
        cb = sb(st, nc, "convb", [128, 12], F32)
        diagw = sb(st, nc, "diagw", [128, 60, 128], BF16)
        dtb = sb(st, nc, "dtb", [128, 32], F32)
        Aneg = sb(st, nc, "Aneg", [128, 32], F32)
        da = sb(st, nc, "ssd_da", [128, NB, 32], F32)
        dtdec = sb(st, nc, "ssd_dtdec", [128, NB, 32], F32)
        rawx = sb(st, nc, "rawx", [128, 2, 12 * 516], BF16)
        xbc = sb(st, nc, "xbc", [128, 2, 12 * 512], BF16)
        xdd = sb(st, nc, "xdd", [128, 2, 2 * 1024], BF16)
        btok = sb(st, nc, "btok", [128, 2, 256], BF16)
        stsb = sb(st, nc, "stsb", [128, 2, 2 * 1024], F32)
        ps = pst(st, nc, "psS1", [128, 4096], F32)

        def bank(i, w=512):
            return ps[:, i * 512:i * 512 + w]

        DMA(P, "sp", cw[:], prm["convwT"], [], ["cw"])
        DMA(P, "sp", cb[:], prm["convbT"], [], ["cb"])
        DMA(P, "sp", dtb[:], prm["dtb"].partition_broadcast(128), [], ["dtb"])
        DMA(P, "sp", Aneg[:], prm["alog"].partition_broadcast(128), [], ["Aneg"])
        ACT(P, Aneg[:], Aneg[:], AF.Exp, ["Aneg"], ["Aneg"])
        TS(P, "dve", Aneg[:], Aneg[:], -1.0, None, ALU.mult, None, ["Aneg"], ["Aneg"])
        for i in range(60):
            TS(P, "dve" if i % 2 else "pool", diagw[:, i, :], C["ident"], cw[:, i:i + 1], None, ALU.mult, None, ["cw"], ["diagw"])
        DMA(P, "sp", dt[:], DTR.rearrange("(n p) c -> p n c", p=128), [], ["dt"])
        TT(P, "dve", dt[:], dt[:], dtb[:].unsqueeze(1).to_broadcast([128, NB, 32]), ALU.add, ["dtb"], ["dt"])
        ACT(P, dt[:], dt[:], AF.Exp, ["dt"], ["dt"])
        ACT(P, dt[:], dt[:], AF.Ln, ["dt"], ["dt"], bias=1.0)
        TT(P, "dve", da[:], dt[:], Aneg[:].unsqueeze(1).to_broadcast([128, NB, 32]), ALU.mult, ["dt", "Aneg"], ["da"])
        for dr in range(2):
            tri = C["mnext32"] if dr == 0 else C["mprev32"]
            MM(P, bank(0).rearrange("p (n h) -> p n h", h=16), tri, da[:, :, dr * 16:(dr + 1) * 16], True, True,
               ["da"], [("ps", 0)])
            CP(P, "dve", acum[:, :, dr * 16:(dr + 1) * 16], bank(0).rearrange("p (n h) -> p n h", h=16), [],
               [("ps", 0), "acum"])
        TS(P, "dve", nacum[:], acum[:], -1.0, None, ALU.mult, None, ["acum"], ["nacum"])
        for hf in range(2):
            nsl = slice(hf * (NB // 2), (hf + 1) * (NB // 2))
            MM(P, bank(1).rearrange("p (n h) -> p n h", h=32), C["ones32"], da[:, nsl, :], True, True, ["da"], [("ps", 1)])
            TT(P, "dve", dtdec[:, nsl, :], bank(1).rearrange("p (n h) -> p n h", h=32), acum[:, nsl, :], ALU.subtract,
               ["acum"], [("ps", 1), "dtdec"])
            ACT(P, EA[:, nsl, :], bank(1).rearrange("p (n h) -> p n h", h=32), AF.Exp, [], [("ps", 1), "EA"])
        ACT(P, dtdec[:], dtdec[:], AF.Exp, ["dtdec"], ["dtdec"])
        TT(P, "dve", dtdec[:], dtdec[:], dt[:], ALU.mult, ["dt"], ["dtdec"])
        for dr in range(2):
            tri = C["mnext32"] if dr == 0 else C["mprev32"]
            for n0 in range(0, NB, 4):
                for c in range(4):
                    n = n0 + c
                    MM(P, ps[0:16, 1024 + c * 128:1024 + (c + 1) * 128], da[:, n, dr * 16:(dr + 1) * 16], tri, True, True,
                       ["da"], [("ps", 2)])
                CP(P, "act", acT[:, dr, n0 * 128:(n0 + 4) * 128], ps[0:16, 1024:1536], [], [("ps", 2), ("acT", dr)])
        for tb in range(NT5):
            j = tb % 2
            t0 = tb * 512
            kr, kx = ("rawx", j), ("xbc", j)
            rv = rawx[:, j, :].rearrange("p (c t) -> p c t", c=12)
            xv = xbc[:, j, :].rearrange("p (c t) -> p c t", c=12)
            lo = max(t0 - 2, 0)
            hi = min(t0 + 514, T)
            if tb == 0:
                MEMSET(P, "dve", rv[:, :, 0:2], 0.0, [], [kr])
            if tb == NT5 - 1:
                MEMSET(P, "dve", rv[:, :, 514:516], 0.0, [], [kr])
            DMA(P, "sp", rv[:, :, lo - (t0 - 2):hi - (t0 - 2)], OFM[1024:2560, lo:hi].rearrange("(c p) t -> p c t", p=128), [], [kr])
            for ch in range(12):
                b = 4 + (ch % 2)
                for jj in range(5):
                    MM(P, bank(b), diagw[:, ch * 5 + jj, :], rv[:, ch, jj:jj + 512], jj == 0, jj == 4, [kr, "diagw"], [("ps", b)])
                ACT(P, xv[:, ch, :], bank(b), AF.Silu, ["cb"], [("ps", b), kx], bias=cb[:, ch:ch + 1])
            DMA(P, "sp", XBCS[:, t0:t0 + 512].rearrange("(c p) t -> p c t", p=128), xv, [kx], [])
            for c in range(4):
                n = tb * 4 + c
                cj = n % 2
                tsl = slice(c * 128, (c + 1) * 128)
                pX = bank(6).bitcast(BF16)
                pB = bank(7).bitcast(BF16)
                for k in range(8):
                    TR(P, pX[:, k * 128:(k + 1) * 128], xv[:, k, tsl], C["ident"], [kx], [("ps", 6)])
                for g in range(2):
                    TR(P, pB[:, g * 128:(g + 1) * 128], xv[:, 8 + g, tsl], C["ident"], [kx], [("ps", 7)])
                for dr in range(2):
                    TT(P, "dve", xdd[:, cj, dr * 1024:(dr + 1) * 1024].rearrange("p (h q) -> p h q", h=16),
                       pX[:, 0:1024].rearrange("p (h q) -> p h q", h=16),
                       dtdec[:, n, dr * 16:(dr + 1) * 16].unsqueeze(2).to_broadcast([128, 16, 64]), ALU.mult,
                       ["dtdec"], [("ps", 6), ("xdd", cj)])
                CP(P, "act", btok[:, cj, :], pB[:, 0:256], [], [("ps", 7), ("btok", cj)])
                for dr in range(2):
                    for g in range(2):
                        b = dr * 2 + g
                        MM(P, bank(b), btok[:, cj, g * 128:(g + 1) * 128],
                           xdd[:, cj, dr * 1024 + g * 512:dr * 1024 + (g + 1) * 512], True, True,
                           [("btok", cj), ("xdd", cj)], [("ps", b)])
                        CP(P, "act" if g else "dve", stsb[:, cj, dr * 1024 + g * 512:dr * 1024 + (g + 1) * 512], bank(b), [],
                           [("ps", b), ("stsb", cj)])
                for dr in range(2):
                    DMA(P, "sp", STATES[dr, n], stsb[:, cj, dr * 1024:(dr + 1) * 1024], [("stsb", cj)], [])
    with P.phase() as st:
        prev = sb(st, nc, "prevS", [128, 2, 1024], F32)
        stl = sb(st, nc, "stl", [128, 4, 1024], F32)
        pvb = sb(st, nc, "pvb", [128, 4, 1024], BF16)
        i = 0
        for dr in range(2):
            kp = ("prev", dr)
            MEMSET(P, "dve" if dr == 0 else "pool", prev[:, dr, :], 0.0, [], [kp])
            order = range(NB) if dr == 0 else range(NB - 1, -1, -1)
            eng = "dve" if dr == 0 else "pool"
            for idx, n in enumerate(order):
                j = (i % 2) + 2 * dr
                i += 1
                CP(P, "act", pvb[:, j, :], prev[:, dr, :], [kp], [("pvb", j)])
                DMA(P, "sp", PREV[dr, n], pvb[:, j, :], [("pvb", j)], [])
                if idx == NB - 1:
                    break
                DMA(P, "sp", stl[:, j, :], STATES[dr, n], [], [("stl", j)])
                TT(P, eng, prev[:, dr, :].rearrange("p (h q) -> p h q", h=16), prev[:, dr, :].rearrange("p (h q) -> p h q", h=16),
                   EA[:, n, dr * 16:(dr + 1) * 16].unsqueeze(2).to_broadcast([128, 16, 64]), ALU.mult, [], [kp])
                TT(P, eng, prev[:, dr, :], prev[:, dr, :], stl[:, j, :], ALU.add, [("stl", j)], [kp])
    with P.phase() as st:
        dsk = sb(st, nc, "dsk", [128, 8], F32)
        snT = sb(st, nc, "snT", [128, 8], F32)
        rsel = sb(st, nc, "rsel", [16, 16 * 128], F32)
        negm = sb(st, nc, "negm", [128, 2, 128], F32)
        xbc = sb(st, nc, "xbc2", [128, 2, 12 * 512], BF16)
        zs = sb(st, nc, "zs", [128, 2, 8 * 512], BF16)
        pv = sb(st, nc, "pv", [128, 2, 2 * 1024], BF16)
        xdt = sb(st, nc, "xdt", [128, 2, 2 * 1024], BF16)
        cbt = sb(st, nc, "cbt", [128, 2, 256], F32)
        tmp = sb(st, nc, "tmpL", [128, 2, 128], F32)
        LT = sb(st, nc, "LT", [128, 2, 128], F32)
        MT = sb(st, nc, "MT", [128, 2, 128], BF16)
        Eh = sb(st, nc, "Eh", [128, 2, 128], F32)
        Cp = sb(st, nc, "Cp", [128, 2, 128], BF16)
        ysb = sb(st, nc, "ysb", [128, 2, 1024], F32)
        sq = sb(st, nc, "sq2", [128, 2, 1024], BF16)
        rt = sb(st, nc, "rt2", [128, 2, 128], F32)
        ost = sb(st, nc, "ost2", [128, 2, 8 * 512], BF16)
        ps = pst(st, nc, "psS2", [128, 4096], F32)

        def bank(i, w=512):
            return ps[:, i * 512:i * 512 + w]

        DMA(P, "sp", dsk[:], prm["dskT"], [], ["dsk"])
        DMA(P, "sp", snT[:], prm["snT"], [], ["snT"])
        DMA(P, "sp", rsel[:], prm["rsel"], [], ["rsel"])
        TS(P, "dve", negm[:, 0, :], C["mnext32"], 30000.0, -30000.0, ALU.mult, ALU.add, [], ["negm"])
        TS(P, "dve", negm[:, 1, :], C["mprev32"], 30000.0, -30000.0, ALU.mult, ALU.add, [], ["negm"])
        li = 0
        for tb in range(NT5):
            j = tb % 2
            t0 = tb * 512
            kx, kz, ko = ("xbc", j), ("zs", j), ("ost", j)
            xv = xbc[:, j, :].rearrange("p (c t) -> p c t", c=12)
            zv = zs[:, j, :].rearrange("p (c t) -> p c t", c=8)
            ov = ost[:, j, :].rearrange("p (c t) -> p c t", c=8)
            DMA(P, "sp", xv, XBCS[:, t0:t0 + 512].rearrange("(c p) t -> p c t", p=128), [], [kx])
            DMA(P, "sp", zv, OFM[0:1024, t0:t0 + 512].rearrange("(c p) t -> p c t", p=128), [], [kz])
            for c in range(4):
                n = tb * 4 + c
                cj = n % 2
                tsl = slice(c * 128, (c + 1) * 128)
                kpv, kxd, kcb = ("pv", cj), ("xdt", cj), ("cbt", cj)
                for dr in range(2):
                    DMA(P, "sp", pv[:, cj, dr * 1024:(dr + 1) * 1024], PREV[dr, n], [], [kpv])
                for g in range(2):
                    MM(P, bank(0)[:, g * 128:(g + 1) * 128], xv[:, 8 + g, tsl], xv[:, 10 + g, tsl], True, True, [kx], [("ps", 0)])
                CP(P, "act", cbt[:, cj, :], bank(0)[:, 0:256], [], [("ps", 0), kcb])
                pX = bank(1).bitcast(BF16)
                for k in range(8):
                    TR(P, pX[:, k * 128:(k + 1) * 128], xv[:, k, tsl], C["ident"], [kx], [("ps", 1)])
                for dr in range(2):
                    TT(P, "dve", xdt[:, cj, dr * 1024:(dr + 1) * 1024].rearrange("p (h q) -> p h q", h=16),
                       pX[:, 0:1024].rearrange("p (h q) -> p h q", h=16),
                       dt[:, n, dr * 16:(dr + 1) * 16].unsqueeze(2).to_broadcast([128, 16, 64]), ALU.mult,
                       [], [("ps", 1), kxd])
                for h in range(16):
                    g = h // 8
                    bY = 4 + (h // 8)
                    pY = bank(bY)[(h % 2) * 64:(h % 2 + 1) * 64, ((h // 2) % 4) * 128:((h // 2) % 4 + 1) * 128]
                    for dr in range(2):
                        lj = li % 2
                        li += 1
                        bA = 2 + lj
                        kt, kl, km, ke, kc = ("tmp", lj), ("LT", lj), ("MT", lj), ("Eh", lj), ("Cp", lj)
                        MM(P, bank(bA, 128), rsel[0:16, h * 128:(h + 1) * 128], acT[0:16, dr, n * 128:(n + 1) * 128], True, True,
                           ["rsel"], [("ps", bA)])
                        TT(P, "dve", tmp[:, lj, :], bank(bA, 128), negm[:, dr, :], ALU.add, ["negm"], [("ps", bA), kt])
                        ACT(P, Eh[:, lj, :], bank(bA, 128), AF.Exp, [], [("ps", bA), ke])
                        ACT(P, LT[:, lj, :], tmp[:, lj, :], AF.Exp, [kt], [kl], bias=nacum[:, n, dr * 16 + h:dr * 16 + h + 1])
                        TT(P, "dve", MT[:, lj, :], LT[:, lj, :], cbt[:, cj, g * 128:(g + 1) * 128], ALU.mult, [kl, kcb], [km])
                        TT(P, "pool", Cp[:, lj, :], xv[:, 10 + g, tsl], Eh[:, lj, :], ALU.mult, [kx, ke], [kc])
                        MM(P, pY, xdt[:, cj, dr * 1024 + h * 64:dr * 1024 + (h + 1) * 64], MT[:, lj, :], dr == 0, False,
                           [kxd, km], [("ps", bY)])
                        MM(P, pY, pv[:, cj, dr * 1024 + h * 64:dr * 1024 + (h + 1) * 64], Cp[:, lj, :], False, dr == 1,
                           [kpv, kc], [("ps", bY)])
                yv = ysb[:, cj, :].rearrange("p (k t) -> p k t", k=8)
                ky = ("ysb", cj)
                for k in range(8):
                    pYk = bank(4 + k // 4)[:, (k % 4) * 128:(k % 4 + 1) * 128]
                    STT(P, yv[:, k, :], xv[:, k, tsl], dsk[:, k:k + 1], pYk, ALU.mult, ALU.add, [kx, "dsk"], [("ps", 4 + k // 4), ky])
                TT(P, "dve", yv, yv, zv[:, :, tsl], ALU.mult, [kz], [ky])
                ACT(P, sq[:, cj, :], ysb[:, cj, :], AF.Square, [ky], [("sq", cj)])
                for k in range(8):
                    MM(P, bank(6, 128), C["ones"], sq[:, cj, k * 128:(k + 1) * 128], k == 0, k == 7, [("sq", cj)], [("ps", 6)])
                ACT(P, rt[:, cj, :], bank(6, 128), AF.Sqrt, [], [("ps", 6), ("rt", cj)], bias=EPS, scale=1.0 / 1024)
                RECIP(P, rt[:, cj, :], rt[:, cj, :], [], [("rt", cj)])
                TT(P, "dve", yv, yv, rt[:, cj, :].unsqueeze(1).to_broadcast([128, 8, 128]), ALU.mult, [("rt", cj)], [ky])
                TT(P, "dve", ov[:, :, tsl], yv, snT[:].unsqueeze(2).to_broadcast([128, 8, 128]), ALU.mult, ["snT"], [ky, ko])
            DMA(P, "sp", MIXT[0:1024, t0:t0 + 512].rearrange("(c p) t -> p c t", p=128), ov, [ko], [])


def ssd_host_params(conv_w, conv_b, dt_bias, a_log, d_skip, ssd_norm):
    p = {}
    p["convwT"] = np.ascontiguousarray(conv_w.reshape(5, 12, 128).transpose(2, 1, 0).reshape(128, 60))
    p["convbT"] = np.ascontiguousarray(conv_b.reshape(12, 128).T)
    p["dtb"] = np.ascontiguousarray(dt_bias.reshape(1, 32))
    p["alog"] = np.ascontiguousarray(a_log.reshape(1, 32))
    p["dskT"] = np.ascontiguousarray(np.repeat(d_skip, 64).reshape(8, 128).T)
    p["snT"] = np.ascontiguousarray(ssd_norm.reshape(8, 128).T)
    return p


def make_rsel():
    r = np.zeros((16, 16, 128), np.float32)
    for i in range(16):
        r[i, i, :] = 1.0
    return r.reshape(16, 16 * 128)


NCORES = 4
EVEN_BLOCKS = [
    dict(col0=0, width=512, mode="fm", dst="EFM", off=0, dt=BF16),
    dict(col0=512, width=512, mode="fm", dst="EFM", off=512, dt=BF16),
    dict(col0=1024, width=1024, mode="tm", dst="ETM", off=0, dt=BF16),
    dict(col0=2048, width=1024, mode="fm", dst="EFM", off=1024, dt=BF16, silu=True),
    dict(col0=3072, width=32, mode="fm", dst="LRT", off=0, dt=F32),
    dict(col0=3104, width=1024, mode="fm", dst="EFM", off=2048, dt=BF16),
    dict(col0=4128, width=1024, mode="fm", dst="EFM", off=3072, dt=BF16),
    dict(col0=5152, width=1024, mode="tm", dst="ETM", off=1024, dt=BF16),
]
ODD_BLOCKS = [
    dict(col0=0, width=1024, mode="fm", dst="OFM", off=0, dt=BF16, silu=True),
    dict(col0=1024, width=1536, mode="fm", dst="OFM", off=1024, dt=BF16),
    dict(col0=2560, width=32, mode="tm", dst="DTR", off=0, dt=F32),
    dict(col0=2592, width=1024, mode="fm", dst="OFM", off=2560, dt=BF16),
    dict(col0=3616, width=256, mode="fm", dst="OFM", off=3584, dt=BF16),
    dict(col0=3872, width=256, mode="tm", dst="OTM", off=0, dt=BF16),
]

INPUT_SHAPES = {
    "x": [T_FULL, D], "norm_gains": [4, 4, D], "gainsT": [128, 256],
    "ffn_w_gate": [4, D, HID], "ffn_w_up": [4, D, HID], "ffn_w_down": [4, HID, D],
    "even_w_in": [2, D, EVEN_IN], "even_w_out": [2, D, D], "odd_w_in": [2, D, ODD_IN], "odd_w_out": [2, D, D],
    "wdec": [2, 32, 2, 512], "bdecT": [2, 128, 8], "gnT": [2, 128, 8], "nab": [2, 8, 128, 3200],
    "convwT": [2, 128, 60], "convbT": [2, 128, 12], "dtb": [2, 1, 32], "alog": [2, 1, 32], "dskT": [2, 128, 8],
    "snT": [2, 128, 8], "sink": [2, 1, 8],
    "c_ident": [128, 128], "c_ones": [128, 128], "c_rotm": [128, 128], "c_mprev": [128, 128], "c_mnext": [128, 128],
    "c_cosT": [128, T_FULL], "c_sinT": [128, T_FULL], "c_rsel": [16, 2048],
}


def build_program(layers=(0, 1, 2, 3)):
    T = T_FULL
    NB = T // 128
    nc = bass.Bass("TRN2", target_bir_lowering=False)
    P = Prog(nc)
    I = {k: nc.dram_tensor(k, v, F32, kind="ExternalInput").ap() for k, v in INPUT_SHAPES.items()}
    out = nc.dram_tensor("out", [T, D], F32, kind="ExternalOutput").ap()
    S = {
        "XS": nc.dram_tensor("XS", [T, D], F32, kind="Internal").ap(),
        "EFM": nc.dram_tensor("EFM", [4096, T], BF16, kind="Internal").ap(),
        "ETM": nc.dram_tensor("ETM", [T, 2048], BF16, kind="Internal").ap(),
        "LRT": nc.dram_tensor("LRT", [32, T], F32, kind="Internal").ap(),
        "OFM": nc.dram_tensor("OFM", [3840, T], BF16, kind="Internal").ap(),
        "OTM": nc.dram_tensor("OTM", [T, 256], BF16, kind="Internal").ap(),
        "DTR": nc.dram_tensor("DTR", [T, 32], F32, kind="Internal").ap(),
        "MIXT": nc.dram_tensor("MIXT", [D, T], BF16, kind="Internal").ap(),
        "XBCS": nc.dram_tensor("XBCS", [1536, T], BF16, kind="Internal").ap(),
        "STATES": nc.dram_tensor("STATES", [2, NB, 128, 1024], F32, kind="Internal").ap(),
        "PREV": nc.dram_tensor("PREV", [2, NB, 128, 1024], BF16, kind="Internal").ap(),
    }
    C = {}
    gT = sb(P.outer, nc, "gTall", [128, 256], F32)
    with P.phase() as st:
        for k in ["ident", "ones", "rotm", "mprev", "mnext"]:
            t = sb(P.outer, nc, "cb_" + k, [128, 128], BF16)
            DMA(P, "pool", t[:], I["c_" + k], [], [k])
            C[k] = t[:]
        for k in ["ones", "mprev", "mnext"]:
            t = sb(P.outer, nc, "cf_" + k, [128, 128], F32)
            DMA(P, "sp", t[:], I["c_" + k], [], [k + "32"])
            C[k + "32"] = t[:]
        DMA(P, "sp", gT[:], I["gainsT"], [], ["gT"])
    first, last = layers[0], layers[-1]
    for L in layers:
        i = L // 2
        x_src = I["x"] if L == first else S["XS"]
        x_dst = out if L == last else S["XS"]
        g = lambda a: gT[:, (L * 4 + a) * 16:(L * 4 + a + 1) * 16]
        if L % 2 == 0:
            blocks = [dict(b, dst=S[b["dst"]]) for b in EVEN_BLOCKS]
            phase_A(P, C, x_src, g(0), I["even_w_in"][i], blocks, T)
            phase_GLA(P, C, S["EFM"], S["ETM"], S["LRT"], S["MIXT"], I["wdec"][i], I["bdecT"][i], I["gnT"][i], T)
            phase_NA(P, C, S["EFM"], S["ETM"], S["MIXT"], I["nab"][i], T)
            Wout = I["even_w_out"][i]
        else:
            blocks = [dict(b, dst=S[b["dst"]]) for b in ODD_BLOCKS]
            phase_A(P, C, x_src, g(0), I["odd_w_in"][i], blocks, T)
            prm = {k: I[k][i] for k in ["convwT", "convbT", "dtb", "alog", "dskT", "snT"]}
            prm["rsel"] = I["c_rsel"]
            phase_SSD(P, C, S["OFM"], S["DTR"], S["MIXT"], S["XBCS"], S["STATES"], S["PREV"], prm, T)
            phase_SWA(P, C, S["OFM"], S["OTM"], S["MIXT"], I["c_cosT"], I["c_sinT"], I["sink"][i], T)
            Wout = I["odd_w_out"][i]
        phase_CD(P, C, S["MIXT"], x_src, x_dst, I["norm_gains"][L, 1:2, :], g(2), I["norm_gains"][L, 3:4, :],
                 Wout, I["ffn_w_gate"][L], I["ffn_w_up"][L], I["ffn_w_down"][L], T)
    n_ops = P.n_ops
    P.close()
    return nc, n_ops


def host_inputs(inp):
    f = lambda a: np.ascontiguousarray(np.asarray(a, dtype=np.float32))
    h = {}
    for k in ["norm_gains", "ffn_w_gate", "ffn_w_up", "ffn_w_down", "even_w_in", "even_w_out", "odd_w_in", "odd_w_out"]:
        h[k] = f(inp[k])
    g = f(inp["norm_gains"])
    h["gainsT"] = f(g.reshape(4, 4, 16, 128).transpose(3, 0, 1, 2).reshape(128, 256))
    wd = f(inp["gla_w_decay"])
    wdec = np.zeros((2, 32, 2, 512), np.float32)
    wdec[:, 0:16, 0, :] = wd[:, 0]
    wdec[:, 16:32, 1, :] = wd[:, 1]
    h["wdec"] = wdec
    bd = f(inp["gla_b_decay"])
    h["bdecT"] = f(bd.reshape(2, 2, 4, 128).transpose(0, 3, 1, 2).reshape(2, 128, 8))
    h["gnT"] = f(f(inp["gla_norm"]).reshape(2, 8, 128).transpose(0, 2, 1))
    rpb = f(inp["na_rpb"])
    h["nab"] = np.stack([na_bias_tables(rpb[i]) for i in range(2)])
    sp = [ssd_host_params(f(inp["ssd_conv_w"])[i], f(inp["ssd_conv_b"])[i], f(inp["ssd_dt_bias"])[i],
                          f(inp["ssd_a_log"])[i], f(inp["ssd_d"])[i], f(inp["ssd_norm"])[i]) for i in range(2)]
    for k in ["convwT", "convbT", "dtb", "alog", "dskT", "snT"]:
        h[k] = np.stack([sp[i][k] for i in range(2)])
    h["sink"] = f(inp["swa_sink"]).reshape(2, 1, 8)
    for k, v in make_consts().items():
        h["c_" + k] = v
    h["c_rsel"] = make_rsel()
    return h


_CACHE = {}


def kernel(**inputs):
    x = np.asarray(inputs["x"], dtype=np.float32)
    h = host_inputs(inputs)
    if "nc" not in _CACHE:
        _CACHE["nc"] = build_program()[0]
    nc = _CACHE["nc"]
    in_maps = []
    for c in range(NCORES):
        m = dict(h)
        m["x"] = np.ascontiguousarray(x[c % 4])
        in_maps.append(m)
    res = run_bass_kernel_spmd(nc, in_maps, core_ids=list(range(NCORES)))
    return np.stack([np.asarray(res.results[c]["out"], dtype=np.float32) for c in range(4)], axis=0)
```

```python
import contextlib
import numpy as np
import concourse.bass as bass
import concourse.mybir as mybir
from concourse.bass_utils import run_bass_kernel_spmd

F32 = mybir.dt.float32
BF16 = mybir.dt.bfloat16
AF = mybir.ActivationFunctionType
ALU = mybir.AluOpType

D = 2048
T_FULL = 4096
HID = 5632
EPS = 1e-6
EVEN_IN = 6176
ODD_IN = 4128

ENGS = ["pe", "act", "dve", "pool", "sp"]
DMA_ENGS = ["sp", "pool", "act"]
NDMASEM = 12


class Prog:
    def __init__(self, nc):
        self.nc = nc
        self.outer = contextlib.ExitStack()
        self.sems = {}
        for e in ENGS:
            self.sems[e] = self.outer.enter_context(nc.semaphore("s_" + e))
        for e in DMA_ENGS:
            for j in range(NDMASEM):
                self.sems[("dma", e, j)] = self.outer.enter_context(nc.semaphore(f"d_{e}_{j}"))
        self.cnt = {e: 0 for e in ENGS}
        self.dma_cnt = {}
        self.dma_rr = {e: 0 for e in ENGS}
        self.known = {e: {} for e in ENGS}
        self.n_ops = 0
        self._reset()

    def _reset(self):
        self.ops = {e: [] for e in ENGS}
        self.last_w = {}
        self.readers = {}

    def op(self, eng, fn, reads=(), writes=(), dma=False):
        deps = {}

        def note(t):
            sk, v, src = t
            cur = deps.get(sk)
            if cur is None or cur[0] < v:
                deps[sk] = (v, src)

        for k in reads:
            t = self.last_w.get(k)
            if t is not None:
                note(t)
        for k in writes:
            t = self.last_w.get(k)
            if t is not None:
                note(t)
            for t in self.readers.get(k, ()):
                note(t)
        if dma:
            j = self.dma_rr[eng]
            self.dma_rr[eng] = (j + 1) % NDMASEM
            sk = ("dma", eng, j)
            c = self.dma_cnt.get(sk, 0)
            if c > 0:
                note((sk, 16 * c, "dma"))
            self.dma_cnt[sk] = c + 1
            tok = (sk, 16 * (c + 1), "dma")
            inc = (sk, 16)
        else:
            self.cnt[eng] += 1
            tok = (eng, self.cnt[eng], eng)
            inc = (eng, 1)
        waits = []
        kn = self.known[eng]
        for sk, (v, src) in deps.items():
            if src == "pe" and eng == "pe" and not dma:
                continue
            if kn.get(sk, 0) >= v:
                continue
            kn[sk] = v
            waits.append((sk, v))
        self.ops[eng].append((fn, waits, inc))
        self.n_ops += 1
        for k in reads:
            self.readers.setdefault(k, []).append(tok)
        for k in writes:
            self.last_w[k] = tok
            self.readers[k] = []
        return tok

    @contextlib.contextmanager
    def phase(self):
        st = contextlib.ExitStack()
        self._reset()
        try:
            yield st
            self._flush()
        finally:
            st.close()

    def _flush(self):
        nc = self.nc
        bar = [(e, self.cnt[e]) for e in ENGS if self.cnt[e] > 0]
        bar += [(sk, 16 * c) for sk, c in self.dma_cnt.items() if c > 0]
        sems = self.sems
        with nc.Block() as block:
            def run(e):
                def body(eng):
                    for (fn, waits, inc) in self.ops[e]:
                        for sk, v in waits:
                            eng.wait_ge(sems[sk], v)
                        fn(eng).then_inc(sems[inc[0]], inc[1])
                    kn = self.known[e]
                    for sk, v in bar:
                        if kn.get(sk, 0) < v:
                            eng.wait_ge(sems[sk], v)
                            kn[sk] = v
                return body

            block.tensor(run("pe"))
            block.scalar(run("act"))
            block.vector(run("dve"))
            block.gpsimd(run("pool"))
            block.sync(run("sp"))
        self._reset()

    def close(self):
        self.outer.close()


_UID = [0]


def sb(st, nc, name, shape, dtype):
    _UID[0] += 1
    return st.enter_context(nc.sbuf_tensor(f"sb{_UID[0]}_{name}", list(shape), dtype))


def pst(st, nc, name, shape, dtype):
    _UID[0] += 1
    return st.enter_context(nc.psum_tensor(f"ps{_UID[0]}_{name}", list(shape), dtype))


def MM(P, out, lhsT, rhs, start, stop, r, w):
    P.op("pe", lambda e: e.matmul(out, lhsT=lhsT, rhs=rhs, start=start, stop=stop), r, w)


def TR(P, out, in_, ident, r, w):
    P.op("pe", lambda e: e.transpose(out=out, in_=in_, identity=ident), r, w)


def ACT(P, out, in_, func, r, w, bias=None, scale=None, accum=None):
    kw = {}
    if bias is not None:
        kw["bias"] = bias
    if scale is not None:
        kw["scale"] = scale
    if accum is not None:
        kw["accum_out"] = accum
    P.op("act", lambda e: e.activation(out=out, in_=in_, func=func, **kw), r, w)


def TT(P, eng, out, in0, in1, op, r, w):
    P.op(eng, lambda e: e.tensor_tensor(out=out, in0=in0, in1=in1, op=op), r, w)


def TS(P, eng, out, in0, s1, s2, op0, op1, r, w):
    if s2 is None:
        P.op(eng, lambda e: e.tensor_scalar(out=out, in0=in0, scalar1=s1, scalar2=None, op0=op0), r, w)
    else:
        P.op(eng, lambda e: e.tensor_scalar(out=out, in0=in0, scalar1=s1, scalar2=s2, op0=op0, op1=op1), r, w)


def STT(P, out, in0, scalar, in1, op0, op1, r, w):
    P.op("dve", lambda e: e.scalar_tensor_tensor(out=out, in0=in0, scalar=scalar, in1=in1, op0=op0, op1=op1), r, w)


def CP(P, eng, out, in_, r, w):
    if eng == "act":
        P.op("act", lambda e: e.copy(out=out, in_=in_), r, w)
    else:
        P.op(eng, lambda e: e.tensor_copy(out=out, in_=in_), r, w)


def RECIP(P, out, in_, r, w):
    P.op("dve", lambda e: e.reciprocal(out=out, in_=in_), r, w)


def DMA(P, eng, out, in_, r, w):
    P.op(eng, lambda e: e.dma_start(out=out, in_=in_), r, w, dma=True)


def MEMSET(P, eng, ap, val, r, w):
    P.op(eng, lambda e: e.memset(ap, val), r, w)


def rstd_from_ss(P, col, n, key):
    ACT(P, col[:, 1:2], col[:, 0:1], AF.Sqrt, [key], [key], bias=EPS, scale=1.0 / n)
    RECIP(P, col[:, 2:3], col[:, 1:2], [key], [key])


def phase_A(P, C, x_src, gT, W, blocks, T):
    nc = P.nc
    TA = 1024 if T >= 1024 else T
    NSUB = TA // 128
    NH = TA // 512
    with P.phase() as st:
        xa = sb(st, nc, "xa", [128, 2, D], F32)
        hbf = sb(st, nc, "hbfA", [128, 2, D], BF16)
        hT = sb(st, nc, "hT", [128, 16, TA], BF16)
        wsl = sb(st, nc, "wslA", [128, 4, 8192], BF16)
        stg = sb(st, nc, "stgA", [128, 4, 512], F32)
        stb = sb(st, nc, "stbA", [128, 4, 512], BF16)
        col = sb(st, nc, "colA", [128, 2, 4], F32)
        ps = pst(st, nc, "psA", [128, 4096], F32)
        slot_i = 0
        stg_i = 0
        bank_i = 0
        ev_i = 0
        for tile in range(T // TA):
            for sub in range(NSUB):
                j = sub % 2
                t0 = tile * TA + sub * 128
                kx, kh, kc = ("xa", j), ("hbf", j), ("col", j)
                DMA(P, "sp", xa[:, j, :], x_src[t0:t0 + 128, :], [], [kx])
                ACT(P, hbf[:, j, :], xa[:, j, :], AF.Square, [kx], [kh, kc], accum=col[:, j, 0:1])
                rstd_from_ss(P, col[:, j, :], D, kc)
                TS(P, "dve", hbf[:, j, :], xa[:, j, :], col[:, j, 2:3], None, ALU.mult, None, [kx, kc], [kh])
                for kg in range(4):
                    bank = 6 + (kg % 2)
                    kb = ("ps", bank)
                    ptb = ps[:, bank * 512:(bank + 1) * 512].bitcast(BF16)
                    for kk in range(4):
                        k = kg * 4 + kk
                        TR(P, ptb[:, kk * 128:(kk + 1) * 128], hbf[:, j, k * 128:(k + 1) * 128], C["ident"], [kh], [kb])
                    TT(P, "dve", hT[:, kg * 4:(kg + 1) * 4, sub * 128:(sub + 1) * 128],
                       ptb[:, 0:512].rearrange("p (a b) -> p a b", a=4),
                       gT[:, kg * 4:(kg + 1) * 4].unsqueeze(2).to_broadcast([128, 4, 128]), ALU.mult,
                       [], [kb, ("hT", sub // 4)])
            for blk in blocks:
                for g0 in range(0, blk["width"], 512):
                    gw = min(512, blk["width"] - g0)
                    s = slot_i % 4
                    slot_i += 1
                    ks = ("wsl", s)
                    wv = wsl[:, s, 0:16 * gw].rearrange("p (k n) -> p k n", k=16)
                    c0w = blk["col0"] + g0
                    DMA(P, "pool", wv, W[:, c0w:c0w + gw].rearrange("(k p) n -> p k n", p=128), [], [ks])
                    if blk["mode"] == "fm":
                        for c0 in range(0, gw, 128):
                            cw = min(128, gw - c0)
                            for half in range(NH):
                                bank = bank_i % 6
                                bank_i += 1
                                kb = ("ps", bank)
                                pb = ps[0:cw, bank * 512:(bank + 1) * 512]
                                for k in range(16):
                                    MM(P, pb, wv[:, k, c0:c0 + cw], hT[:, k, half * 512:(half + 1) * 512],
                                       k == 0, k == 15, [ks, ("hT", half)], [kb])
                                si = stg_i % 4
                                stg_i += 1
                                kst = ("stg", si)
                                dsb = stb[0:cw, si, :] if blk["dt"] == BF16 else stg[0:cw, si, :]
                                if blk.get("silu"):
                                    ACT(P, dsb, pb, AF.Silu, [], [kb, kst])
                                else:
                                    ev_i += 1
                                    CP(P, "act" if ev_i % 2 else "dve", dsb, pb, [], [kb, kst])
                                r0 = blk["off"] + g0 + c0
                                tt0 = tile * TA + half * 512
                                DMA(P, "sp", blk["dst"][r0:r0 + cw, tt0:tt0 + 512], dsb, [kst], [])
                    else:
                        for sub in range(NSUB):
                            bank = bank_i % 6
                            bank_i += 1
                            kb = ("ps", bank)
                            pb = ps[:, bank * 512:bank * 512 + gw]
                            for k in range(16):
                                MM(P, pb, hT[:, k, sub * 128:(sub + 1) * 128], wv[:, k, 0:gw],
                                   k == 0, k == 15, [ks, ("hT", sub // 4)], [kb])
                            si = stg_i % 4
                            stg_i += 1
                            kst = ("stg", si)
                            dsb = stb[:, si, 0:gw] if blk["dt"] == BF16 else stg[:, si, 0:gw]
                            ev_i += 1
                            CP(P, "act" if ev_i % 2 else "dve", dsb, pb, [], [kb, kst])
                            tt0 = tile * TA + sub * 128
                            cc0 = blk["off"] + g0
                            DMA(P, "sp", blk["dst"][tt0:tt0 + 128, cc0:cc0 + gw], dsb, [kst], [])


def phase_CD(P, C, MIXT, x_src, x_dst, g1row, g2T, g3row, Wout, Wg, Wu, Wd, T, do_ffn=True):
    nc = P.nc
    NKH = HID // 128
    with P.phase() as st:
        aT = sb(st, nc, "aT", [128, 16, 512], BF16)
        actT = sb(st, nc, "actT", [128, NKH, 512], BF16)
        xt = sb(st, nc, "xt", [128, 2, D], F32)
        f = sb(st, nc, "f", [128, 4, D], F32)
        gb = sb(st, nc, "gb", [128, D], F32)
        hbf = sb(st, nc, "hbfC", [128, 2, D], BF16)
        wsl = sb(st, nc, "wslC", [128, 4, 8192], BF16)
        sg = sb(st, nc, "sg", [128, 2, 512], F32)
        ss = sb(st, nc, "ss", [128, 4, 8], F32)
        col = sb(st, nc, "colC", [128, 2, 4], F32)
        ps = pst(st, nc, "psC", [128, 4096], F32)
        state = {"slot": 0}

        junk = sb(st, nc, "junkC", [128, 512], BF16)

        def tok_proj(src, nk, Wdram, srckey):
            for cg in range(4):
                for kg0 in range(0, nk, 16):
                    kn = min(16, nk - kg0)
                    s = state["slot"] % 4
                    state["slot"] += 1
                    ks = ("wsl", s)
                    wv = wsl[:, s, 0:kn * 512].rearrange("p (k n) -> p k n", k=kn)
                    DMA(P, "pool", wv,
                        Wdram[kg0 * 128:(kg0 + kn) * 128, cg * 512:(cg + 1) * 512].rearrange("(k p) n -> p k n", p=128),
                        [], [ks])
                    for k in range(kn):
                        kk = kg0 + k
                        for sub in range(4):
                            MM(P, ps[:, sub * 512:(sub + 1) * 512], src[:, kk, sub * 128:(sub + 1) * 128], wv[:, k, :],
                               kk == 0, kk == nk - 1, [ks, srckey], [("ps", sub)])
                for sub in range(4):
                    pb = ps[:, sub * 512:(sub + 1) * 512]
                    ACT(P, junk[:], pb, AF.Square, [], [("ps", sub), ("ss", sub)],
                        accum=ss[:, sub, cg:cg + 1])
                    TT(P, "dve", f[:, sub, cg * 512:(cg + 1) * 512], pb, gb[:, cg * 512:(cg + 1) * 512], ALU.mult,
                       ["gb"], [("ps", sub), ("f", sub)])

        def finish_rstd(sub):
            k = ("ss", sub)
            P.op("dve", lambda e: e.tensor_reduce(out=ss[:, sub, 4:5], in_=ss[:, sub, 0:4], axis=mybir.AxisListType.X,
                                                   op=ALU.add), [k], [k])
            ACT(P, ss[:, sub, 5:6], ss[:, sub, 4:5], AF.Sqrt, [k], [k], bias=EPS, scale=1.0 / D)
            RECIP(P, ss[:, sub, 6:7], ss[:, sub, 5:6], [k], [k])

        for tile in range(T // 512):
            t0 = tile * 512
            DMA(P, "sp", aT[:], MIXT[:, t0:t0 + 512].rearrange("(k p) t -> p k t", p=128), [], ["aT"])
            DMA(P, "sp", gb[:], g1row.partition_broadcast(128), [], ["gb"])
            tok_proj(aT, 16, Wout, "aT")
            for sub in range(4):
                j = sub % 2
                kx, kh, kc = ("xt", j), ("hbf", j), ("col", j)
                tt0 = t0 + sub * 128
                kd = ("xd", tile, sub)
                finish_rstd(sub)
                DMA(P, "sp", xt[:, j, :], x_src[tt0:tt0 + 128, :], [kd], [kx])
                STT(P, xt[:, j, :], f[:, sub, :], ss[:, sub, 6:7], xt[:, j, :], ALU.mult, ALU.add,
                    [("f", sub), ("ss", sub), kx], [kx])
                DMA(P, "sp", x_dst[tt0:tt0 + 128, :], xt[:, j, :], [kx], [kd])
                if not do_ffn:
                    continue
                ACT(P, hbf[:, j, :], xt[:, j, :], AF.Square, [kx], [kh, kc], accum=col[:, j, 0:1])
                rstd_from_ss(P, col[:, j, :], D, kc)
                TS(P, "dve", hbf[:, j, :], xt[:, j, :], col[:, j, 2:3], None, ALU.mult, None, [kx, kc], [kh])
                for kg in range(4):
                    bank = 4 + 2 * (kg % 2)
                    kb = ("ps", bank)
                    ptb = ps[:, bank * 512:(bank + 1) * 512].bitcast(BF16)
                    for kk in range(4):
                        k = kg * 4 + kk
                        TR(P, ptb[:, kk * 128:(kk + 1) * 128], hbf[:, j, k * 128:(k + 1) * 128], C["ident"], [kh], [kb])
                    TT(P, "dve", aT[:, kg * 4:(kg + 1) * 4, sub * 128:(sub + 1) * 128],
                       ptb[:, 0:512].rearrange("p (a b) -> p a b", a=4),
                       g2T[:, kg * 4:(kg + 1) * 4].unsqueeze(2).to_broadcast([128, 4, 128]), ALU.mult,
                       [], [kb, "aT"])
            if not do_ffn:
                continue
            DMA(P, "sp", gb[:], g3row.partition_broadcast(128), [], ["gb"])
            for hg in range(HID // 512):
                sl = []
                for Wm in (Wg, Wu):
                    s = state["slot"] % 4
                    state["slot"] += 1
                    wv = wsl[:, s, 0:8192].rearrange("p (k n) -> p k n", k=16)
                    DMA(P, "pool", wv, Wm[:, hg * 512:(hg + 1) * 512].rearrange("(k p) n -> p k n", p=128), [], [("wsl", s)])
                    sl.append((wv, ("wsl", s)))
                for c in range(4):
                    hc = hg * 4 + c
                    bG = 4 + 2 * (hc % 2)
                    bU = bG + 1
                    for (wv, ks), b in zip(sl, (bG, bU)):
                        for k in range(16):
                            MM(P, ps[:, b * 512:(b + 1) * 512], wv[:, k, c * 128:(c + 1) * 128], aT[:, k, :],
                               k == 0, k == 15, [ks, "aT"], [("ps", b)])
                    ACT(P, sg[:, hc % 2, :], ps[:, bG * 512:(bG + 1) * 512], AF.Silu, [], [("ps", bG), ("sg", hc % 2)])
                    TT(P, "dve", actT[:, hc, :], sg[:, hc % 2, :], ps[:, bU * 512:(bU + 1) * 512], ALU.mult,
                       [("sg", hc % 2)], [("ps", bU), "actT"])
            tok_proj(actT, NKH, Wd, "actT")
            for sub in range(4):
                j = sub % 2
                kx = ("xt", j)
                tt0 = t0 + sub * 128
                kd = ("xd", tile, sub)
                finish_rstd(sub)
                DMA(P, "sp", xt[:, j, :], x_dst[tt0:tt0 + 128, :], [kd], [kx])
                STT(P, xt[:, j, :], f[:, sub, :], ss[:, sub, 6:7], xt[:, j, :], ALU.mult, ALU.add,
                    [("f", sub), ("ss", sub), kx], [kx])
                DMA(P, "sp", x_dst[tt0:tt0 + 128, :], xt[:, j, :], [kx], [kd])


def phase_SWA(P, C, OFM, OTM, MIXT, cosT_d, sinT_d, sink_row, T):
    nc = P.nc
    NB = T // 128
    SC = 128 ** -0.5
    with P.phase() as st:
        cosT = sb(st, nc, "cosT", [128, T], F32)
        sinT = sb(st, nc, "sinT", [128, T], F32)
        raw = sb(st, nc, "raw", [128, 2, 512], BF16)
        t1 = sb(st, nc, "t1", [128, 2, 512], F32)
        t2 = sb(st, nc, "t2", [128, 2, 512], F32)
        qrot = sb(st, nc, "qrot", [128, 4, T], BF16)
        krot = sb(st, nc, "krot", [128, T], BF16)
        V = sb(st, nc, "Vswa", [128, NB, 128], BF16)
        pT = sb(st, nc, "pT", [128, 4, 512], BF16)
        es = sb(st, nc, "es", [128, 8], F32)
        den = sb(st, nc, "den", [128, 2, 512], F32)
        ost = sb(st, nc, "ost", [128, 4, 512], BF16)
        ps = pst(st, nc, "psW", [128, 4096], F32)
        DMA(P, "sp", cosT[:], cosT_d[:, 0:T], [], ["cos"])
        DMA(P, "sp", sinT[:], sinT_d[:, 0:T], [], ["sin"])
        DMA(P, "sp", es[:], sink_row.partition_broadcast(128), [], ["es"])
        ACT(P, es[:], es[:], AF.Exp, ["es"], ["es"])
        ri = 0
        pi = 0

        def rope(src_rows, dst_ap_fn, dkey):
            nonlocal ri
            for tb in range(T // 512):
                j = ri % 2
                ri += 1
                bank = 6 + j
                kb = ("ps", bank)
                pb = ps[:, bank * 512:(bank + 1) * 512]
                tsl = slice(tb * 512, (tb + 1) * 512)
                DMA(P, "sp", raw[:, j, :], OFM[src_rows:src_rows + 128, tsl], [], [("raw", j)])
                MM(P, pb, C["rotm"], raw[:, j, :], True, True, [("raw", j)], [kb])
                TT(P, "dve", t1[:, j, :], raw[:, j, :], cosT[:, tsl], ALU.mult, [("raw", j), "cos"], [("t1", j)])
                TT(P, "dve", t2[:, j, :], pb, sinT[:, tsl], ALU.mult, ["sin"], [kb, ("t2", j)])
                TT(P, "pool", dst_ap_fn(tsl), t1[:, j, :], t2[:, j, :], ALU.add, [("t1", j), ("t2", j)], [dkey])

        for g in range(2):
            rope(3584 + g * 128, lambda tsl: krot[:, tsl], "krot")
            for h in range(4):
                rope(2560 + (g * 4 + h) * 128, (lambda hh: (lambda tsl: qrot[:, hh, tsl]))(h), "qrot")
            DMA(P, "sp", V[:], OTM[:, g * 128:(g + 1) * 128].rearrange("(n p) c -> p n c", p=128), [], ["V"])
            for n in range(NB):
                bO = 2 + (n % 2)
                bD = 4 + (n % 2)
                pO = ps[:, bO * 512:(bO + 1) * 512]
                pD = ps[:, bD * 512:(bD + 1) * 512]
                js = [j for j in (n - 1, n, n + 1) if 0 <= j < NB]
                for idx, jb in enumerate(js):
                    bS = pi % 2
                    pslot = pi % 4
                    pi += 1
                    pS = ps[:, bS * 512:(bS + 1) * 512]
                    kp = ("pT", pslot)
                    MM(P, pS, krot[:, jb * 128:(jb + 1) * 128], qrot[:, :, n * 128:(n + 1) * 128], True, True,
                       ["krot", "qrot"], [("ps", bS)])
                    ACT(P, pT[:, pslot, :], pS, AF.Exp, [], [("ps", bS), kp], scale=SC)
                    if jb != n:
                        mk = C["mprev"] if jb < n else C["mnext"]
                        TT(P, "dve", pT[:, pslot, :].rearrange("p (h q) -> p h q", h=4),
                           pT[:, pslot, :].rearrange("p (h q) -> p h q", h=4),
                           mk.unsqueeze(1).to_broadcast([128, 4, 128]), ALU.mult, [], [kp])
                    MM(P, pO, V[:, jb, :], pT[:, pslot, :], idx == 0, idx == len(js) - 1, ["V", kp], [("ps", bO)])
                    MM(P, pD, C["ones"], pT[:, pslot, :], idx == 0, idx == len(js) - 1, [kp], [("ps", bD)])
                dj = n % 2
                TT(P, "dve", den[:, dj, :].rearrange("p (h q) -> p h q", h=4), pD.rearrange("p (h q) -> p h q", h=4),
                   es[:, g * 4:(g + 1) * 4].unsqueeze(2).to_broadcast([128, 4, 128]), ALU.add,
                   ["es"], [("ps", bD), ("den", dj)])
                RECIP(P, den[:, dj, :], den[:, dj, :], [], [("den", dj)])
                TT(P, "dve", ost[:, :, (n % 4) * 128:(n % 4 + 1) * 128], pO.rearrange("p (h q) -> p h q", h=4),
                   den[:, dj, :].rearrange("p (h q) -> p h q", h=4), ALU.mult, [("den", dj)], [("ps", bO), "ost"])
                if n % 4 == 3:
                    n0 = n - 3
                    r0 = 1024 + g * 512
                    DMA(P, "sp", MIXT[r0:r0 + 512, n0 * 128:n0 * 128 + 512].rearrange("(h d) t -> d h t", d=128),
                        ost[:], ["ost"], [])


def na_pattern(m):
    return 0 if m == 0 else 1 if m == 1 else 3 if m == 30 else 4 if m == 31 else 2


def phase_NA(P, C, EFM, ETM, MIXT, nab, T):
    nc = P.nc
    NB = T // 128
    SC = 128 ** -0.5
    assert T == 4096
    with P.phase() as st:
        qT = sb(st, nc, "naq", [128, 2, T], BF16)
        kT = sb(st, nc, "nak", [128, 2, T], BF16)
        V = sb(st, nc, "nav", [128, 2, NB * 128], BF16)
        bias = sb(st, nc, "nabias", [128, 2, 5 * 640], F32)
        sbf = sb(st, nc, "nasb", [128, 2, 640], F32)
        pT = sb(st, nc, "napT", [128, 2, 640], BF16)
        rden = sb(st, nc, "narden", [128, 2, 128], F32)
        ost = sb(st, nc, "naost", [128, 2, 512], BF16)
        ps = pst(st, nc, "psN", [128, 4096], F32)
        it = 0
        for h in range(8):
            hj = h % 2
            kq, kk, kv, kbz = ("q", hj), ("k", hj), ("v", hj), ("b", hj)
            DMA(P, "sp", qT[:, hj, :], EFM[2048 + h * 128:2048 + (h + 1) * 128, :], [], [kq])
            DMA(P, "sp", kT[:, hj, :], EFM[3072 + h * 128:3072 + (h + 1) * 128, :], [], [kk])
            DMA(P, "sp", V[:, hj, :].rearrange("p (n c) -> p n c", c=128),
                ETM[:, 1024 + h * 128:1024 + (h + 1) * 128].rearrange("(n p) c -> p n c", p=128), [], [kv])
            DMA(P, "sp", bias[:, hj, :], nab[h], [], [kbz])
            for m in range(NB):
                u0 = min(max(m - 2, 0), NB - 5)
                pat = na_pattern(m)
                sj = it % 2
                it += 1
                b0 = 2 * sj
                kS = ("ps", b0)
                pS = ps[:, b0 * 512:b0 * 512 + 640]
                for j in range(5):
                    MM(P, pS[:, j * 128:(j + 1) * 128], kT[:, hj, (u0 + j) * 128:(u0 + j + 1) * 128],
                       qT[:, hj, m * 128:(m + 1) * 128], True, True, [kq, kk], [kS if j < 4 else ("ps", b0 + 1)])
                STT(P, sbf[:, sj, 0:512], pS[:, 0:512], SC, bias[:, hj, pat * 640:pat * 640 + 512], ALU.mult, ALU.add,
                    [kbz], [kS, ("sbf", sj)])
                STT(P, sbf[:, sj, 512:640], pS[:, 512:640], SC, bias[:, hj, pat * 640 + 512:(pat + 1) * 640], ALU.mult, ALU.add,
                    [kbz], [("ps", b0 + 1), ("sbf", sj)])
                ACT(P, pT[:, sj, :], sbf[:, sj, :], AF.Exp, [("sbf", sj)], [("pT", sj)])
                bO = 4 + sj
                pO = ps[:, bO * 512:bO * 512 + 128]
                pD = ps[:, bO * 512 + 128:bO * 512 + 256]
                for j in range(5):
                    MM(P, pO, V[:, hj, (u0 + j) * 128:(u0 + j + 1) * 128], pT[:, sj, j * 128:(j + 1) * 128],
                       j == 0, j == 4, [kv, ("pT", sj)], [("ps", bO)])
                for j in range(5):
                    MM(P, pD, C["ones"], pT[:, sj, j * 128:(j + 1) * 128], j == 0, j == 4, [("pT", sj)], [("ps", bO)])
                RECIP(P, rden[:, sj, :], pD, [], [("ps", bO), ("rden", sj)])
                oj = (m // 4) % 2
                TT(P, "dve", ost[:, oj, (m % 4) * 128:(m % 4 + 1) * 128], pO, rden[:, sj, :], ALU.mult,
                   [("rden", sj)], [("ps", bO), ("ost", oj)])
                if m % 4 == 3:
                    n0 = m - 3
                    DMA(P, "sp", MIXT[1024 + h * 128:1024 + (h + 1) * 128, n0 * 128:n0 * 128 + 512], ost[:, oj, :],
                        [("ost", oj)], [])


def make_consts(T=T_FULL):
    c = {}
    c["ident"] = np.eye(128, dtype=np.float32)
    c["ones"] = np.ones((128, 128), np.float32)
    rot = np.zeros((128, 128), np.float32)
    for m in range(64):
        rot[m + 64, m] = -1.0
        rot[m, m + 64] = 1.0
    c["rotm"] = rot
    jj = np.arange(128)[:, None]
    ii = np.arange(128)[None, :]
    c["mprev"] = (ii <= jj).astype(np.float32)
    c["mnext"] = (jj <= ii).astype(np.float32)
    half = 64
    inv = (10000.0 ** (-np.arange(half, dtype=np.float32) / half)).astype(np.float32)
    ang = np.arange(T, dtype=np.float32)[None, :] * np.concatenate([inv, inv])[:, None]
    c["cosT"] = np.cos(ang).astype(np.float32)
    c["sinT"] = np.sin(ang).astype(np.float32)
    return c


def na_bias_tables(rpb):
    H = rpb.shape[0]
    out = np.full((H, 128, 5, 5, 128), -30000.0, np.float32)
    reps = [0, 1, 5, 30, 31]
    for p, m in enumerate(reps):
        u0 = min(max(m - 2, 0), 27)
        ktok = (u0 * 128 + np.arange(640)).reshape(5, 128)
        kr, kc = ktok // 64, ktok % 64
        qtok = m * 128 + np.arange(128)
        r, cq = qtok // 64, qtok % 64
        rs = np.clip(r - 4, 0, 56)
        cs = np.clip(cq - 8, 0, 48)
        KR = kr[:, :, None]
        KC = kc[:, :, None]
        valid = (KR >= rs[None, None, :]) & (KR < rs[None, None, :] + 8) & (KC >= cs[None, None, :]) & (KC < cs[None, None, :] + 16)
        dr = np.clip(KR - r[None, None, :] + 7, 0, 14)
        dc = np.clip(KC - cq[None, None, :] + 15, 0, 30)
        for h in range(H):
            vals = rpb[h][dr, dc]
            tab = np.where(valid, vals, np.float32(-30000.0)).astype(np.float32)
            out[h, :, p, :, :] = tab.transpose(1, 0, 2)
    return out.reshape(H, 128, 5 * 640)


def phase_GLA(P, C, EFM, ETM, LRT, MIXT, wdec_d, bdecT_d, gnT_d, T):
    nc = P.nc
    NB = T // 128
    NT5 = T // 512
    with P.phase() as st:
        lrT = sb(st, nc, "lrT", [32, T], F32)
        wdec = sb(st, nc, "wdec", [32, 2 * 512], F32)
        negb = sb(st, nc, "negb", [128, 8], F32)
        gnT = sb(st, nc, "gnT", [128, 8], F32)
        rmask = sb(st, nc, "rmask", [128, 512], F32)
        mk2 = sb(st, nc, "mk2", [128, 2, 128], BF16)
        cs = sb(st, nc, "cs", [128, 2, T], F32)
        tmpe = sb(st, nc, "tmpe", [128, 2, 512], F32)
        tmps = sb(st, nc, "tmps", [128, 2, 512], F32)
        tmpp = sb(st, nc, "tmpp", [128, 2, 512], F32)
        qk = sb(st, nc, "qkraw", [128, 2, T], BF16)
        qkt = sb(st, nc, "qkt", [128, 4, T], BF16)
        ktok = sb(st, nc, "ktok", [128, 2, NB * 128], BF16)
        Vp = sb(st, nc, "Vp", [128, NB, 256], BF16)
        Sprev = sb(st, nc, "Sprev", [128, 2, NB * 128], BF16)
        S = sb(st, nc, "Sst", [128, 2, 128], F32)
        tS = sb(st, nc, "tS", [128, 2, 128], F32)
        dl = sb(st, nc, "dl", [128, 2, NB], F32)
        ATm = sb(st, nc, "ATm", [128, 2, 512], BF16)
        sq = sb(st, nc, "sq", [128, 2, 512], BF16)
        rt = sb(st, nc, "rt", [128, 2, 512], F32)
        o1 = sb(st, nc, "o1", [128, 2, 512], F32)
        sgt = sb(st, nc, "sgt", [128, 2, 512], BF16)
        ostg = sb(st, nc, "ostg", [128, 2, 512], BF16)
        ps = pst(st, nc, "psG", [128, 4096], F32)

        def bank(i, w=512):
            return ps[:, i * 512:i * 512 + w]

        DMA(P, "sp", lrT[:], LRT, [], ["lrT"])
        DMA(P, "sp", wdec[:], wdec_d.rearrange("r d c -> r (d c)"), [], ["wdec"])
        DMA(P, "sp", negb[:], bdecT_d, [], ["negb"])
        TS(P, "dve", negb[:], negb[:], -1.0, None, ALU.mult, None, ["negb"], ["negb"])
        DMA(P, "sp", gnT[:], gnT_d, [], ["gnT"])
        MEMSET(P, "dve", rmask[:], 1.0, [], ["rmask"])
        for c in range(4):
            MEMSET(P, "dve", rmask[:, c * 128:c * 128 + 1], 0.0, [], ["rmask"])
        CP(P, "dve", mk2[:, 0, :], C["mnext"], [], ["mk2"])
        CP(P, "dve", mk2[:, 1, :], C["mprev"], [], ["mk2"])
        ti = 0
        ei = 0
        for pr in range(4):
            DMA(P, "sp", qk[:, 0, :], EFM[pr * 128:(pr + 1) * 128, :], [], ["qraw"])
            DMA(P, "sp", qk[:, 1, :], EFM[512 + pr * 128:512 + (pr + 1) * 128, :], [], ["kraw"])
            DMA(P, "sp", Vp[:], ETM[:, pr * 256:(pr + 1) * 256].rearrange("(n p) c -> p n c", p=128), [], ["Vp"])
            for dr in range(2):
                for tb in range(NT5):
                    j = ti % 2
                    ti += 1
                    tsl = slice(tb * 512, (tb + 1) * 512)
                    MM(P, bank(0), wdec[0:32, dr * 512 + pr * 128:dr * 512 + (pr + 1) * 128], lrT[0:32, tsl], True, True,
                       ["wdec", "lrT"], [("ps", 0)])
                    ACT(P, tmpe[:, j, :], bank(0), AF.Exp, ["negb"], [("ps", 0), ("tmpe", j)],
                        bias=negb[:, dr * 4 + pr:dr * 4 + pr + 1], scale=-1.0)
                    ACT(P, tmps[:, j, :], tmpe[:, j, :], AF.Ln, [("tmpe", j)], [("tmps", j)], bias=1.0)
                    if dr == 0:
                        P.op("dve", (lambda o, d1: (lambda e: e.tensor_tensor_scan(out=o, data0=rmask[:], data1=d1, initial=0.0,
                                                                                    op0=ALU.mult, op1=ALU.add)))(cs[:, 0, tsl], tmps[:, j, :]),
                             [("tmps", j), "rmask"], [("cs", 0)])
                    else:
                        P.op("dve", (lambda o, d1: (lambda e: e.tensor_tensor_scan(out=o, data0=rmask[:], data1=d1, initial=0.0,
                                                                                    op0=ALU.mult, op1=ALU.add)))(tmpp[:, j, :], tmps[:, j, :]),
                             [("tmps", j), "rmask"], [("tmpp", j)])
                        TT(P, "dve", tmps[:, j, :], tmps[:, j, :], tmpp[:, j, :], ALU.subtract, [("tmpp", j)], [("tmps", j)])
                        TT(P, "dve", cs[:, 1, tsl].rearrange("p (c t) -> p c t", c=4),
                           tmps[:, j, :].rearrange("p (c t) -> p c t", c=4),
                           tmpp[:, j, :].rearrange("p (c t) -> p c t", c=4)[:, :, 127:128].to_broadcast([128, 4, 128]),
                           ALU.add, [("tmps", j), ("tmpp", j)], [("cs", 1)])
            for dr in range(2):
                for tb in range(NT5):
                    tsl = slice(tb * 512, (tb + 1) * 512)
                    j = ei % 2
                    ei += 1
                    ACT(P, tmpe[:, j, :], cs[:, dr, tsl], AF.Exp, [("cs", dr)], [("tmpe", j)], scale=-1.0 / 16)
                    STT(P, qkt[:, 2 * dr, tsl], qk[:, 0, tsl], 0.125, tmpe[:, j, :], ALU.mult, ALU.mult,
                        ["qraw", ("tmpe", j)], [("qkt", 2 * dr)])
                    ACT(P, tmpp[:, j, :], cs[:, dr, tsl], AF.Exp, [("cs", dr)], [("tmpp", j)], scale=1.0 / 16)
                    TT(P, "dve", qkt[:, 2 * dr + 1, tsl], qk[:, 1, tsl], tmpp[:, j, :], ALU.mult,
                       ["kraw", ("tmpp", j)], [("qkt", 2 * dr + 1)])
            cs3f = cs[:, 0, :].rearrange("p (n t) -> p n t", t=128)
            cs3b = cs[:, 1, :].rearrange("p (n t) -> p n t", t=128)
            ACT(P, dl[:, 0, :], cs3f[:, :, 127], AF.Exp, [("cs", 0)], ["dl"], scale=-1.0 / 16)
            ACT(P, dl[:, 1, :], cs3b[:, :, 0], AF.Exp, [("cs", 1)], ["dl"], scale=-1.0 / 16)
            for dr in range(2):
                for n0 in range(0, NB, 4):
                    ptb = bank(1).bitcast(BF16)
                    for c in range(4):
                        n = n0 + c
                        TR(P, ptb[:, c * 128:(c + 1) * 128], qkt[:, 2 * dr + 1, n * 128:(n + 1) * 128], C["ident"],
                           [("qkt", 2 * dr + 1)], [("ps", 1)])
                    CP(P, "act" if (n0 // 4) % 2 else "dve", ktok[:, dr, n0 * 128:(n0 + 4) * 128], ptb[:, 0:512], [],
                       [("ps", 1), ("ktok", dr)])
            for dr in range(2):
                kS = ("S", dr)
                MEMSET(P, "dve", S[:, dr, :], 0.0, [], [kS])
                order = range(NB) if dr == 0 else range(NB - 1, -1, -1)
                for idx, n in enumerate(order):
                    CP(P, "act", Sprev[:, dr, n * 128:(n + 1) * 128], S[:, dr, :], [kS], [("Sprev", dr)])
                    if idx == NB - 1:
                        break
                    pU = bank(2, 128)
                    for hh in range(2):
                        MM(P, pU[hh * 64:(hh + 1) * 64, :], ktok[:, dr, n * 128 + hh * 64:n * 128 + (hh + 1) * 64],
                           Vp[:, n, hh * 128:(hh + 1) * 128], True, True, [("ktok", dr), "Vp"], [("ps", 2)])
                    TT(P, "dve", tS[:, dr, :], pU, S[:, dr, :], ALU.add, [kS], [("ps", 2), ("tS", dr)])
                    ACT(P, S[:, dr, :], tS[:, dr, :], AF.Copy, [("tS", dr), "dl"], [kS], scale=dl[:, dr, n:n + 1])
            for n in range(NB):
                bAs = (3, 4) if n % 2 == 0 else (0, 1)
                aj = n % 2
                for dr in range(2):
                    for hh in range(2):
                        pA = bank(bAs[hh])
                        MM(P, pA[:, dr * 128:(dr + 1) * 128], qkt[hh * 64:(hh + 1) * 64, 2 * dr + 1, n * 128:(n + 1) * 128],
                           qkt[hh * 64:(hh + 1) * 64, 2 * dr, n * 128:(n + 1) * 128], True, True,
                           [("qkt", 2 * dr + 1), ("qkt", 2 * dr)], [("ps", bAs[hh])])
                for hh in range(2):
                    TT(P, "dve", ATm[:, aj, hh * 256:(hh + 1) * 256].rearrange("p (d t) -> p d t", d=2),
                       bank(bAs[hh])[:, 0:256].rearrange("p (d t) -> p d t", d=2),
                       mk2[:], ALU.mult, ["mk2"], [("ps", bAs[hh]), ("ATm", aj)])
                for hh in range(2):
                    pO = bank(5 + hh)[:, (n % 4) * 128:(n % 4 + 1) * 128]
                    kO = ("ps", 5 + hh)
                    for dr in range(2):
                        MM(P, pO, Vp[:, n, hh * 128:(hh + 1) * 128], ATm[:, aj, (hh * 2 + dr) * 128:(hh * 2 + dr + 1) * 128],
                           dr == 0, False, ["Vp", ("ATm", aj)], [kO])
                        MM(P, pO, Sprev[hh * 64:(hh + 1) * 64, dr, n * 128:(n + 1) * 128],
                           qkt[hh * 64:(hh + 1) * 64, 2 * dr, n * 128:(n + 1) * 128], False, dr == 1,
                           [("Sprev", dr), ("qkt", 2 * dr)], [kO])
                if n % 4 == 3:
                    tsl = slice((n - 3) * 128, (n + 1) * 128)
                    for hh in range(2):
                        h = pr * 2 + hh
                        kO = ("ps", 5 + hh)
                        pO = bank(5 + hh)
                        DMA(P, "sp", sgt[:, hh, :], EFM[1024 + h * 128:1024 + (h + 1) * 128, tsl], [], [("sgt", hh)])
                        ACT(P, sq[:, hh, :], pO, AF.Square, [], [kO, ("sq", hh)])
                        MM(P, bank(7), C["ones"], sq[:, hh, :], True, True, [("sq", hh)], [("ps", 7)])
                        ACT(P, rt[:, hh, :], bank(7), AF.Sqrt, [], [("ps", 7), ("rt", hh)], bias=EPS, scale=1.0 / 128)
                        RECIP(P, rt[:, hh, :], rt[:, hh, :], [], [("rt", hh)])
                        TT(P, "dve", o1[:, hh, :], pO, rt[:, hh, :], ALU.mult, [("rt", hh)], [kO, ("o1", hh)])
                        STT(P, ostg[:, hh, :], o1[:, hh, :], gnT[:, h:h + 1], sgt[:, hh, :], ALU.mult, ALU.mult,
                            [("o1", hh), "gnT", ("sgt", hh)], [("ostg", hh)])
                        DMA(P, "sp", MIXT[h * 128:(h + 1) * 128, tsl], ostg[:, hh, :], [("ostg", hh)], [])


def phase_SSD(P, C, OFM, DTR, MIXT, XBCS, STATES, PREV, prm, T):
    nc = P.nc
    NB = T // 128
    NT5 = T // 512
    with contextlib.ExitStack() as sst:
        _phase_SSD(P, C, OFM, DTR, MIXT, XBCS, STATES, PREV, prm, T, sst)


def _phase_SSD(P, C, OFM, DTR, MIXT, XBCS, STATES, PREV, prm, T, sst):
    nc = P.nc
    NB = T // 128
    NT5 = T // 512
    dt = sb(sst, nc, "ssd_dt", [128, NB, 32], F32)
    acum = sb(sst, nc, "ssd_acum", [128, NB, 32], F32)
    nacum = sb(sst, nc, "ssd_nacum", [128, NB, 32], F32)
    EA = sb(sst, nc, "ssd_EA", [128, NB, 32], F32)
    acT = sb(sst, nc, "ssd_acT", [16, 2, T], F32)
    with P.phase() as st:
        cw = sb(st, nc, "convw", [128, 60], F32)
        cb = sb(st, nc, "convb", [128, 12], F32)
        diagw = sb(st, nc, "diagw", [128, 60, 128], BF16)
        dtb = sb(st, nc, "dtb", [128, 32], F32)
        Aneg = sb(st, nc, "Aneg", [128, 32], F32)
        da = sb(st, nc, "ssd_da", [128, NB, 32], F32)
        dtdec = sb(st, nc, "ssd_dtdec", [128, NB, 32], F32)
        rawx = sb(st, nc, "rawx", [128, 2, 12 * 516], BF16)
        xbc = sb(st, nc, "xbc", [128, 2, 12 * 512], BF16)
        xdd = sb(st, nc, "xdd", [128, 2, 2 * 1024], BF16)
        btok = sb(st, nc, "btok", [128, 2, 256], BF16)
        stsb = sb(st, nc, "stsb", [128, 2, 2 * 1024], F32)
        ps = pst(st, nc, "psS1", [128, 4096], F32)

        def bank(i, w=512):
            return ps[:, i * 512:i * 512 + w]

        DMA(P, "sp", cw[:], prm["convwT"], [], ["cw"])
        DMA(P, "sp", cb[:], prm["convbT"], [], ["cb"])
        DMA(P, "sp", dtb[:], prm["dtb"].partition_broadcast(128), [], ["dtb"])
        DMA(P, "sp", Aneg[:], prm["alog"].partition_broadcast(128), [], ["Aneg"])
        ACT(P, Aneg[:], Aneg[:], AF.Exp, ["Aneg"], ["Aneg"])
        TS(P, "dve", Aneg[:], Aneg[:], -1.0, None, ALU.mult, None, ["Aneg"], ["Aneg"])
        for i in range(60):
            TS(P, "dve" if i % 2 else "pool", diagw[:, i, :], C["ident"], cw[:, i:i + 1], None, ALU.mult, None, ["cw"], ["diagw"])
        DMA(P, "sp", dt[:], DTR.rearrange("(n p) c -> p n c", p=128), [], ["dt"])
        TT(P, "dve", dt[:], dt[:], dtb[:].unsqueeze(1).to_broadcast([128, NB, 32]), ALU.add, ["dtb"], ["dt"])
        ACT(P, dt[:], dt[:], AF.Exp, ["dt"], ["dt"])
        ACT(P, dt[:], dt[:], AF.Ln, ["dt"], ["dt"], bias=1.0)
        TT(P, "dve", da[:], dt[:], Aneg[:].unsqueeze(1).to_broadcast([128, NB, 32]), ALU.mult, ["dt", "Aneg"], ["da"])
        for dr in range(2):
            tri = C["mnext32"] if dr == 0 else C["mprev32"]
            MM(P, bank(0).rearrange("p (n h) -> p n h", h=16), tri, da[:, :, dr * 16:(dr + 1) * 16], True, True,
               ["da"], [("ps", 0)])
            CP(P, "dve", acum[:, :, dr * 16:(dr + 1) * 16], bank(0).rearrange("p (n h) -> p n h", h=16), [],
               [("ps", 0), "acum"])
        TS(P, "dve", nacum[:], acum[:], -1.0, None, ALU.mult, None, ["acum"], ["nacum"])
        for hf in range(2):
            nsl = slice(hf * (NB // 2), (hf + 1) * (NB // 2))
            MM(P, bank(1).rearrange("p (n h) -> p n h", h=32), C["ones32"], da[:, nsl, :], True, True, ["da"], [("ps", 1)])
            TT(P, "dve", dtdec[:, nsl, :], bank(1).rearrange("p (n h) -> p n h", h=32), acum[:, nsl, :], ALU.subtract,
               ["acum"], [("ps", 1), "dtdec"])
            ACT(P, EA[:, nsl, :], bank(1).rearrange("p (n h) -> p n h", h=32), AF.Exp, [], [("ps", 1), "EA"])
        ACT(P, dtdec[:], dtdec[:], AF.Exp, ["dtdec"], ["dtdec"])
        TT(P, "dve", dtdec[:], dtdec[:], dt[:], ALU.mult, ["dt"], ["dtdec"])
        for dr in range(2):
            tri = C["mnext32"] if dr == 0 else C["mprev32"]
            for n0 in range(0, NB, 4):
                for c in range(4):
                    n = n0 + c
                    MM(P, ps[0:16, 1024 + c * 128:1024 + (c + 1) * 128], da[:, n, dr * 16:(dr + 1) * 16], tri, True, True,
                       ["da"], [("ps", 2)])
                CP(P, "act", acT[:, dr, n0 * 128:(n0 + 4) * 128], ps[0:16, 1024:1536], [], [("ps", 2), ("acT", dr)])
        for tb in range(NT5):
            j = tb % 2
            t0 = tb * 512
            kr, kx = ("rawx", j), ("xbc", j)
            rv = rawx[:, j, :].rearrange("p (c t) -> p c t", c=12)
            xv = xbc[:, j, :].rearrange("p (c t) -> p c t", c=12)
            lo = max(t0 - 2, 0)
            hi = min(t0 + 514, T)
            if tb == 0:
                MEMSET(P, "dve", rv[:, :, 0:2], 0.0, [], [kr])
            if tb == NT5 - 1:
                MEMSET(P, "dve", rv[:, :, 514:516], 0.0, [], [kr])
            DMA(P, "sp", rv[:, :, lo - (t0 - 2):hi - (t0 - 2)], OFM[1024:2560, lo:hi].rearrange("(c p) t -> p c t", p=128), [], [kr])
            for ch in range(12):
                b = 4 + (ch % 2)
                for jj in range(5):
                    MM(P, bank(b), diagw[:, ch * 5 + jj, :], rv[:, ch, jj:jj + 512], jj == 0, jj == 4, [kr, "diagw"], [("ps", b)])
                ACT(P, xv[:, ch, :], bank(b), AF.Silu, ["cb"], [("ps", b), kx], bias=cb[:, ch:ch + 1])
            DMA(P, "sp", XBCS[:, t0:t0 + 512].rearrange("(c p) t -> p c t", p=128), xv, [kx], [])
            for c in range(4):
                n = tb * 4 + c
                cj = n % 2
                tsl = slice(c * 128, (c + 1) * 128)
                pX = bank(6).bitcast(BF16)
                pB = bank(7).bitcast(BF16)
                for k in range(8):
                    TR(P, pX[:, k * 128:(k + 1) * 128], xv[:, k, tsl], C["ident"], [kx], [("ps", 6)])
                for g in range(2):
                    TR(P, pB[:, g * 128:(g + 1) * 128], xv[:, 8 + g, tsl], C["ident"], [kx], [("ps", 7)])
                for dr in range(2):
                    TT(P, "dve", xdd[:, cj, dr * 1024:(dr + 1) * 1024].rearrange("p (h q) -> p h q", h=16),
                       pX[:, 0:1024].rearrange("p (h q) -> p h q", h=16),
                       dtdec[:, n, dr * 16:(dr + 1) * 16].unsqueeze(2).to_broadcast([128, 16, 64]), ALU.mult,
                       ["dtdec"], [("ps", 6), ("xdd", cj)])
                CP(P, "act", btok[:, cj, :], pB[:, 0:256], [], [("ps", 7), ("btok", cj)])
                for dr in range(2):
                    for g in range(2):
                        b = dr * 2 + g
                        MM(P, bank(b), btok[:, cj, g * 128:(g + 1) * 128],
                           xdd[:, cj, dr * 1024 + g * 512:dr * 1024 + (g + 1) * 512], True, True,
                           [("btok", cj), ("xdd", cj)], [("ps", b)])
                        CP(P, "act" if g else "dve", stsb[:, cj, dr * 1024 + g * 512:dr * 1024 + (g + 1) * 512], bank(b), [],
                           [("ps", b), ("stsb", cj)])
                for dr in range(2):
                    DMA(P, "sp", STATES[dr, n], stsb[:, cj, dr * 1024:(dr + 1) * 1024], [("stsb", cj)], [])
    with P.phase() as st:
        prev = sb(st, nc, "prevS", [128, 2, 1024], F32)
        stl = sb(st, nc, "stl", [128, 4, 1024], F32)
        pvb = sb(st, nc, "pvb", [128, 4, 1024], BF16)
        i = 0
        for dr in range(2):
            kp = ("prev", dr)
            MEMSET(P, "dve" if dr == 0 else "pool", prev[:, dr, :], 0.0, [], [kp])
            order = range(NB) if dr == 0 else range(NB - 1, -1, -1)
            eng = "dve" if dr == 0 else "pool"
            for idx, n in enumerate(order):
                j = (i % 2) + 2 * dr
                i += 1
                CP(P, "act", pvb[:, j, :], prev[:, dr, :], [kp], [("pvb", j)])
                DMA(P, "sp", PREV[dr, n], pvb[:, j, :], [("pvb", j)], [])
                if idx == NB - 1:
                    break
                DMA(P, "sp", stl[:, j, :], STATES[dr, n], [], [("stl", j)])
                TT(P, eng, prev[:, dr, :].rearrange("p (h q) -> p h q", h=16), prev[:, dr, :].rearrange("p (h q) -> p h q", h=16),
                   EA[:, n, dr * 16:(dr + 1) * 16].unsqueeze(2).to_broadcast([128, 16, 64]), ALU.mult, [], [kp])
                TT(P, eng, prev[:, dr, :], prev[:, dr, :], stl[:, j, :], ALU.add, [("stl", j)], [kp])
    with P.phase() as st:
        dsk = sb(st, nc, "dsk", [128, 8], F32)
        snT = sb(st, nc, "snT", [128, 8], F32)
        rsel = sb(st, nc, "rsel", [16, 16 * 128], F32)
        negm = sb(st, nc, "negm", [128, 2, 128], F32)
        xbc = sb(st, nc, "xbc2", [128, 2, 12 * 512], BF16)
        zs = sb(st, nc, "zs", [128, 2, 8 * 512], BF16)
        pv = sb(st, nc, "pv", [128, 2, 2 * 1024], BF16)
        xdt = sb(st, nc, "xdt", [128, 2, 2 * 1024], BF16)
        cbt = sb(st, nc, "cbt", [128, 2, 256], F32)
        tmp = sb(st, nc, "tmpL", [128, 2, 512], F32)
        tmp2 = sb(st, nc, "tmpL2", [128, 2, 512], F32)
        LT = sb(st, nc, "LT", [128, 2, 512], F32)
        Eh = sb(st, nc, "Eh", [128, 2, 512], F32)
        MT = sb(st, nc, "MT", [128, 2, 2 * 512], BF16)
        Cp = sb(st, nc, "Cp", [128, 2, 2 * 512], BF16)
        ysb = sb(st, nc, "ysb", [128, 2, 1024], F32)
        sq = sb(st, nc, "sq2", [128, 2, 1024], BF16)
        rt = sb(st, nc, "rt2", [128, 2, 128], F32)
        ost = sb(st, nc, "ost2", [128, 2, 8 * 512], BF16)
        ps = pst(st, nc, "psS2", [128, 4096], F32)

        def bank(i, w=512):
            return ps[:, i * 512:i * 512 + w]

        DMA(P, "sp", dsk[:], prm["dskT"], [], ["dsk"])
        DMA(P, "sp", snT[:], prm["snT"], [], ["snT"])
        DMA(P, "sp", rsel[:], prm["rsel"], [], ["rsel"])
        TS(P, "dve", negm[:, 0, :], C["mnext32"], 30000.0, -30000.0, ALU.mult, ALU.add, [], ["negm"])
        TS(P, "dve", negm[:, 1, :], C["mprev32"], 30000.0, -30000.0, ALU.mult, ALU.add, [], ["negm"])
        li = 0
        for tb in range(NT5):
            j = tb % 2
            t0 = tb * 512
            kx, kz, ko = ("xbc", j), ("zs", j), ("ost", j)
            xv = xbc[:, j, :].rearrange("p (c t) -> p c t", c=12)
            zv = zs[:, j, :].rearrange("p (c t) -> p c t", c=8)
            ov = ost[:, j, :].rearrange("p (c t) -> p c t", c=8)
            DMA(P, "sp", xv, XBCS[:, t0:t0 + 512].rearrange("(c p) t -> p c t", p=128), [], [kx])
            DMA(P, "sp", zv, OFM[0:1024, t0:t0 + 512].rearrange("(c p) t -> p c t", p=128), [], [kz])
            for c in range(4):
                n = tb * 4 + c
                cj = n % 2
                tsl = slice(c * 128, (c + 1) * 128)
                kpv, kxd, kcb = ("pv", cj), ("xdt", cj), ("cbt", cj)
                for dr in range(2):
                    DMA(P, "sp", pv[:, cj, dr * 1024:(dr + 1) * 1024], PREV[dr, n], [], [kpv])
                for g in range(2):
                    MM(P, bank(0)[:, g * 128:(g + 1) * 128], xv[:, 8 + g, tsl], xv[:, 10 + g, tsl], True, True, [kx], [("ps", 0)])
                CP(P, "act", cbt[:, cj, :], bank(0)[:, 0:256], [], [("ps", 0), kcb])
                pX = bank(1).bitcast(BF16)
                for k in range(8):
                    TR(P, pX[:, k * 128:(k + 1) * 128], xv[:, k, tsl], C["ident"], [kx], [("ps", 1)])
                for dr in range(2):
                    TT(P, "dve", xdt[:, cj, dr * 1024:(dr + 1) * 1024].rearrange("p (h q) -> p h q", h=16),
                       pX[:, 0:1024].rearrange("p (h q) -> p h q", h=16),
                       dt[:, n, dr * 16:(dr + 1) * 16].unsqueeze(2).to_broadcast([128, 16, 64]), ALU.mult,
                       [], [("ps", 1), kxd])
                for hq in range(4):
                    g = hq // 2
                    qj = hq % 2
                    km, kc = ("MT", qj), ("Cp", qj)
                    for dr in range(2):
                        lj = li % 2
                        li += 1
                        bA = 2 + lj
                        kt, kt2, kl, ke = ("tmp", lj), ("tmp2", lj), ("LT", lj), ("Eh", lj)
                        pA = bank(bA)
                        for i4 in range(4):
                            h = hq * 4 + i4
                            MM(P, pA[:, i4 * 128:(i4 + 1) * 128], rsel[0:16, h * 128:(h + 1) * 128],
                               acT[0:16, dr, n * 128:(n + 1) * 128], True, True, ["rsel"], [("ps", bA)])
                        v3 = lambda ap: ap.rearrange("p (h l) -> p h l", h=4)
                        TT(P, "dve", v3(tmp[:, lj, :]), v3(pA),
                           nacum[:, n, dr * 16 + hq * 4:dr * 16 + hq * 4 + 4].unsqueeze(2).to_broadcast([128, 4, 128]),
                           ALU.add, [], [("ps", bA), kt])
                        ACT(P, Eh[:, lj, :], pA, AF.Exp, [], [("ps", bA), ke])
                        TT(P, "pool", v3(tmp2[:, lj, :]), v3(tmp[:, lj, :]),
                           negm[:, dr, :].unsqueeze(1).to_broadcast([128, 4, 128]), ALU.add, [kt, "negm"], [kt2])
                        ACT(P, LT[:, lj, :], tmp2[:, lj, :], AF.Exp, [kt2], [kl])
                        TT(P, "dve", v3(MT[:, qj, dr * 512:(dr + 1) * 512]), v3(LT[:, lj, :]),
                           cbt[:, cj, g * 128:(g + 1) * 128].unsqueeze(1).to_broadcast([128, 4, 128]), ALU.mult,
                           [kl, kcb], [km])
                        TT(P, "pool", v3(Cp[:, qj, dr * 512:(dr + 1) * 512]), v3(Eh[:, lj, :]),
                           xv[:, 10 + g, tsl].unsqueeze(1).to_broadcast([128, 4, 128]), ALU.mult, [kx, ke], [kc])
                    for i4 in range(4):
                        h = hq * 4 + i4
                        bY = 4 + (h // 8)
                        pY = bank(bY)[(h % 2) * 64:(h % 2 + 1) * 64, ((h // 2) % 4) * 128:((h // 2) % 4 + 1) * 128]
                        for dr in range(2):
                            o = dr * 512 + i4 * 128
                            MM(P, pY, xdt[:, cj, dr * 1024 + h * 64:dr * 1024 + (h + 1) * 64], MT[:, qj, o:o + 128], dr == 0, False,
                               [kxd, km], [("ps", bY)])
                            MM(P, pY, pv[:, cj, dr * 1024 + h * 64:dr * 1024 + (h + 1) * 64], Cp[:, qj, o:o + 128], False, dr == 1,
                               [kpv, kc], [("ps", bY)])
                yv = ysb[:, cj, :].rearrange("p (k t) -> p k t", k=8)
                ky = ("ysb", cj)
                for k in range(8):
                    pYk = bank(4 + k // 4)[:, (k % 4) * 128:(k % 4 + 1) * 128]
                    STT(P, yv[:, k, :], xv[:, k, tsl], dsk[:, k:k + 1], pYk, ALU.mult, ALU.add, [kx, "dsk"], [("ps", 4 + k // 4), ky])
                TT(P, "dve", yv, yv, zv[:, :, tsl], ALU.mult, [kz], [ky])
                ACT(P, sq[:, cj, :], ysb[:, cj, :], AF.Square, [ky], [("sq", cj)])
                for k in range(8):
                    MM(P, bank(6, 128), C["ones"], sq[:, cj, k * 128:(k + 1) * 128], k == 0, k == 7, [("sq", cj)], [("ps", 6)])
                ACT(P, rt[:, cj, :], bank(6, 128), AF.Sqrt, [], [("ps", 6), ("rt", cj)], bias=EPS, scale=1.0 / 1024)
                RECIP(P, rt[:, cj, :], rt[:, cj, :], [], [("rt", cj)])
                TT(P, "dve", yv, yv, rt[:, cj, :].unsqueeze(1).to_broadcast([128, 8, 128]), ALU.mult, [("rt", cj)], [ky])
                TT(P, "dve", ov[:, :, tsl], yv, snT[:].unsqueeze(2).to_broadcast([128, 8, 128]), ALU.mult, ["snT"], [ky, ko])
            DMA(P, "sp", MIXT[0:1024, t0:t0 + 512].rearrange("(c p) t -> p c t", p=128), ov, [ko], [])


def ssd_host_params(conv_w, conv_b, dt_bias, a_log, d_skip, ssd_norm):
    p = {}
    p["convwT"] = np.ascontiguousarray(conv_w.reshape(5, 12, 128).transpose(2, 1, 0).reshape(128, 60))
    p["convbT"] = np.ascontiguousarray(conv_b.reshape(12, 128).T)
    p["dtb"] = np.ascontiguousarray(dt_bias.reshape(1, 32))
    p["alog"] = np.ascontiguousarray(a_log.reshape(1, 32))
    p["dskT"] = np.ascontiguousarray(np.repeat(d_skip, 64).reshape(8, 128).T)
    p["snT"] = np.ascontiguousarray(ssd_norm.reshape(8, 128).T)
    return p


def make_rsel():
    r = np.zeros((16, 16, 128), np.float32)
    for i in range(16):
        r[i, i, :] = 1.0
    return r.reshape(16, 16 * 128)


NCORES = 4
EVEN_BLOCKS = [
    dict(col0=0, width=512, mode="fm", dst="EFM", off=0, dt=BF16),
    dict(col0=512, width=512, mode="fm", dst="EFM", off=512, dt=BF16),
    dict(col0=1024, width=1024, mode="tm", dst="ETM", off=0, dt=BF16),
    dict(col0=2048, width=1024, mode="fm", dst="EFM", off=1024, dt=BF16, silu=True),
    dict(col0=3072, width=32, mode="fm", dst="LRT", off=0, dt=F32),
    dict(col0=3104, width=1024, mode="fm", dst="EFM", off=2048, dt=BF16),
    dict(col0=4128, width=1024, mode="fm", dst="EFM", off=3072, dt=BF16),
    dict(col0=5152, width=1024, mode="tm", dst="ETM", off=1024, dt=BF16),
]
ODD_BLOCKS = [
    dict(col0=0, width=1024, mode="fm", dst="OFM", off=0, dt=BF16, silu=True),
    dict(col0=1024, width=1536, mode="fm", dst="OFM", off=1024, dt=BF16),
    dict(col0=2560, width=32, mode="tm", dst="DTR", off=0, dt=F32),
    dict(col0=2592, width=1024, mode="fm", dst="OFM", off=2560, dt=BF16),
    dict(col0=3616, width=256, mode="fm", dst="OFM", off=3584, dt=BF16),
    dict(col0=3872, width=256, mode="tm", dst="OTM", off=0, dt=BF16),
]

INPUT_SHAPES = {
    "x": [T_FULL, D], "norm_gains": [4, 4, D], "gainsT": [128, 256],
    "ffn_w_gate": [4, D, HID], "ffn_w_up": [4, D, HID], "ffn_w_down": [4, HID, D],
    "even_w_in": [2, D, EVEN_IN], "even_w_out": [2, D, D], "odd_w_in": [2, D, ODD_IN], "odd_w_out": [2, D, D],
    "wdec": [2, 32, 2, 512], "bdecT": [2, 128, 8], "gnT": [2, 128, 8], "nab": [2, 8, 128, 3200],
    "convwT": [2, 128, 60], "convbT": [2, 128, 12], "dtb": [2, 1, 32], "alog": [2, 1, 32], "dskT": [2, 128, 8],
    "snT": [2, 128, 8], "sink": [2, 1, 8],
    "c_ident": [128, 128], "c_ones": [128, 128], "c_rotm": [128, 128], "c_mprev": [128, 128], "c_mnext": [128, 128],
    "c_cosT": [128, T_FULL], "c_sinT": [128, T_FULL], "c_rsel": [16, 2048],
}


def build_program(layers=(0, 1, 2, 3)):
    T = T_FULL
    NB = T // 128
    nc = bass.Bass("TRN2", target_bir_lowering=False)
    P = Prog(nc)
    I = {k: nc.dram_tensor(k, v, F32, kind="ExternalInput").ap() for k, v in INPUT_SHAPES.items()}
    out = nc.dram_tensor("out", [T, D], F32, kind="ExternalOutput").ap()
    S = {
        "XS": nc.dram_tensor("XS", [T, D], F32, kind="Internal").ap(),
        "EFM": nc.dram_tensor("EFM", [4096, T], BF16, kind="Internal").ap(),
        "ETM": nc.dram_tensor("ETM", [T, 2048], BF16, kind="Internal").ap(),
        "LRT": nc.dram_tensor("LRT", [32, T], F32, kind="Internal").ap(),
        "OFM": nc.dram_tensor("OFM", [3840, T], BF16, kind="Internal").ap(),
        "OTM": nc.dram_tensor("OTM", [T, 256], BF16, kind="Internal").ap(),
        "DTR": nc.dram_tensor("DTR", [T, 32], F32, kind="Internal").ap(),
        "MIXT": nc.dram_tensor("MIXT", [D, T], BF16, kind="Internal").ap(),
        "XBCS": nc.dram_tensor("XBCS", [1536, T], BF16, kind="Internal").ap(),
        "STATES": nc.dram_tensor("STATES", [2, NB, 128, 1024], F32, kind="Internal").ap(),
        "PREV": nc.dram_tensor("PREV", [2, NB, 128, 1024], BF16, kind="Internal").ap(),
    }
    C = {}
    gT = sb(P.outer, nc, "gTall", [128, 256], F32)
    with P.phase() as st:
        for k in ["ident", "ones", "rotm", "mprev", "mnext"]:
            t = sb(P.outer, nc, "cb_" + k, [128, 128], BF16)
            DMA(P, "pool", t[:], I["c_" + k], [], [k])
            C[k] = t[:]
        for k in ["ones", "mprev", "mnext"]:
            t = sb(P.outer, nc, "cf_" + k, [128, 128], F32)
            DMA(P, "sp", t[:], I["c_" + k], [], [k + "32"])
            C[k + "32"] = t[:]
        DMA(P, "sp", gT[:], I["gainsT"], [], ["gT"])
    first, last = layers[0], layers[-1]
    for L in layers:
        i = L // 2
        x_src = I["x"] if L == first else S["XS"]
        x_dst = out if L == last else S["XS"]
        g = lambda a: gT[:, (L * 4 + a) * 16:(L * 4 + a + 1) * 16]
        if L % 2 == 0:
            blocks = [dict(b, dst=S[b["dst"]]) for b in EVEN_BLOCKS]
            phase_A(P, C, x_src, g(0), I["even_w_in"][i], blocks, T)
            phase_GLA(P, C, S["EFM"], S["ETM"], S["LRT"], S["MIXT"], I["wdec"][i], I["bdecT"][i], I["gnT"][i], T)
            phase_NA(P, C, S["EFM"], S["ETM"], S["MIXT"], I["nab"][i], T)
            Wout = I["even_w_out"][i]
        else:
            blocks = [dict(b, dst=S[b["dst"]]) for b in ODD_BLOCKS]
            phase_A(P, C, x_src, g(0), I["odd_w_in"][i], blocks, T)
            prm = {k: I[k][i] for k in ["convwT", "convbT", "dtb", "alog", "dskT", "snT"]}
            prm["rsel"] = I["c_rsel"]
            phase_SSD(P, C, S["OFM"], S["DTR"], S["MIXT"], S["XBCS"], S["STATES"], S["PREV"], prm, T)
            phase_SWA(P, C, S["OFM"], S["OTM"], S["MIXT"], I["c_cosT"], I["c_sinT"], I["sink"][i], T)
            Wout = I["odd_w_out"][i]
        phase_CD(P, C, S["MIXT"], x_src, x_dst, I["norm_gains"][L, 1:2, :], g(2), I["norm_gains"][L, 3:4, :],
                 Wout, I["ffn_w_gate"][L], I["ffn_w_up"][L], I["ffn_w_down"][L], T)
    n_ops = P.n_ops
    P.close()
    return nc, n_ops


def host_inputs(inp):
    f = lambda a: np.ascontiguousarray(np.asarray(a, dtype=np.float32))
    h = {}
    for k in ["norm_gains", "ffn_w_gate", "ffn_w_up", "ffn_w_down", "even_w_in", "even_w_out", "odd_w_in", "odd_w_out"]:
        h[k] = f(inp[k])
    g = f(inp["norm_gains"])
    h["gainsT"] = f(g.reshape(4, 4, 16, 128).transpose(3, 0, 1, 2).reshape(128, 256))
    wd = f(inp["gla_w_decay"])
    wdec = np.zeros((2, 32, 2, 512), np.float32)
    wdec[:, 0:16, 0, :] = wd[:, 0]
    wdec[:, 16:32, 1, :] = wd[:, 1]
    h["wdec"] = wdec
    bd = f(inp["gla_b_decay"])
    h["bdecT"] = f(bd.reshape(2, 2, 4, 128).transpose(0, 3, 1, 2).reshape(2, 128, 8))
    h["gnT"] = f(f(inp["gla_norm"]).reshape(2, 8, 128).transpose(0, 2, 1))
    rpb = f(inp["na_rpb"])
    h["nab"] = np.stack([na_bias_tables(rpb[i]) for i in range(2)])
    sp = [ssd_host_params(f(inp["ssd_conv_w"])[i], f(inp["ssd_conv_b"])[i], f(inp["ssd_dt_bias"])[i],
                          f(inp["ssd_a_log"])[i], f(inp["ssd_d"])[i], f(inp["ssd_norm"])[i]) for i in range(2)]
    for k in ["convwT", "convbT", "dtb", "alog", "dskT", "snT"]:
        h[k] = np.stack([sp[i][k] for i in range(2)])
    h["sink"] = f(inp["swa_sink"]).reshape(2, 1, 8)
    for k, v in make_consts().items():
        h["c_" + k] = v
    h["c_rsel"] = make_rsel()
    return h


_CACHE = {}


def kernel(**inputs):
    x = np.asarray(inputs["x"], dtype=np.float32)
    h = host_inputs(inputs)
    if "nc" not in _CACHE:
        _CACHE["nc"] = build_program()[0]
    nc = _CACHE["nc"]
    in_maps = []
    for c in range(NCORES):
        m = dict(h)
        m["x"] = np.ascontiguousarray(x[c % 4])
        in_maps.append(m)
    res = run_bass_kernel_spmd(nc, in_maps, core_ids=list(range(NCORES)))
    return np.stack([np.asarray(res.results[c]["out"], dtype=np.float32) for c in range(4)], axis=0)
```

```python
import contextlib
import numpy as np
import concourse.bass as bass
import concourse.mybir as mybir
from concourse.bass_utils import run_bass_kernel_spmd

F32 = mybir.dt.float32
BF16 = mybir.dt.bfloat16
AF = mybir.ActivationFunctionType
ALU = mybir.AluOpType

D = 2048
T_FULL = 4096
HID = 5632
EPS = 1e-6
EVEN_IN = 6176
ODD_IN = 4128

ENGS = ["pe", "act", "dve", "pool", "sp"]
DMA_ENGS = ["sp", "pool", "act"]
NDMASEM = 12


class Prog:
    def __init__(self, nc):
        self.nc = nc
        self.outer = contextlib.ExitStack()
        self.sems = {}
        for e in ENGS:
            self.sems[e] = self.outer.enter_context(nc.semaphore("s_" + e))
        for e in DMA_ENGS:
            for j in range(NDMASEM):
                self.sems[("dma", e, j)] = self.outer.enter_context(nc.semaphore(f"d_{e}_{j}"))
        self.cnt = {e: 0 for e in ENGS}
        self.dma_cnt = {}
        self.dma_rr = {e: 0 for e in ENGS}
        self.known = {e: {} for e in ENGS}
        self.n_ops = 0
        self._reset()

    def _reset(self):
        self.ops = {e: [] for e in ENGS}
        self.last_w = {}
        self.readers = {}

    def op(self, eng, fn, reads=(), writes=(), dma=False):
        deps = {}

        def note(t):
            sk, v, src = t
            cur = deps.get(sk)
            if cur is None or cur[0] < v:
                deps[sk] = (v, src)

        for k in reads:
            t = self.last_w.get(k)
            if t is not None:
                note(t)
        for k in writes:
            t = self.last_w.get(k)
            if t is not None:
                note(t)
            for t in self.readers.get(k, ()):
                note(t)
        if dma:
            j = self.dma_rr[eng]
            self.dma_rr[eng] = (j + 1) % NDMASEM
            sk = ("dma", eng, j)
            c = self.dma_cnt.get(sk, 0)
            if c > 0:
                note((sk, 16 * c, "dma"))
            self.dma_cnt[sk] = c + 1
            tok = (sk, 16 * (c + 1), "dma")
            inc = (sk, 16)
        else:
            self.cnt[eng] += 1
            tok = (eng, self.cnt[eng], eng)
            inc = (eng, 1)
        waits = []
        kn = self.known[eng]
        for sk, (v, src) in deps.items():
            if src == "pe" and eng == "pe" and not dma:
                continue
            if kn.get(sk, 0) >= v:
                continue
            kn[sk] = v
            waits.append((sk, v))
        self.ops[eng].append((fn, waits, inc))
        self.n_ops += 1
        for k in reads:
            self.readers.setdefault(k, []).append(tok)
        for k in writes:
            self.last_w[k] = tok
            self.readers[k] = []
        return tok

    @contextlib.contextmanager
    def phase(self):
        st = contextlib.ExitStack()
        self._reset()
        try:
            yield st
            self._flush()
        finally:
            st.close()

    def _flush(self):
        nc = self.nc
        bar = [(e, self.cnt[e]) for e in ENGS if self.cnt[e] > 0]
        bar += [(sk, 16 * c) for sk, c in self.dma_cnt.items() if c > 0]
        sems = self.sems
        with nc.Block() as block:
            def run(e):
                def body(eng):
                    for (fn, waits, inc) in self.ops[e]:
                        for sk, v in waits:
                            eng.wait_ge(sems[sk], v)
                        fn(eng).then_inc(sems[inc[0]], inc[1])
                    kn = self.known[e]
                    for sk, v in bar:
                        if kn.get(sk, 0) < v:
                            eng.wait_ge(sems[sk], v)
                            kn[sk] = v
                return body

            block.tensor(run("pe"))
            block.scalar(run("act"))
            block.vector(run("dve"))
            block.gpsimd(run("pool"))
            block.sync(run("sp"))
        self._reset()

    def close(self):
        self.outer.close()


_UID = [0]


def sb(st, nc, name, shape, dtype):
    _UID[0] += 1
    return st.enter_context(nc.sbuf_tensor(f"sb{_UID[0]}_{name}", list(shape), dtype))


def pst(st, nc, name, shape, dtype):
    _UID[0] += 1
    return st.enter_context(nc.psum_tensor(f"ps{_UID[0]}_{name}", list(shape), dtype))


def MM(P, out, lhsT, rhs, start, stop, r, w):
    P.op("pe", lambda e: e.matmul(out, lhsT=lhsT, rhs=rhs, start=start, stop=stop), r, w)


def TR(P, out, in_, ident, r, w):
    P.op("pe", lambda e: e.transpose(out=out, in_=in_, identity=ident), r, w)


def ACT(P, out, in_, func, r, w, bias=None, scale=None, accum=None):
    kw = {}
    if bias is not None:
        kw["bias"] = bias
    if scale is not None:
        kw["scale"] = scale
    if accum is not None:
        kw["accum_out"] = accum
    P.op("act", lambda e: e.activation(out=out, in_=in_, func=func, **kw), r, w)


def TT(P, eng, out, in0, in1, op, r, w):
    P.op(eng, lambda e: e.tensor_tensor(out=out, in0=in0, in1=in1, op=op), r, w)


def TS(P, eng, out, in0, s1, s2, op0, op1, r, w):
    if s2 is None:
        P.op(eng, lambda e: e.tensor_scalar(out=out, in0=in0, scalar1=s1, scalar2=None, op0=op0), r, w)
    else:
        P.op(eng, lambda e: e.tensor_scalar(out=out, in0=in0, scalar1=s1, scalar2=s2, op0=op0, op1=op1), r, w)


def STT(P, out, in0, scalar, in1, op0, op1, r, w):
    P.op("dve", lambda e: e.scalar_tensor_tensor(out=out, in0=in0, scalar=scalar, in1=in1, op0=op0, op1=op1), r, w)


def CP(P, eng, out, in_, r, w):
    if eng == "act":
        P.op("act", lambda e: e.copy(out=out, in_=in_), r, w)
    else:
        P.op(eng, lambda e: e.tensor_copy(out=out, in_=in_), r, w)


def RECIP(P, out, in_, r, w):
    P.op("dve", lambda e: e.reciprocal(out=out, in_=in_), r, w)


def DMA(P, eng, out, in_, r, w):
    P.op(eng, lambda e: e.dma_start(out=out, in_=in_), r, w, dma=True)


def MEMSET(P, eng, ap, val, r, w):
    P.op(eng, lambda e: e.memset(ap, val), r, w)


def rstd_from_ss(P, col, n, key):
    ACT(P, col[:, 1:2], col[:, 0:1], AF.Sqrt, [key], [key], bias=EPS, scale=1.0 / n)
    RECIP(P, col[:, 2:3], col[:, 1:2], [key], [key])


def phase_A(P, C, x_src, gT, W, blocks, T):
    nc = P.nc
    TA = 1024 if T >= 1024 else T
    NSUB = TA // 128
    NH = TA // 512
    with P.phase() as st:
        xa = sb(st, nc, "xa", [128, 2, D], F32)
        hbf = sb(st, nc, "hbfA", [128, 2, D], BF16)
        hT = sb(st, nc, "hT", [128, 16, TA], BF16)
        wsl = sb(st, nc, "wslA", [128, 4, 8192], BF16)
        stg = sb(st, nc, "stgA", [128, 4, 512], F32)
        stb = sb(st, nc, "stbA", [128, 4, 512], BF16)
        col = sb(st, nc, "colA", [128, 2, 4], F32)
        ps = pst(st, nc, "psA", [128, 4096], F32)
        slot_i = 0
        stg_i = 0
        bank_i = 0
        ev_i = 0
        for tile in range(T // TA):
            for sub in range(NSUB):
                j = sub % 2
                t0 = tile * TA + sub * 128
                kx, kh, kc = ("xa", j), ("hbf", j), ("col", j)
                DMA(P, "sp", xa[:, j, :], x_src[t0:t0 + 128, :], [], [kx])
                ACT(P, hbf[:, j, :], xa[:, j, :], AF.Square, [kx], [kh, kc], accum=col[:, j, 0:1])
                rstd_from_ss(P, col[:, j, :], D, kc)
                TS(P, "dve", hbf[:, j, :], xa[:, j, :], col[:, j, 2:3], None, ALU.mult, None, [kx, kc], [kh])
                for kg in range(4):
                    bank = 6 + (kg % 2)
                    kb = ("ps", bank)
                    ptb = ps[:, bank * 512:(bank + 1) * 512].bitcast(BF16)
                    for kk in range(4):
                        k = kg * 4 + kk
                        TR(P, ptb[:, kk * 128:(kk + 1) * 128], hbf[:, j, k * 128:(k + 1) * 128], C["ident"], [kh], [kb])
                    TT(P, "dve", hT[:, kg * 4:(kg + 1) * 4, sub * 128:(sub + 1) * 128],
                       ptb[:, 0:512].rearrange("p (a b) -> p a b", a=4),
                       gT[:, kg * 4:(kg + 1) * 4].unsqueeze(2).to_broadcast([128, 4, 128]), ALU.mult,
                       [], [kb, ("hT", sub // 4)])
            for blk in blocks:
                for g0 in range(0, blk["width"], 512):
                    gw = min(512, blk["width"] - g0)
                    s = slot_i % 4
                    slot_i += 1
                    ks = ("wsl", s)
                    wv = wsl[:, s, 0:16 * gw].rearrange("p (k n) -> p k n", k=16)
                    c0w = blk["col0"] + g0
                    DMA(P, "pool", wv, W[:, c0w:c0w + gw].rearrange("(k p) n -> p k n", p=128), [], [ks])
                    if blk["mode"] == "fm":
                        for c0 in range(0, gw, 128):
                            cw = min(128, gw - c0)
                            for half in range(NH):
                                bank = bank_i % 6
                                bank_i += 1
                                kb = ("ps", bank)
                                pb = ps[0:cw, bank * 512:(bank + 1) * 512]
                                for k in range(16):
                                    MM(P, pb, wv[:, k, c0:c0 + cw], hT[:, k, half * 512:(half + 1) * 512],
                                       k == 0, k == 15, [ks, ("hT", half)], [kb])
                                si = stg_i % 4
                                stg_i += 1
                                kst = ("stg", si)
                                dsb = stb[0:cw, si, :] if blk["dt"] == BF16 else stg[0:cw, si, :]
                                if blk.get("silu"):
                                    ACT(P, dsb, pb, AF.Silu, [], [kb, kst])
                                else:
                                    ev_i += 1
                                    CP(P, "act" if ev_i % 2 else "dve", dsb, pb, [], [kb, kst])
                                r0 = blk["off"] + g0 + c0
                                tt0 = tile * TA + half * 512
                                DMA(P, "sp", blk["dst"][r0:r0 + cw, tt0:tt0 + 512], dsb, [kst], [])
                    else:
                        for sub in range(NSUB):
                            bank = bank_i % 6
                            bank_i += 1
                            kb = ("ps", bank)
                            pb = ps[:, bank * 512:bank * 512 + gw]
                            for k in range(16):
                                MM(P, pb, hT[:, k, sub * 128:(sub + 1) * 128], wv[:, k, 0:gw],
                                   k == 0, k == 15, [ks, ("hT", sub // 4)], [kb])
                            si = stg_i % 4
                            stg_i += 1
                            kst = ("stg", si)
                            dsb = stb[:, si, 0:gw] if blk["dt"] == BF16 else stg[:, si, 0:gw]
                            ev_i += 1
                            CP(P, "act" if ev_i % 2 else "dve", dsb, pb, [], [kb, kst])
                            tt0 = tile * TA + sub * 128
                            cc0 = blk["off"] + g0
                            DMA(P, "sp", blk["dst"][tt0:tt0 + 128, cc0:cc0 + gw], dsb, [kst], [])


def phase_CD(P, C, MIXT, x_src, x_dst, g1row, g2T, g3row, Wout, Wg, Wu, Wd, T, do_ffn=True):
    nc = P.nc
    NKH = HID // 128
    with P.phase() as st:
        aT = sb(st, nc, "aT", [128, 16, 512], BF16)
        actT = sb(st, nc, "actT", [128, NKH, 512], BF16)
        xt = sb(st, nc, "xt", [128, 2, D], F32)
        f = sb(st, nc, "f", [128, 4, D], F32)
        gb = sb(st, nc, "gb", [128, D], F32)
        hbf = sb(st, nc, "hbfC", [128, 2, D], BF16)
        wsl = sb(st, nc, "wslC", [128, 4, 8192], BF16)
        sg = sb(st, nc, "sg", [128, 2, 512], F32)
        ss = sb(st, nc, "ss", [128, 4, 8], F32)
        col = sb(st, nc, "colC", [128, 2, 4], F32)
        ps = pst(st, nc, "psC", [128, 4096], F32)
        state = {"slot": 0}

        junk = sb(st, nc, "junkC", [128, 512], BF16)

        def tok_proj(src, nk, Wdram, srckey):
            for cg in range(4):
                for kg0 in range(0, nk, 16):
                    kn = min(16, nk - kg0)
                    s = state["slot"] % 4
                    state["slot"] += 1
                    ks = ("wsl", s)
                    wv = wsl[:, s, 0:kn * 512].rearrange("p (k n) -> p k n", k=kn)
                    DMA(P, "pool", wv,
                        Wdram[kg0 * 128:(kg0 + kn) * 128, cg * 512:(cg + 1) * 512].rearrange("(k p) n -> p k n", p=128),
                        [], [ks])
                    for k in range(kn):
                        kk = kg0 + k
                        for sub in range(4):
                            bk = (cg % 2) * 4 + sub
                            MM(P, ps[:, bk * 512:(bk + 1) * 512], src[:, kk, sub * 128:(sub + 1) * 128], wv[:, k, :],
                               kk == 0, kk == nk - 1, [ks, srckey], [("ps", bk)])
                for sub in range(4):
                    bk = (cg % 2) * 4 + sub
                    pb = ps[:, bk * 512:(bk + 1) * 512]
                    ACT(P, junk[:], pb, AF.Square, [], [("ps", bk), ("ss", sub)],
                        accum=ss[:, sub, cg:cg + 1])
                    TT(P, "dve", f[:, sub, cg * 512:(cg + 1) * 512], pb, gb[:, cg * 512:(cg + 1) * 512], ALU.mult,
                       ["gb"], [("ps", bk), ("f", sub)])

        def finish_rstd(sub):
            k = ("ss", sub)
            P.op("dve", lambda e: e.tensor_reduce(out=ss[:, sub, 4:5], in_=ss[:, sub, 0:4], axis=mybir.AxisListType.X,
                                                   op=ALU.add), [k], [k])
            ACT(P, ss[:, sub, 5:6], ss[:, sub, 4:5], AF.Sqrt, [k], [k], bias=EPS, scale=1.0 / D)
            RECIP(P, ss[:, sub, 6:7], ss[:, sub, 5:6], [k], [k])

        for tile in range(T // 512):
            t0 = tile * 512
            DMA(P, "sp", aT[:], MIXT[:, t0:t0 + 512].rearrange("(k p) t -> p k t", p=128), [], ["aT"])
            DMA(P, "sp", gb[:], g1row.partition_broadcast(128), [], ["gb"])
            tok_proj(aT, 16, Wout, "aT")
            for sub in range(4):
                j = sub % 2
                kx, kh, kc = ("xt", j), ("hbf", j), ("col", j)
                tt0 = t0 + sub * 128
                kd = ("xd", tile, sub)
                finish_rstd(sub)
                DMA(P, "sp", xt[:, j, :], x_src[tt0:tt0 + 128, :], [kd], [kx])
                STT(P, xt[:, j, :], f[:, sub, :], ss[:, sub, 6:7], xt[:, j, :], ALU.mult, ALU.add,
                    [("f", sub), ("ss", sub), kx], [kx])
                DMA(P, "sp", x_dst[tt0:tt0 + 128, :], xt[:, j, :], [kx], [kd])
                if not do_ffn:
                    continue
                ACT(P, hbf[:, j, :], xt[:, j, :], AF.Square, [kx], [kh, kc], accum=col[:, j, 0:1])
                rstd_from_ss(P, col[:, j, :], D, kc)
                TS(P, "dve", hbf[:, j, :], xt[:, j, :], col[:, j, 2:3], None, ALU.mult, None, [kx, kc], [kh])
                for kg in range(4):
                    bank = 4 + 2 * (kg % 2)
                    kb = ("ps", bank)
                    ptb = ps[:, bank * 512:(bank + 1) * 512].bitcast(BF16)
                    for kk in range(4):
                        k = kg * 4 + kk
                        TR(P, ptb[:, kk * 128:(kk + 1) * 128], hbf[:, j, k * 128:(k + 1) * 128], C["ident"], [kh], [kb])
                    TT(P, "dve", aT[:, kg * 4:(kg + 1) * 4, sub * 128:(sub + 1) * 128],
                       ptb[:, 0:512].rearrange("p (a b) -> p a b", a=4),
                       g2T[:, kg * 4:(kg + 1) * 4].unsqueeze(2).to_broadcast([128, 4, 128]), ALU.mult,
                       [], [kb, "aT"])
            if not do_ffn:
                continue
            DMA(P, "sp", gb[:], g3row.partition_broadcast(128), [], ["gb"])
            for hg in range(HID // 512):
                sl = []
                for Wm in (Wg, Wu):
                    s = state["slot"] % 4
                    state["slot"] += 1
                    wv = wsl[:, s, 0:8192].rearrange("p (k n) -> p k n", k=16)
                    DMA(P, "pool", wv, Wm[:, hg * 512:(hg + 1) * 512].rearrange("(k p) n -> p k n", p=128), [], [("wsl", s)])
                    sl.append((wv, ("wsl", s)))
                for c in range(4):
                    hc = hg * 4 + c
                    bG = 4 + 2 * (hc % 2)
                    bU = bG + 1
                    for (wv, ks), b in zip(sl, (bG, bU)):
                        for k in range(16):
                            MM(P, ps[:, b * 512:(b + 1) * 512], wv[:, k, c * 128:(c + 1) * 128], aT[:, k, :],
                               k == 0, k == 15, [ks, "aT"], [("ps", b)])
                    ACT(P, sg[:, hc % 2, :], ps[:, bG * 512:(bG + 1) * 512], AF.Silu, [], [("ps", bG), ("sg", hc % 2)])
                    TT(P, "dve", actT[:, hc, :], sg[:, hc % 2, :], ps[:, bU * 512:(bU + 1) * 512], ALU.mult,
                       [("sg", hc % 2)], [("ps", bU), "actT"])
            tok_proj(actT, NKH, Wd, "actT")
            for sub in range(4):
                j = sub % 2
                kx = ("xt", j)
                tt0 = t0 + sub * 128
                kd = ("xd", tile, sub)
                finish_rstd(sub)
                DMA(P, "sp", xt[:, j, :], x_dst[tt0:tt0 + 128, :], [kd], [kx])
                STT(P, xt[:, j, :], f[:, sub, :], ss[:, sub, 6:7], xt[:, j, :], ALU.mult, ALU.add,
                    [("f", sub), ("ss", sub), kx], [kx])
                DMA(P, "sp", x_dst[tt0:tt0 + 128, :], xt[:, j, :], [kx], [kd])


def phase_SWA(P, C, OFM, OTM, MIXT, cosT_d, sinT_d, sink_row, T):
    nc = P.nc
    NB = T // 128
    SC = 128 ** -0.5
    with P.phase() as st:
        cosT = sb(st, nc, "cosT", [128, T], F32)
        sinT = sb(st, nc, "sinT", [128, T], F32)
        raw = sb(st, nc, "raw", [128, 2, 512], BF16)
        t1 = sb(st, nc, "t1", [128, 2, 512], F32)
        t2 = sb(st, nc, "t2", [128, 2, 512], F32)
        qrot = sb(st, nc, "qrot", [128, 4, T], BF16)
        krot = sb(st, nc, "krot", [128, T], BF16)
        V = sb(st, nc, "Vswa", [128, NB, 128], BF16)
        pT = sb(st, nc, "pT", [128, 6, 512], BF16)
        es = sb(st, nc, "es", [128, 8], F32)
        den = sb(st, nc, "den", [128, 2, 512], F32)
        ost = sb(st, nc, "ost", [128, 4, 512], BF16)
        ps = pst(st, nc, "psW", [128, 4096], F32)
        DMA(P, "sp", cosT[:], cosT_d[:, 0:T], [], ["cos"])
        DMA(P, "sp", sinT[:], sinT_d[:, 0:T], [], ["sin"])
        DMA(P, "sp", es[:], sink_row.partition_broadcast(128), [], ["es"])
        ACT(P, es[:], es[:], AF.Exp, ["es"], ["es"])
        ri = 0
        pi = 0

        def rope(src_rows, dst_ap_fn, dkey):
            nonlocal ri
            for tb in range(T // 512):
                j = ri % 2
                ri += 1
                bank = 6 + j
                kb = ("ps", bank)
                pb = ps[:, bank * 512:(bank + 1) * 512]
                tsl = slice(tb * 512, (tb + 1) * 512)
                DMA(P, "sp", raw[:, j, :], OFM[src_rows:src_rows + 128, tsl], [], [("raw", j)])
                MM(P, pb, C["rotm"], raw[:, j, :], True, True, [("raw", j)], [kb])
                TT(P, "dve", t1[:, j, :], raw[:, j, :], cosT[:, tsl], ALU.mult, [("raw", j), "cos"], [("t1", j)])
                TT(P, "dve", t2[:, j, :], pb, sinT[:, tsl], ALU.mult, ["sin"], [kb, ("t2", j)])
                TT(P, "pool", dst_ap_fn(tsl), t1[:, j, :], t2[:, j, :], ALU.add, [("t1", j), ("t2", j)], [dkey])

        for g in range(2):
            rope(3584 + g * 128, lambda tsl: krot[:, tsl], "krot")
            for h in range(4):
                rope(2560 + (g * 4 + h) * 128, (lambda hh: (lambda tsl: qrot[:, hh, tsl]))(h), "qrot")
            DMA(P, "sp", V[:], OTM[:, g * 128:(g + 1) * 128].rearrange("(n p) c -> p n c", p=128), [], ["V"])
            def jsof(n):
                return [j for j in (n - 1, n, n + 1) if 0 <= j < NB]

            def st1(n):
                nonlocal pi
                for idx, jb in enumerate(jsof(n)):
                    bS = pi % 2
                    pi += 1
                    pslot = (n % 2) * 3 + idx
                    pS = ps[:, bS * 512:(bS + 1) * 512]
                    kp = ("pT", pslot)
                    MM(P, pS, krot[:, jb * 128:(jb + 1) * 128], qrot[:, :, n * 128:(n + 1) * 128], True, True,
                       ["krot", "qrot"], [("ps", bS)])
                    ACT(P, pT[:, pslot, :], pS, AF.Exp, [], [("ps", bS), kp], scale=SC)
                    if jb != n:
                        mk = C["mprev"] if jb < n else C["mnext"]
                        TT(P, "dve", pT[:, pslot, :].rearrange("p (h q) -> p h q", h=4),
                           pT[:, pslot, :].rearrange("p (h q) -> p h q", h=4),
                           mk.unsqueeze(1).to_broadcast([128, 4, 128]), ALU.mult, [], [kp])

            def st2(n, g=g):
                bO = 2 + (n % 2)
                bD = 4 + (n % 2)
                pO = ps[:, bO * 512:(bO + 1) * 512]
                pD = ps[:, bD * 512:(bD + 1) * 512]
                js = jsof(n)
                for idx, jb in enumerate(js):
                    pslot = (n % 2) * 3 + idx
                    kp = ("pT", pslot)
                    MM(P, pO, V[:, jb, :], pT[:, pslot, :], idx == 0, idx == len(js) - 1, ["V", kp], [("ps", bO)])
                    MM(P, pD, C["ones"], pT[:, pslot, :], idx == 0, idx == len(js) - 1, [kp], [("ps", bD)])
                dj = n % 2
                TT(P, "dve", den[:, dj, :].rearrange("p (h q) -> p h q", h=4), pD.rearrange("p (h q) -> p h q", h=4),
                   es[:, g * 4:(g + 1) * 4].unsqueeze(2).to_broadcast([128, 4, 128]), ALU.add,
                   ["es"], [("ps", bD), ("den", dj)])
                RECIP(P, den[:, dj, :], den[:, dj, :], [], [("den", dj)])
                TT(P, "dve", ost[:, :, (n % 4) * 128:(n % 4 + 1) * 128], pO.rearrange("p (h q) -> p h q", h=4),
                   den[:, dj, :].rearrange("p (h q) -> p h q", h=4), ALU.mult, [("den", dj)], [("ps", bO), "ost"])
                if n % 4 == 3:
                    n0 = n - 3
                    r0 = 1024 + g * 512
                    DMA(P, "sp", MIXT[r0:r0 + 512, n0 * 128:n0 * 128 + 512].rearrange("(h d) t -> d h t", d=128),
                        ost[:], ["ost"], [])

            st1(0)
            for n in range(NB):
                if n + 1 < NB:
                    st1(n + 1)
                st2(n)


def na_pattern(m):
    return 0 if m == 0 else 1 if m == 1 else 3 if m == 30 else 4 if m == 31 else 2


def phase_NA(P, C, EFM, ETM, MIXT, nab, T):
    nc = P.nc
    NB = T // 128
    SC = 128 ** -0.5
    assert T == 4096
    with P.phase() as st:
        qT = sb(st, nc, "naq", [128, 2, T], BF16)
        kT = sb(st, nc, "nak", [128, 2, T], BF16)
        V = sb(st, nc, "nav", [128, 2, NB * 128], BF16)
        bias = sb(st, nc, "nabias", [128, 2, 5 * 640], F32)
        sbf = sb(st, nc, "nasb", [128, 2, 640], F32)
        pT = sb(st, nc, "napT", [128, 2, 640], BF16)
        rden = sb(st, nc, "narden", [128, 2, 128], F32)
        ost = sb(st, nc, "naost", [128, 2, 512], BF16)
        ps = pst(st, nc, "psN", [128, 4096], F32)
        def keys(h):
            hj = h % 2
            return hj, ("q", hj), ("k", hj), ("v", hj), ("b", hj)

        def loads(h):
            hj, kq, kk, kv, kbz = keys(h)
            DMA(P, "sp", qT[:, hj, :], EFM[2048 + h * 128:2048 + (h + 1) * 128, :], [], [kq])
            DMA(P, "sp", kT[:, hj, :], EFM[3072 + h * 128:3072 + (h + 1) * 128, :], [], [kk])
            DMA(P, "sp", V[:, hj, :].rearrange("p (n c) -> p n c", c=128),
                ETM[:, 1024 + h * 128:1024 + (h + 1) * 128].rearrange("(n p) c -> p n c", p=128), [], [kv])
            DMA(P, "sp", bias[:, hj, :], nab[h], [], [kbz])

        iters = [(h, m) for h in range(8) for m in range(NB)]

        def stage1(idx):
            h, m = iters[idx]
            hj, kq, kk, kv, kbz = keys(h)
            u0 = min(max(m - 2, 0), NB - 5)
            pat = na_pattern(m)
            sj = idx % 2
            b0 = 2 * sj
            kS = ("ps", b0)
            pS = ps[:, b0 * 512:b0 * 512 + 640]
            for j in range(5):
                MM(P, pS[:, j * 128:(j + 1) * 128], kT[:, hj, (u0 + j) * 128:(u0 + j + 1) * 128],
                   qT[:, hj, m * 128:(m + 1) * 128], True, True, [kq, kk], [kS if j < 4 else ("ps", b0 + 1)])
            STT(P, sbf[:, sj, 0:512], pS[:, 0:512], SC, bias[:, hj, pat * 640:pat * 640 + 512], ALU.mult, ALU.add,
                [kbz], [kS, ("sbf", sj)])
            STT(P, sbf[:, sj, 512:640], pS[:, 512:640], SC, bias[:, hj, pat * 640 + 512:(pat + 1) * 640], ALU.mult, ALU.add,
                [kbz], [("ps", b0 + 1), ("sbf", sj)])
            ACT(P, pT[:, sj, :], sbf[:, sj, :], AF.Exp, [("sbf", sj)], [("pT", sj)])

        def stage2(idx):
            h, m = iters[idx]
            hj, kq, kk, kv, kbz = keys(h)
            u0 = min(max(m - 2, 0), NB - 5)
            sj = idx % 2
            bO = 4 + sj
            pO = ps[:, bO * 512:bO * 512 + 128]
            pD = ps[:, bO * 512 + 128:bO * 512 + 256]
            for j in range(5):
                MM(P, pO, V[:, hj, (u0 + j) * 128:(u0 + j + 1) * 128], pT[:, sj, j * 128:(j + 1) * 128],
                   j == 0, j == 4, [kv, ("pT", sj)], [("ps", bO)])
            for j in range(5):
                MM(P, pD, C["ones"], pT[:, sj, j * 128:(j + 1) * 128], j == 0, j == 4, [("pT", sj)], [("ps", bO)])
            RECIP(P, rden[:, sj, :], pD, [], [("ps", bO), ("rden", sj)])
            oj = (m // 4) % 2
            TT(P, "dve", ost[:, oj, (m % 4) * 128:(m % 4 + 1) * 128], pO, rden[:, sj, :], ALU.mult,
               [("rden", sj)], [("ps", bO), ("ost", oj)])
            if m % 4 == 3:
                n0 = m - 3
                DMA(P, "sp", MIXT[1024 + h * 128:1024 + (h + 1) * 128, n0 * 128:n0 * 128 + 512], ost[:, oj, :],
                    [("ost", oj)], [])

        loads(0)
        stage1(0)
        for idx in range(len(iters)):
            h, m = iters[idx]
            if m == 0 and h + 1 < 8:
                loads(h + 1)
            if idx + 1 < len(iters):
                stage1(idx + 1)
            stage2(idx)


def make_consts(T=T_FULL):
    c = {}
    c["ident"] = np.eye(128, dtype=np.float32)
    c["ones"] = np.ones((128, 128), np.float32)
    rot = np.zeros((128, 128), np.float32)
    for m in range(64):
        rot[m + 64, m] = -1.0
        rot[m, m + 64] = 1.0
    c["rotm"] = rot
    jj = np.arange(128)[:, None]
    ii = np.arange(128)[None, :]
    c["mprev"] = (ii <= jj).astype(np.float32)
    c["mnext"] = (jj <= ii).astype(np.float32)
    half = 64
    inv = (10000.0 ** (-np.arange(half, dtype=np.float32) / half)).astype(np.float32)
    ang = np.arange(T, dtype=np.float32)[None, :] * np.concatenate([inv, inv])[:, None]
    c["cosT"] = np.cos(ang).astype(np.float32)
    c["sinT"] = np.sin(ang).astype(np.float32)
    return c


def na_bias_tables(rpb):
    H = rpb.shape[0]
    out = np.full((H, 128, 5, 5, 128), -30000.0, np.float32)
    reps = [0, 1, 5, 30, 31]
    for p, m in enumerate(reps):
        u0 = min(max(m - 2, 0), 27)
        ktok = (u0 * 128 + np.arange(640)).reshape(5, 128)
        kr, kc = ktok // 64, ktok % 64
        qtok = m * 128 + np.arange(128)
        r, cq = qtok // 64, qtok % 64
        rs = np.clip(r - 4, 0, 56)
        cs = np.clip(cq - 8, 0, 48)
        KR = kr[:, :, None]
        KC = kc[:, :, None]
        valid = (KR >= rs[None, None, :]) & (KR < rs[None, None, :] + 8) & (KC >= cs[None, None, :]) & (KC < cs[None, None, :] + 16)
        dr = np.clip(KR - r[None, None, :] + 7, 0, 14)
        dc = np.clip(KC - cq[None, None, :] + 15, 0, 30)
        for h in range(H):
            vals = rpb[h][dr, dc]
            tab = np.where(valid, vals, np.float32(-30000.0)).astype(np.float32)
            out[h, :, p, :, :] = tab.transpose(1, 0, 2)
    return out.reshape(H, 128, 5 * 640)


def phase_GLA(P, C, EFM, ETM, LRT, MIXT, wdec_d, bdecT_d, gnT_d, T):
    nc = P.nc
    NB = T // 128
    NT5 = T // 512
    with P.phase() as st:
        lrT = sb(st, nc, "lrT", [32, T], F32)
        wdec = sb(st, nc, "wdec", [32, 2 * 512], F32)
        negb = sb(st, nc, "negb", [128, 8], F32)
        gnT = sb(st, nc, "gnT", [128, 8], F32)
        rmask = sb(st, nc, "rmask", [128, 512], F32)
        mk2 = sb(st, nc, "mk2", [128, 2, 128], BF16)
        cs = sb(st, nc, "cs", [128, 2, T], F32)
        tmpe = sb(st, nc, "tmpe", [128, 2, 512], F32)
        tmps = sb(st, nc, "tmps", [128, 2, 512], F32)
        tmpp = sb(st, nc, "tmpp", [128, 2, 512], F32)
        qk = sb(st, nc, "qkraw", [128, 2, T], BF16)
        qkt = sb(st, nc, "qkt", [128, 4, T], BF16)
        ktok = sb(st, nc, "ktok", [128, 2, NB * 128], BF16)
        Vp = sb(st, nc, "Vp", [128, NB, 256], BF16)
        Sprev = sb(st, nc, "Sprev", [128, 2, NB * 128], BF16)
        S = sb(st, nc, "Sst", [128, 2, 128], F32)
        tS = sb(st, nc, "tS", [128, 2, 128], F32)
        dl = sb(st, nc, "dl", [128, 2, NB], F32)
        ATm = sb(st, nc, "ATm", [128, 2, 512], BF16)
        sq = sb(st, nc, "sq", [128, 2, 512], BF16)
        rt = sb(st, nc, "rt", [128, 2, 512], F32)
        o1 = sb(st, nc, "o1", [128, 2, 512], F32)
        sgt = sb(st, nc, "sgt", [128, 2, 512], BF16)
        ostg = sb(st, nc, "ostg", [128, 2, 512], BF16)
        ps = pst(st, nc, "psG", [128, 4096], F32)

        def bank(i, w=512):
            return ps[:, i * 512:i * 512 + w]

        DMA(P, "sp", lrT[:], LRT, [], ["lrT"])
        DMA(P, "sp", wdec[:], wdec_d.rearrange("r d c -> r (d c)"), [], ["wdec"])
        DMA(P, "sp", negb[:], bdecT_d, [], ["negb"])
        TS(P, "dve", negb[:], negb[:], -1.0, None, ALU.mult, None, ["negb"], ["negb"])
        DMA(P, "sp", gnT[:], gnT_d, [], ["gnT"])
        MEMSET(P, "dve", rmask[:], 1.0, [], ["rmask"])
        for c in range(4):
            MEMSET(P, "dve", rmask[:, c * 128:c * 128 + 1], 0.0, [], ["rmask"])
        CP(P, "dve", mk2[:, 0, :], C["mnext"], [], ["mk2"])
        CP(P, "dve", mk2[:, 1, :], C["mprev"], [], ["mk2"])
        ti = 0
        ei = 0
        for pr in range(4):
            DMA(P, "sp", qk[:, 0, :], EFM[pr * 128:(pr + 1) * 128, :], [], ["qraw"])
            DMA(P, "sp", qk[:, 1, :], EFM[512 + pr * 128:512 + (pr + 1) * 128, :], [], ["kraw"])
            DMA(P, "sp", Vp[:], ETM[:, pr * 256:(pr + 1) * 256].rearrange("(n p) c -> p n c", p=128), [], ["Vp"])
            for dr in range(2):
                for tb in range(NT5):
                    j = ti % 2
                    ti += 1
                    tsl = slice(tb * 512, (tb + 1) * 512)
                    MM(P, bank(0), wdec[0:32, dr * 512 + pr * 128:dr * 512 + (pr + 1) * 128], lrT[0:32, tsl], True, True,
                       ["wdec", "lrT"], [("ps", 0)])
                    ACT(P, tmpe[:, j, :], bank(0), AF.Exp, ["negb"], [("ps", 0), ("tmpe", j)],
                        bias=negb[:, dr * 4 + pr:dr * 4 + pr + 1], scale=-1.0)
                    ACT(P, tmps[:, j, :], tmpe[:, j, :], AF.Ln, [("tmpe", j)], [("tmps", j)], bias=1.0)
                    if dr == 0:
                        P.op("dve", (lambda o, d1: (lambda e: e.tensor_tensor_scan(out=o, data0=rmask[:], data1=d1, initial=0.0,
                                                                                    op0=ALU.mult, op1=ALU.add)))(cs[:, 0, tsl], tmps[:, j, :]),
                             [("tmps", j), "rmask"], [("cs", 0)])
                    else:
                        P.op("dve", (lambda o, d1: (lambda e: e.tensor_tensor_scan(out=o, data0=rmask[:], data1=d1, initial=0.0,
                                                                                    op0=ALU.mult, op1=ALU.add)))(tmpp[:, j, :], tmps[:, j, :]),
                             [("tmps", j), "rmask"], [("tmpp", j)])
                        TT(P, "dve", tmps[:, j, :], tmps[:, j, :], tmpp[:, j, :], ALU.subtract, [("tmpp", j)], [("tmps", j)])
                        TT(P, "dve", cs[:, 1, tsl].rearrange("p (c t) -> p c t", c=4),
                           tmps[:, j, :].rearrange("p (c t) -> p c t", c=4),
                           tmpp[:, j, :].rearrange("p (c t) -> p c t", c=4)[:, :, 127:128].to_broadcast([128, 4, 128]),
                           ALU.add, [("tmps", j), ("tmpp", j)], [("cs", 1)])
            for dr in range(2):
                for tb in range(NT5):
                    tsl = slice(tb * 512, (tb + 1) * 512)
                    j = ei % 2
                    ei += 1
                    ACT(P, tmpe[:, j, :], cs[:, dr, tsl], AF.Exp, [("cs", dr)], [("tmpe", j)], scale=-1.0 / 16)
                    STT(P, qkt[:, 2 * dr, tsl], qk[:, 0, tsl], 0.125, tmpe[:, j, :], ALU.mult, ALU.mult,
                        ["qraw", ("tmpe", j)], [("qkt", 2 * dr)])
                    ACT(P, tmpp[:, j, :], cs[:, dr, tsl], AF.Exp, [("cs", dr)], [("tmpp", j)], scale=1.0 / 16)
                    TT(P, "dve", qkt[:, 2 * dr + 1, tsl], qk[:, 1, tsl], tmpp[:, j, :], ALU.mult,
                       ["kraw", ("tmpp", j)], [("qkt", 2 * dr + 1)])
            cs3f = cs[:, 0, :].rearrange("p (n t) -> p n t", t=128)
            cs3b = cs[:, 1, :].rearrange("p (n t) -> p n t", t=128)
            ACT(P, dl[:, 0, :], cs3f[:, :, 127], AF.Exp, [("cs", 0)], ["dl"], scale=-1.0 / 16)
            ACT(P, dl[:, 1, :], cs3b[:, :, 0], AF.Exp, [("cs", 1)], ["dl"], scale=-1.0 / 16)
            for dr in range(2):
                for n0 in range(0, NB, 4):
                    ptb = bank(1).bitcast(BF16)
                    for c in range(4):
                        n = n0 + c
                        TR(P, ptb[:, c * 128:(c + 1) * 128], qkt[:, 2 * dr + 1, n * 128:(n + 1) * 128], C["ident"],
                           [("qkt", 2 * dr + 1)], [("ps", 1)])
                    CP(P, "act" if (n0 // 4) % 2 else "dve", ktok[:, dr, n0 * 128:(n0 + 4) * 128], ptb[:, 0:512], [],
                       [("ps", 1), ("ktok", dr)])
            for dr in range(2):
                MEMSET(P, "dve", S[:, dr, :], 0.0, [], [("S", dr)])
            for idx in range(NB):
                for dr in range(2):
                    kS = ("S", dr)
                    n = idx if dr == 0 else NB - 1 - idx
                    CP(P, "act", Sprev[:, dr, n * 128:(n + 1) * 128], S[:, dr, :], [kS], [("Sprev", dr)])
                    if idx == NB - 1:
                        continue
                    bU = 2 if dr == 0 else 5
                    pU = bank(bU, 128)
                    for hh in range(2):
                        MM(P, pU[hh * 64:(hh + 1) * 64, :], ktok[:, dr, n * 128 + hh * 64:n * 128 + (hh + 1) * 64],
                           Vp[:, n, hh * 128:(hh + 1) * 128], True, True, [("ktok", dr), "Vp"], [("ps", bU)])
                    TT(P, "dve", tS[:, dr, :], pU, S[:, dr, :], ALU.add, [kS], [("ps", bU), ("tS", dr)])
                    ACT(P, S[:, dr, :], tS[:, dr, :], AF.Copy, [("tS", dr), "dl"], [kS], scale=dl[:, dr, n:n + 1])
            def stageA(n):
                bAs = (3, 4) if n % 2 == 0 else (0, 1)
                aj = n % 2
                for dr in range(2):
                    for hh in range(2):
                        pA = bank(bAs[hh])
                        MM(P, pA[:, dr * 128:(dr + 1) * 128], qkt[hh * 64:(hh + 1) * 64, 2 * dr + 1, n * 128:(n + 1) * 128],
                           qkt[hh * 64:(hh + 1) * 64, 2 * dr, n * 128:(n + 1) * 128], True, True,
                           [("qkt", 2 * dr + 1), ("qkt", 2 * dr)], [("ps", bAs[hh])])
                for hh in range(2):
                    TT(P, "dve", ATm[:, aj, hh * 256:(hh + 1) * 256].rearrange("p (d t) -> p d t", d=2),
                       bank(bAs[hh])[:, 0:256].rearrange("p (d t) -> p d t", d=2),
                       mk2[:], ALU.mult, ["mk2"], [("ps", bAs[hh]), ("ATm", aj)])

            def stageB(n, pr=pr):
                aj = n % 2
                for hh in range(2):
                    pO = bank(5 + hh)[:, (n % 4) * 128:(n % 4 + 1) * 128]
                    kO = ("ps", 5 + hh)
                    for dr in range(2):
                        MM(P, pO, Vp[:, n, hh * 128:(hh + 1) * 128], ATm[:, aj, (hh * 2 + dr) * 128:(hh * 2 + dr + 1) * 128],
                           dr == 0, False, ["Vp", ("ATm", aj)], [kO])
                        MM(P, pO, Sprev[hh * 64:(hh + 1) * 64, dr, n * 128:(n + 1) * 128],
                           qkt[hh * 64:(hh + 1) * 64, 2 * dr, n * 128:(n + 1) * 128], False, dr == 1,
                           [("Sprev", dr), ("qkt", 2 * dr)], [kO])
                if n % 4 == 3:
                    tsl = slice((n - 3) * 128, (n + 1) * 128)
                    for hh in range(2):
                        h = pr * 2 + hh
                        kO = ("ps", 5 + hh)
                        pO = bank(5 + hh)
                        DMA(P, "sp", sgt[:, hh, :], EFM[1024 + h * 128:1024 + (h + 1) * 128, tsl], [], [("sgt", hh)])
                        ACT(P, sq[:, hh, :], pO, AF.Square, [], [kO, ("sq", hh)])
                        MM(P, bank(7), C["ones"], sq[:, hh, :], True, True, [("sq", hh)], [("ps", 7)])
                        ACT(P, rt[:, hh, :], bank(7), AF.Sqrt, [], [("ps", 7), ("rt", hh)], bias=EPS, scale=1.0 / 128)
                        RECIP(P, rt[:, hh, :], rt[:, hh, :], [], [("rt", hh)])
                        TT(P, "dve", o1[:, hh, :], pO, rt[:, hh, :], ALU.mult, [("rt", hh)], [kO, ("o1", hh)])
                        STT(P, ostg[:, hh, :], o1[:, hh, :], gnT[:, h:h + 1], sgt[:, hh, :], ALU.mult, ALU.mult,
                            [("o1", hh), "gnT", ("sgt", hh)], [("ostg", hh)])
                        DMA(P, "sp", MIXT[h * 128:(h + 1) * 128, tsl], ostg[:, hh, :], [("ostg", hh)], [])

            stageA(0)
            for n in range(NB):
                if n + 1 < NB:
                    stageA(n + 1)
                stageB(n)


def phase_SSD(P, C, OFM, DTR, MIXT, XBCS, STATES, PREV, prm, T):
    nc = P.nc
    NB = T // 128
    NT5 = T // 512
    with contextlib.ExitStack() as sst:
        _phase_SSD(P, C, OFM, DTR, MIXT, XBCS, STATES, PREV, prm, T, sst)


def _phase_SSD(P, C, OFM, DTR, MIXT, XBCS, STATES, PREV, prm, T, sst):
    nc = P.nc
    NB = T // 128
    NT5 = T // 512
    dt = sb(sst, nc, "ssd_dt", [128, NB, 32], F32)
    acum = sb(sst, nc, "ssd_acum", [128, NB, 32], F32)
    nacum = sb(sst, nc, "ssd_nacum", [128, NB, 32], F32)
    EA = sb(sst, nc, "ssd_EA", [128, NB, 32], F32)
    acT = sb(sst, nc, "ssd_acT", [16, 2, T], F32)
    with P.phase() as st:
        cw = sb(st, nc, "convw", [128, 60], F32)
        cb = sb(st, nc, "convb", [128, 12], F32)
        diagw = sb(st, nc, "diagw", [128, 60, 128], BF16)
        dtb = sb(st, nc, "dtb", [128, 32], F32)
        Aneg = sb(st, nc, "Aneg", [128, 32], F32)
        da = sb(st, nc, "ssd_da", [128, NB, 32], F32)
        dtdec = sb(st, nc, "ssd_dtdec", [128, NB, 32], F32)
        rawx = sb(st, nc, "rawx", [128, 2, 12 * 516], BF16)
        xbc = sb(st, nc, "xbc", [128, 2, 12 * 512], BF16)
        xdd = sb(st, nc, "xdd", [128, 2, 2 * 1024], BF16)
        btok = sb(st, nc, "btok", [128, 2, 256], BF16)
        stsb = sb(st, nc, "stsb", [128, 2, 2 * 1024], F32)
        ps = pst(st, nc, "psS1", [128, 4096], F32)

        def bank(i, w=512):
            return ps[:, i * 512:i * 512 + w]

        DMA(P, "sp", cw[:], prm["convwT"], [], ["cw"])
        DMA(P, "sp", cb[:], prm["convbT"], [], ["cb"])
        DMA(P, "sp", dtb[:], prm["dtb"].partition_broadcast(128), [], ["dtb"])
        DMA(P, "sp", Aneg[:], prm["alog"].partition_broadcast(128), [], ["Aneg"])
        ACT(P, Aneg[:], Aneg[:], AF.Exp, ["Aneg"], ["Aneg"])
        TS(P, "dve", Aneg[:], Aneg[:], -1.0, None, ALU.mult, None, ["Aneg"], ["Aneg"])
        for i in range(60):
            TS(P, "dve" if i % 2 else "pool", diagw[:, i, :], C["ident"], cw[:, i:i + 1], None, ALU.mult, None, ["cw"], ["diagw"])
        DMA(P, "sp", dt[:], DTR.rearrange("(n p) c -> p n c", p=128), [], ["dt"])
        TT(P, "dve", dt[:], dt[:], dtb[:].unsqueeze(1).to_broadcast([128, NB, 32]), ALU.add, ["dtb"], ["dt"])
        ACT(P, dt[:], dt[:], AF.Exp, ["dt"], ["dt"])
        ACT(P, dt[:], dt[:], AF.Ln, ["dt"], ["dt"], bias=1.0)
        TT(P, "dve", da[:], dt[:], Aneg[:].unsqueeze(1).to_broadcast([128, NB, 32]), ALU.mult, ["dt", "Aneg"], ["da"])
        for dr in range(2):
            tri = C["mnext32"] if dr == 0 else C["mprev32"]
            MM(P, bank(0).rearrange("p (n h) -> p n h", h=16), tri, da[:, :, dr * 16:(dr + 1) * 16], True, True,
               ["da"], [("ps", 0)])
            CP(P, "dve", acum[:, :, dr * 16:(dr + 1) * 16], bank(0).rearrange("p (n h) -> p n h", h=16), [],
               [("ps", 0), "acum"])
        TS(P, "dve", nacum[:], acum[:], -1.0, None, ALU.mult, None, ["acum"], ["nacum"])
        for hf in range(2):
            nsl = slice(hf * (NB // 2), (hf + 1) * (NB // 2))
            MM(P, bank(1).rearrange("p (n h) -> p n h", h=32), C["ones32"], da[:, nsl, :], True, True, ["da"], [("ps", 1)])
            TT(P, "dve", dtdec[:, nsl, :], bank(1).rearrange("p (n h) -> p n h", h=32), acum[:, nsl, :], ALU.subtract,
               ["acum"], [("ps", 1), "dtdec"])
            ACT(P, EA[:, nsl, :], bank(1).rearrange("p (n h) -> p n h", h=32), AF.Exp, [], [("ps", 1), "EA"])
        ACT(P, dtdec[:], dtdec[:], AF.Exp, ["dtdec"], ["dtdec"])
        TT(P, "dve", dtdec[:], dtdec[:], dt[:], ALU.mult, ["dt"], ["dtdec"])
        for dr in range(2):
            tri = C["mnext32"] if dr == 0 else C["mprev32"]
            for n0 in range(0, NB, 4):
                for c in range(4):
                    n = n0 + c
                    MM(P, ps[0:16, 1024 + c * 128:1024 + (c + 1) * 128], da[:, n, dr * 16:(dr + 1) * 16], tri, True, True,
                       ["da"], [("ps", 2)])
                CP(P, "act", acT[:, dr, n0 * 128:(n0 + 4) * 128], ps[0:16, 1024:1536], [], [("ps", 2), ("acT", dr)])
        for tb in range(NT5):
            j = tb % 2
            t0 = tb * 512
            kr, kx = ("rawx", j), ("xbc", j)
            rv = rawx[:, j, :].rearrange("p (c t) -> p c t", c=12)
            xv = xbc[:, j, :].rearrange("p (c t) -> p c t", c=12)
            lo = max(t0 - 2, 0)
            hi = min(t0 + 514, T)
            if tb == 0:
                MEMSET(P, "dve", rv[:, :, 0:2], 0.0, [], [kr])
            if tb == NT5 - 1:
                MEMSET(P, "dve", rv[:, :, 514:516], 0.0, [], [kr])
            DMA(P, "sp", rv[:, :, lo - (t0 - 2):hi - (t0 - 2)], OFM[1024:2560, lo:hi].rearrange("(c p) t -> p c t", p=128), [], [kr])
            for ch in range(12):
                b = 4 + (ch % 2)
                for jj in range(5):
                    MM(P, bank(b), diagw[:, ch * 5 + jj, :], rv[:, ch, jj:jj + 512], jj == 0, jj == 4, [kr, "diagw"], [("ps", b)])
                ACT(P, xv[:, ch, :], bank(b), AF.Silu, ["cb"], [("ps", b), kx], bias=cb[:, ch:ch + 1])
            DMA(P, "sp", XBCS[:, t0:t0 + 512].rearrange("(c p) t -> p c t", p=128), xv, [kx], [])
            for c in range(4):
                n = tb * 4 + c
                cj = n % 2
                tsl = slice(c * 128, (c + 1) * 128)
                pX = bank(6).bitcast(BF16)
                pB = bank(7).bitcast(BF16)
                for k in range(8):
                    TR(P, pX[:, k * 128:(k + 1) * 128], xv[:, k, tsl], C["ident"], [kx], [("ps", 6)])
                for g in range(2):
                    TR(P, pB[:, g * 128:(g + 1) * 128], xv[:, 8 + g, tsl], C["ident"], [kx], [("ps", 7)])
                for dr in range(2):
                    TT(P, "dve", xdd[:, cj, dr * 1024:(dr + 1) * 1024].rearrange("p (h q) -> p h q", h=16),
                       pX[:, 0:1024].rearrange("p (h q) -> p h q", h=16),
                       dtdec[:, n, dr * 16:(dr + 1) * 16].unsqueeze(2).to_broadcast([128, 16, 64]), ALU.mult,
                       ["dtdec"], [("ps", 6), ("xdd", cj)])
                CP(P, "act", btok[:, cj, :], pB[:, 0:256], [], [("ps", 7), ("btok", cj)])
                for dr in range(2):
                    for g in range(2):
                        b = dr * 2 + g
                        MM(P, bank(b), btok[:, cj, g * 128:(g + 1) * 128],
                           xdd[:, cj, dr * 1024 + g * 512:dr * 1024 + (g + 1) * 512], True, True,
                           [("btok", cj), ("xdd", cj)], [("ps", b)])
                        CP(P, "act" if g else "dve", stsb[:, cj, dr * 1024 + g * 512:dr * 1024 + (g + 1) * 512], bank(b), [],
                           [("ps", b), ("stsb", cj)])
                for dr in range(2):
                    DMA(P, "sp", STATES[dr, n], stsb[:, cj, dr * 1024:(dr + 1) * 1024], [("stsb", cj)], [])
    with P.phase() as st:
        prev = sb(st, nc, "prevS", [128, 2, 1024], F32)
        stl = sb(st, nc, "stl", [128, 4, 1024], F32)
        pvb = sb(st, nc, "pvb", [128, 4, 1024], BF16)
        i = 0
        for dr in range(2):
            kp = ("prev", dr)
            MEMSET(P, "dve" if dr == 0 else "pool", prev[:, dr, :], 0.0, [], [kp])
            order = range(NB) if dr == 0 else range(NB - 1, -1, -1)
            eng = "dve" if dr == 0 else "pool"
            for idx, n in enumerate(order):
                j = (i % 2) + 2 * dr
                i += 1
                CP(P, "act", pvb[:, j, :], prev[:, dr, :], [kp], [("pvb", j)])
                DMA(P, "sp", PREV[dr, n], pvb[:, j, :], [("pvb", j)], [])
                if idx == NB - 1:
                    break
                DMA(P, "sp", stl[:, j, :], STATES[dr, n], [], [("stl", j)])
                TT(P, eng, prev[:, dr, :].rearrange("p (h q) -> p h q", h=16), prev[:, dr, :].rearrange("p (h q) -> p h q", h=16),
                   EA[:, n, dr * 16:(dr + 1) * 16].unsqueeze(2).to_broadcast([128, 16, 64]), ALU.mult, [], [kp])
                TT(P, eng, prev[:, dr, :], prev[:, dr, :], stl[:, j, :], ALU.add, [("stl", j)], [kp])
    with P.phase() as st:
        dsk = sb(st, nc, "dsk", [128, 8], F32)
        snT = sb(st, nc, "snT", [128, 8], F32)
        rsel = sb(st, nc, "rsel", [16, 16 * 128], F32)
        negm = sb(st, nc, "negm", [128, 2, 128], F32)
        xbc = sb(st, nc, "xbc2", [128, 2, 12 * 512], BF16)
        zs = sb(st, nc, "zs", [128, 2, 8 * 512], BF16)
        pv = sb(st, nc, "pv", [128, 2, 2 * 1024], BF16)
        xdt = sb(st, nc, "xdt", [128, 2, 2 * 1024], BF16)
        cbt = sb(st, nc, "cbt", [128, 2, 256], F32)
        tmp = sb(st, nc, "tmpL", [128, 2, 512], F32)
        tmp2 = sb(st, nc, "tmpL2", [128, 2, 512], F32)
        LT = sb(st, nc, "LT", [128, 2, 512], F32)
        Eh = sb(st, nc, "Eh", [128, 2, 512], F32)
        MT = sb(st, nc, "MT", [128, 2, 2 * 512], BF16)
        Cp = sb(st, nc, "Cp", [128, 2, 2 * 512], BF16)
        ysb = sb(st, nc, "ysb", [128, 2, 1024], F32)
        sq = sb(st, nc, "sq2", [128, 2, 1024], BF16)
        rt = sb(st, nc, "rt2", [128, 2, 128], F32)
        ost = sb(st, nc, "ost2", [128, 2, 8 * 512], BF16)
        ps = pst(st, nc, "psS2", [128, 4096], F32)

        def bank(i, w=512):
            return ps[:, i * 512:i * 512 + w]

        DMA(P, "sp", dsk[:], prm["dskT"], [], ["dsk"])
        DMA(P, "sp", snT[:], prm["snT"], [], ["snT"])
        DMA(P, "sp", rsel[:], prm["rsel"], [], ["rsel"])
        TS(P, "dve", negm[:, 0, :], C["mnext32"], 30000.0, -30000.0, ALU.mult, ALU.add, [], ["negm"])
        TS(P, "dve", negm[:, 1, :], C["mprev32"], 30000.0, -30000.0, ALU.mult, ALU.add, [], ["negm"])
        li = 0
        for tb in range(NT5):
            j = tb % 2
            t0 = tb * 512
            kx, kz, ko = ("xbc", j), ("zs", j), ("ost", j)
            xv = xbc[:, j, :].rearrange("p (c t) -> p c t", c=12)
            zv = zs[:, j, :].rearrange("p (c t) -> p c t", c=8)
            ov = ost[:, j, :].rearrange("p (c t) -> p c t", c=8)
            DMA(P, "sp", xv, XBCS[:, t0:t0 + 512].rearrange("(c p) t -> p c t", p=128), [], [kx])
            DMA(P, "sp", zv, OFM[0:1024, t0:t0 + 512].rearrange("(c p) t -> p c t", p=128), [], [kz])
            for c in range(4):
                n = tb * 4 + c
                cj = n % 2
                tsl = slice(c * 128, (c + 1) * 128)
                kpv, kxd, kcb = ("pv", cj), ("xdt", cj), ("cbt", cj)
                for dr in range(2):
                    DMA(P, "sp", pv[:, cj, dr * 1024:(dr + 1) * 1024], PREV[dr, n], [], [kpv])
                for g in range(2):
                    MM(P, bank(0)[:, g * 128:(g + 1) * 128], xv[:, 8 + g, tsl], xv[:, 10 + g, tsl], True, True, [kx], [("ps", 0)])
                CP(P, "act", cbt[:, cj, :], bank(0)[:, 0:256], [], [("ps", 0), kcb])
                pX = bank(1).bitcast(BF16)
                for k in range(8):
                    TR(P, pX[:, k * 128:(k + 1) * 128], xv[:, k, tsl], C["ident"], [kx], [("ps", 1)])
                for dr in range(2):
                    TT(P, "dve", xdt[:, cj, dr * 1024:(dr + 1) * 1024].rearrange("p (h q) -> p h q", h=16),
                       pX[:, 0:1024].rearrange("p (h q) -> p h q", h=16),
                       dt[:, n, dr * 16:(dr + 1) * 16].unsqueeze(2).to_broadcast([128, 16, 64]), ALU.mult,
                       [], [("ps", 1), kxd])
                def chain_(hq, li0):
                    li = li0
                    g = hq // 2
                    qj = hq % 2
                    km, kc = ("MT", qj), ("Cp", qj)
                    for dr in range(2):
                        lj = li % 2
                        li += 1
                        bA = 2 + lj
                        kt, kt2, kl, ke = ("tmp", lj), ("tmp2", lj), ("LT", lj), ("Eh", lj)
                        pA = bank(bA)
                        for i4 in range(4):
                            h = hq * 4 + i4
                            MM(P, pA[:, i4 * 128:(i4 + 1) * 128], rsel[0:16, h * 128:(h + 1) * 128],
                               acT[0:16, dr, n * 128:(n + 1) * 128], True, True, ["rsel"], [("ps", bA)])
                        v3 = lambda ap: ap.rearrange("p (h l) -> p h l", h=4)
                        TT(P, "dve", v3(tmp[:, lj, :]), v3(pA),
                           nacum[:, n, dr * 16 + hq * 4:dr * 16 + hq * 4 + 4].unsqueeze(2).to_broadcast([128, 4, 128]),
                           ALU.add, [], [("ps", bA), kt])
                        ACT(P, Eh[:, lj, :], pA, AF.Exp, [], [("ps", bA), ke])
                        TT(P, "pool", v3(tmp2[:, lj, :]), v3(tmp[:, lj, :]),
                           negm[:, dr, :].unsqueeze(1).to_broadcast([128, 4, 128]), ALU.add, [kt, "negm"], [kt2])
                        ACT(P, LT[:, lj, :], tmp2[:, lj, :], AF.Exp, [kt2], [kl])
                        TT(P, "dve", v3(MT[:, qj, dr * 512:(dr + 1) * 512]), v3(LT[:, lj, :]),
                           cbt[:, cj, g * 128:(g + 1) * 128].unsqueeze(1).to_broadcast([128, 4, 128]), ALU.mult,
                           [kl, kcb], [km])
                        TT(P, "pool", v3(Cp[:, qj, dr * 512:(dr + 1) * 512]), v3(Eh[:, lj, :]),
                           xv[:, 10 + g, tsl].unsqueeze(1).to_broadcast([128, 4, 128]), ALU.mult, [kx, ke], [kc])
                def mm_(hq):
                    qj = hq % 2
                    km, kc = ("MT", qj), ("Cp", qj)
                    for i4 in range(4):
                        h = hq * 4 + i4
                        bY = 4 + (h // 8)
                        pY = bank(bY)[(h % 2) * 64:(h % 2 + 1) * 64, ((h // 2) % 4) * 128:((h // 2) % 4 + 1) * 128]
                        for dr in range(2):
                            o = dr * 512 + i4 * 128
                            MM(P, pY, xdt[:, cj, dr * 1024 + h * 64:dr * 1024 + (h + 1) * 64], MT[:, qj, o:o + 128], dr == 0, False,
                               [kxd, km], [("ps", bY)])
                            MM(P, pY, pv[:, cj, dr * 1024 + h * 64:dr * 1024 + (h + 1) * 64], Cp[:, qj, o:o + 128], False, dr == 1,
                               [kpv, kc], [("ps", bY)])
                chain_(0, li)
                for hq in range(4):
                    if hq + 1 < 4:
                        chain_(hq + 1, li + 2 * (hq + 1))
                    mm_(hq)
                li += 8
                yv = ysb[:, cj, :].rearrange("p (k t) -> p k t", k=8)
                ky = ("ysb", cj)
                for k in range(8):
                    pYk = bank(4 + k // 4)[:, (k % 4) * 128:(k % 4 + 1) * 128]
                    STT(P, yv[:, k, :], xv[:, k, tsl], dsk[:, k:k + 1], pYk, ALU.mult, ALU.add, [kx, "dsk"], [("ps", 4 + k // 4), ky])
                TT(P, "dve", yv, yv, zv[:, :, tsl], ALU.mult, [kz], [ky])
                ACT(P, sq[:, cj, :], ysb[:, cj, :], AF.Square, [ky], [("sq", cj)])
                for k in range(8):
                    MM(P, bank(6, 128), C["ones"], sq[:, cj, k * 128:(k + 1) * 128], k == 0, k == 7, [("sq", cj)], [("ps", 6)])
                ACT(P, rt[:, cj, :], bank(6, 128), AF.Sqrt, [], [("ps", 6), ("rt", cj)], bias=EPS, scale=1.0 / 1024)
                RECIP(P, rt[:, cj, :], rt[:, cj, :], [], [("rt", cj)])
                TT(P, "dve", yv, yv, rt[:, cj, :].unsqueeze(1).to_broadcast([128, 8, 128]), ALU.mult, [("rt", cj)], [ky])
                TT(P, "dve", ov[:, :, tsl], yv, snT[:].unsqueeze(2).to_broadcast([128, 8, 128]), ALU.mult, ["snT"], [ky, ko])
            DMA(P, "sp", MIXT[0:1024, t0:t0 + 512].rearrange("(c p) t -> p c t", p=128), ov, [ko], [])


def ssd_host_params(conv_w, conv_b, dt_bias, a_log, d_skip, ssd_norm):
    p = {}
    p["convwT"] = np.ascontiguousarray(conv_w.reshape(5, 12, 128).transpose(2, 1, 0).reshape(128, 60))
    p["convbT"] = np.ascontiguousarray(conv_b.reshape(12, 128).T)
    p["dtb"] = np.ascontiguousarray(dt_bias.reshape(1, 32))
    p["alog"] = np.ascontiguousarray(a_log.reshape(1, 32))
    p["dskT"] = np.ascontiguousarray(np.repeat(d_skip, 64).reshape(8, 128).T)
    p["snT"] = np.ascontiguousarray(ssd_norm.reshape(8, 128).T)
    return p


def make_rsel():
    r = np.zeros((16, 16, 128), np.float32)
    for i in range(16):
        r[i, i, :] = 1.0
    return r.reshape(16, 16 * 128)


NCORES = 4
EVEN_BLOCKS = [
    dict(col0=0, width=512, mode="fm", dst="EFM", off=0, dt=BF16),
    dict(col0=512, width=512, mode="fm", dst="EFM", off=512, dt=BF16),
    dict(col0=1024, width=1024, mode="tm", dst="ETM", off=0, dt=BF16),
    dict(col0=2048, width=1024, mode="fm", dst="EFM", off=1024, dt=BF16, silu=True),
    dict(col0=3072, width=32, mode="fm", dst="LRT", off=0, dt=F32),
    dict(col0=3104, width=1024, mode="fm", dst="EFM", off=2048, dt=BF16),
    dict(col0=4128, width=1024, mode="fm", dst="EFM", off=3072, dt=BF16),
    dict(col0=5152, width=1024, mode="tm", dst="ETM", off=1024, dt=BF16),
]
ODD_BLOCKS = [
    dict(col0=0, width=1024, mode="fm", dst="OFM", off=0, dt=BF16, silu=True),
    dict(col0=1024, width=1536, mode="fm", dst="OFM", off=1024, dt=BF16),
    dict(col0=2560, width=32, mode="tm", dst="DTR", off=0, dt=F32),
    dict(col0=2592, width=1024, mode="fm", dst="OFM", off=2560, dt=BF16),
    dict(col0=3616, width=256, mode="fm", dst="OFM", off=3584, dt=BF16),
    dict(col0=3872, width=256, mode="tm", dst="OTM", off=0, dt=BF16),
]

INPUT_SHAPES = {
    "x": [T_FULL, D], "norm_gains": [4, 4, D], "gainsT": [128, 256],
    "ffn_w_gate": [4, D, HID], "ffn_w_up": [4, D, HID], "ffn_w_down": [4, HID, D],
    "even_w_in": [2, D, EVEN_IN], "even_w_out": [2, D, D], "odd_w_in": [2, D, ODD_IN], "odd_w_out": [2, D, D],
    "wdec": [2, 32, 2, 512], "bdecT": [2, 128, 8], "gnT": [2, 128, 8], "nab": [2, 8, 128, 3200],
    "convwT": [2, 128, 60], "convbT": [2, 128, 12], "dtb": [2, 1, 32], "alog": [2, 1, 32], "dskT": [2, 128, 8],
    "snT": [2, 128, 8], "sink": [2, 1, 8],
    "c_ident": [128, 128], "c_ones": [128, 128], "c_rotm": [128, 128], "c_mprev": [128, 128], "c_mnext": [128, 128],
    "c_cosT": [128, T_FULL], "c_sinT": [128, T_FULL], "c_rsel": [16, 2048],
}


def build_program(layers=(0, 1, 2, 3)):
    T = T_FULL
    NB = T // 128
    nc = bass.Bass("TRN2", target_bir_lowering=False)
    P = Prog(nc)
    I = {k: nc.dram_tensor(k, v, F32, kind="ExternalInput").ap() for k, v in INPUT_SHAPES.items()}
    out = nc.dram_tensor("out", [T, D], F32, kind="ExternalOutput").ap()
    S = {
        "XS": nc.dram_tensor("XS", [T, D], F32, kind="Internal").ap(),
        "EFM": nc.dram_tensor("EFM", [4096, T], BF16, kind="Internal").ap(),
        "ETM": nc.dram_tensor("ETM", [T, 2048], BF16, kind="Internal").ap(),
        "LRT": nc.dram_tensor("LRT", [32, T], F32, kind="Internal").ap(),
        "OFM": nc.dram_tensor("OFM", [3840, T], BF16, kind="Internal").ap(),
        "OTM": nc.dram_tensor("OTM", [T, 256], BF16, kind="Internal").ap(),
        "DTR": nc.dram_tensor("DTR", [T, 32], F32, kind="Internal").ap(),
        "MIXT": nc.dram_tensor("MIXT", [D, T], BF16, kind="Internal").ap(),
        "XBCS": nc.dram_tensor("XBCS", [1536, T], BF16, kind="Internal").ap(),
        "STATES": nc.dram_tensor("STATES", [2, NB, 128, 1024], F32, kind="Internal").ap(),
        "PREV": nc.dram_tensor("PREV", [2, NB, 128, 1024], BF16, kind="Internal").ap(),
    }
    C = {}
    gT = sb(P.outer, nc, "gTall", [128, 256], F32)
    with P.phase() as st:
        for k in ["ident", "ones", "rotm", "mprev", "mnext"]:
            t = sb(P.outer, nc, "cb_" + k, [128, 128], BF16)
            DMA(P, "pool", t[:], I["c_" + k], [], [k])
            C[k] = t[:]
        for k in ["ones", "mprev", "mnext"]:
            t = sb(P.outer, nc, "cf_" + k, [128, 128], F32)
            DMA(P, "sp", t[:], I["c_" + k], [], [k + "32"])
            C[k + "32"] = t[:]
        DMA(P, "sp", gT[:], I["gainsT"], [], ["gT"])
    first, last = layers[0], layers[-1]
    for L in layers:
        i = L // 2
        x_src = I["x"] if L == first else S["XS"]
        x_dst = out if L == last else S["XS"]
        g = lambda a: gT[:, (L * 4 + a) * 16:(L * 4 + a + 1) * 16]
        if L % 2 == 0:
            blocks = [dict(b, dst=S[b["dst"]]) for b in EVEN_BLOCKS]
            phase_A(P, C, x_src, g(0), I["even_w_in"][i], blocks, T)
            phase_GLA(P, C, S["EFM"], S["ETM"], S["LRT"], S["MIXT"], I["wdec"][i], I["bdecT"][i], I["gnT"][i], T)
            phase_NA(P, C, S["EFM"], S["ETM"], S["MIXT"], I["nab"][i], T)
            Wout = I["even_w_out"][i]
        else:
            blocks = [dict(b, dst=S[b["dst"]]) for b in ODD_BLOCKS]
            phase_A(P, C, x_src, g(0), I["odd_w_in"][i], blocks, T)
            prm = {k: I[k][i] for k in ["convwT", "convbT", "dtb", "alog", "dskT", "snT"]}
            prm["rsel"] = I["c_rsel"]
            phase_SSD(P, C, S["OFM"], S["DTR"], S["MIXT"], S["XBCS"], S["STATES"], S["PREV"], prm, T)
            phase_SWA(P, C, S["OFM"], S["OTM"], S["MIXT"], I["c_cosT"], I["c_sinT"], I["sink"][i], T)
            Wout = I["odd_w_out"][i]
        phase_CD(P, C, S["MIXT"], x_src, x_dst, I["norm_gains"][L, 1:2, :], g(2), I["norm_gains"][L, 3:4, :],
                 Wout, I["ffn_w_gate"][L], I["ffn_w_up"][L], I["ffn_w_down"][L], T)
    n_ops = P.n_ops
    P.close()
    return nc, n_ops


def host_inputs(inp):
    f = lambda a: np.ascontiguousarray(np.asarray(a, dtype=np.float32))
    h = {}
    for k in ["norm_gains", "ffn_w_gate", "ffn_w_up", "ffn_w_down", "even_w_in", "even_w_out", "odd_w_in", "odd_w_out"]:
        h[k] = f(inp[k])
    g = f(inp["norm_gains"])
    h["gainsT"] = f(g.reshape(4, 4, 16, 128).transpose(3, 0, 1, 2).reshape(128, 256))
    wd = f(inp["gla_w_decay"])
    wdec = np.zeros((2, 32, 2, 512), np.float32)
    wdec[:, 0:16, 0, :] = wd[:, 0]
    wdec[:, 16:32, 1, :] = wd[:, 1]
    h["wdec"] = wdec
    bd = f(inp["gla_b_decay"])
    h["bdecT"] = f(bd.reshape(2, 2, 4, 128).transpose(0, 3, 1, 2).reshape(2, 128, 8))
    h["gnT"] = f(f(inp["gla_norm"]).reshape(2, 8, 128).transpose(0, 2, 1))
    rpb = f(inp["na_rpb"])
    h["nab"] = np.stack([na_bias_tables(rpb[i]) for i in range(2)])
    sp = [ssd_host_params(f(inp["ssd_conv_w"])[i], f(inp["ssd_conv_b"])[i], f(inp["ssd_dt_bias"])[i],
                          f(inp["ssd_a_log"])[i], f(inp["ssd_d"])[i], f(inp["ssd_norm"])[i]) for i in range(2)]
    for k in ["convwT", "convbT", "dtb", "alog", "dskT", "snT"]:
        h[k] = np.stack([sp[i][k] for i in range(2)])
    h["sink"] = f(inp["swa_sink"]).reshape(2, 1, 8)
    for k, v in make_consts().items():
        h["c_" + k] = v
    h["c_rsel"] = make_rsel()
    return h


_CACHE = {}


def kernel(**inputs):
    x = np.asarray(inputs["x"], dtype=np.float32)
    h = host_inputs(inputs)
    if "nc" not in _CACHE:
        _CACHE["nc"] = build_program()[0]
    nc = _CACHE["nc"]
    in_maps = []
    for c in range(NCORES):
        m = dict(h)
        m["x"] = np.ascontiguousarray(x[c % 4])
        in_maps.append(m)
    res = run_bass_kernel_spmd(nc, in_maps, core_ids=list(range(NCORES)))
    return np.stack([np.asarray(res.results[c]["out"], dtype=np.float32) for c in range(4)], axis=0)
```

```python
import contextlib
import numpy as np
import concourse.bass as bass
import concourse.mybir as mybir
from concourse.bass_utils import run_bass_kernel_spmd

F32 = mybir.dt.float32
BF16 = mybir.dt.bfloat16
AF = mybir.ActivationFunctionType
ALU = mybir.AluOpType

D = 2048
T_FULL = 4096
HID = 5632
EPS = 1e-6
EVEN_IN = 6176
ODD_IN = 4128

ENGS = ["pe", "act", "dve", "pool", "sp"]
DMA_ENGS = ["sp", "pool", "act"]
NDMASEM = 12


class Prog:
    def __init__(self, nc):
        self.nc = nc
        self.outer = contextlib.ExitStack()
        self.sems = {}
        for e in ENGS:
            self.sems[e] = self.outer.enter_context(nc.semaphore("s_" + e))
        for e in DMA_ENGS:
            for j in range(NDMASEM):
                self.sems[("dma", e, j)] = self.outer.enter_context(nc.semaphore(f"d_{e}_{j}"))
        self.cnt = {e: 0 for e in ENGS}
        self.dma_cnt = {}
        self.dma_rr = {e: 0 for e in ENGS}
        self.known = {e: {} for e in ENGS}
        self.n_ops = 0
        self._reset()

    def _reset(self):
        self.ops = {e: [] for e in ENGS}
        self.last_w = {}
        self.readers = {}

    def op(self, eng, fn, reads=(), writes=(), dma=False, signal=True):
        deps = {}

        def note(t):
            sk, v, src = t
            cur = deps.get(sk)
            if cur is None or cur[0] < v:
                deps[sk] = (v, src)

        for k in reads:
            t = self.last_w.get(k)
            if t is not None:
                note(t)
        for k in writes:
            t = self.last_w.get(k)
            if t is not None:
                note(t)
            for t in self.readers.get(k, ()):
                note(t)
        if dma:
            j = self.dma_rr[eng]
            self.dma_rr[eng] = (j + 1) % NDMASEM
            sk = ("dma", eng, j)
            c = self.dma_cnt.get(sk, 0)
            if c > 0:
                note((sk, 16 * c, "dma"))
            self.dma_cnt[sk] = c + 1
            tok = (sk, 16 * (c + 1), "dma")
            inc = (sk, 16)
        elif signal:
            self.cnt[eng] += 1
            tok = (eng, self.cnt[eng], eng)
            inc = (eng, 1)
        else:
            tok = (eng, self.cnt[eng] + 1, eng)
            inc = None
        waits = []
        kn = self.known[eng]
        for sk, (v, src) in deps.items():
            if src == "pe" and eng == "pe" and not dma:
                continue
            if kn.get(sk, 0) >= v:
                continue
            kn[sk] = v
            waits.append((sk, v))
        self.ops[eng].append((fn, waits, inc))
        self.n_ops += 1
        for k in reads:
            self.readers.setdefault(k, []).append(tok)
        for k in writes:
            self.last_w[k] = tok
            self.readers[k] = []
        return tok

    @contextlib.contextmanager
    def phase(self):
        st = contextlib.ExitStack()
        self._reset()
        try:
            yield st
            self._flush()
        finally:
            st.close()

    def _flush(self):
        nc = self.nc
        bar = [(e, self.cnt[e]) for e in ENGS if self.cnt[e] > 0]
        bar += [(sk, 16 * c) for sk, c in self.dma_cnt.items() if c > 0]
        sems = self.sems
        with nc.Block() as block:
            def run(e):
                def body(eng):
                    for (fn, waits, inc) in self.ops[e]:
                        for sk, v in waits:
                            eng.wait_ge(sems[sk], v)
                        ins = fn(eng)
                        if inc is not None:
                            ins.then_inc(sems[inc[0]], inc[1])
                    kn = self.known[e]
                    for sk, v in bar:
                        if kn.get(sk, 0) < v:
                            eng.wait_ge(sems[sk], v)
                            kn[sk] = v
                return body

            block.tensor(run("pe"))
            block.scalar(run("act"))
            block.vector(run("dve"))
            block.gpsimd(run("pool"))
            block.sync(run("sp"))
        self._reset()

    def close(self):
        self.outer.close()


_UID = [0]


def sb(st, nc, name, shape, dtype):
    _UID[0] += 1
    return st.enter_context(nc.sbuf_tensor(f"sb{_UID[0]}_{name}", list(shape), dtype))


def pst(st, nc, name, shape, dtype):
    _UID[0] += 1
    return st.enter_context(nc.psum_tensor(f"ps{_UID[0]}_{name}", list(shape), dtype))


def MM(P, out, lhsT, rhs, start, stop, r, w, signal=None):
    P.op("pe", lambda e: e.matmul(out, lhsT=lhsT, rhs=rhs, start=start, stop=stop), r, w,
         signal=(bool(stop) if signal is None else signal))


def TR(P, out, in_, ident, r, w):
    P.op("pe", lambda e: e.transpose(out=out, in_=in_, identity=ident), r, w)


def ACT(P, out, in_, func, r, w, bias=None, scale=None, accum=None):
    kw = {}
    if bias is not None:
        kw["bias"] = bias
    if scale is not None:
        kw["scale"] = scale
    if accum is not None:
        kw["accum_out"] = accum
    P.op("act", lambda e: e.activation(out=out, in_=in_, func=func, **kw), r, w)


def TT(P, eng, out, in0, in1, op, r, w):
    P.op(eng, lambda e: e.tensor_tensor(out=out, in0=in0, in1=in1, op=op), r, w)


def TS(P, eng, out, in0, s1, s2, op0, op1, r, w):
    if s2 is None:
        P.op(eng, lambda e: e.tensor_scalar(out=out, in0=in0, scalar1=s1, scalar2=None, op0=op0), r, w)
    else:
        P.op(eng, lambda e: e.tensor_scalar(out=out, in0=in0, scalar1=s1, scalar2=s2, op0=op0, op1=op1), r, w)


def STT(P, out, in0, scalar, in1, op0, op1, r, w):
    P.op("dve", lambda e: e.scalar_tensor_tensor(out=out, in0=in0, scalar=scalar, in1=in1, op0=op0, op1=op1), r, w)


def CP(P, eng, out, in_, r, w):
    if eng == "act":
        P.op("act", lambda e: e.copy(out=out, in_=in_), r, w)
    else:
        P.op(eng, lambda e: e.tensor_copy(out=out, in_=in_), r, w)


def RECIP(P, out, in_, r, w):
    P.op("dve", lambda e: e.reciprocal(out=out, in_=in_), r, w)


def DMA(P, eng, out, in_, r, w):
    P.op(eng, lambda e: e.dma_start(out=out, in_=in_), r, w, dma=True)


def MEMSET(P, eng, ap, val, r, w):
    P.op(eng, lambda e: e.memset(ap, val), r, w)


def rstd_from_ss(P, col, n, key):
    ACT(P, col[:, 1:2], col[:, 0:1], AF.Sqrt, [key], [key], bias=EPS, scale=1.0 / n)
    RECIP(P, col[:, 2:3], col[:, 1:2], [key], [key])


def phase_A(P, C, x_src, gT, W, blocks, T):
    nc = P.nc
    TA = 1024 if T >= 1024 else T
    NSUB = TA // 128
    NH = TA // 512
    with P.phase() as st:
        xa = sb(st, nc, "xa", [128, 2, D], F32)
        hbf = sb(st, nc, "hbfA", [128, 2, D], BF16)
        hT = sb(st, nc, "hT", [128, 16, TA], BF16)
        wsl = sb(st, nc, "wslA", [128, 4, 8192], BF16)
        stg = sb(st, nc, "stgA", [128, 4, 512], F32)
        stb = sb(st, nc, "stbA", [128, 4, 512], BF16)
        col = sb(st, nc, "colA", [128, 2, 4], F32)
        ps = pst(st, nc, "psA", [128, 4096], F32)
        slot_i = 0
        stg_i = 0
        bank_i = 0
        ev_i = 0
        for tile in range(T // TA):
            for sub in range(NSUB):
                j = sub % 2
                t0 = tile * TA + sub * 128
                kx, kh, kc = ("xa", j), ("hbf", j), ("col", j)
                DMA(P, "sp", xa[:, j, :], x_src[t0:t0 + 128, :], [], [kx])
                ACT(P, hbf[:, j, :], xa[:, j, :], AF.Square, [kx], [kh, kc], accum=col[:, j, 0:1])
                rstd_from_ss(P, col[:, j, :], D, kc)
                TS(P, "dve", hbf[:, j, :], xa[:, j, :], col[:, j, 2:3], None, ALU.mult, None, [kx, kc], [kh])
                for kg in range(4):
                    bank = 6 + (kg % 2)
                    kb = ("ps", bank)
                    ptb = ps[:, bank * 512:(bank + 1) * 512].bitcast(BF16)
                    for kk in range(4):
                        k = kg * 4 + kk
                        TR(P, ptb[:, kk * 128:(kk + 1) * 128], hbf[:, j, k * 128:(k + 1) * 128], C["ident"], [kh], [kb])
                    TT(P, "dve", hT[:, kg * 4:(kg + 1) * 4, sub * 128:(sub + 1) * 128],
                       ptb[:, 0:512].rearrange("p (a b) -> p a b", a=4),
                       gT[:, kg * 4:(kg + 1) * 4].unsqueeze(2).to_broadcast([128, 4, 128]), ALU.mult,
                       [], [kb, ("hT", sub // 4)])
            for blk in blocks:
                for g0 in range(0, blk["width"], 512):
                    gw = min(512, blk["width"] - g0)
                    s = slot_i % 4
                    slot_i += 1
                    ks = ("wsl", s)
                    wv = wsl[:, s, 0:16 * gw].rearrange("p (k n) -> p k n", k=16)
                    c0w = blk["col0"] + g0
                    DMA(P, "pool", wv, W[:, c0w:c0w + gw].rearrange("(k p) n -> p k n", p=128), [], [ks])
                    if blk["mode"] == "fm":
                        for c0 in range(0, gw, 128):
                            cw = min(128, gw - c0)
                            for half in range(NH):
                                bank = bank_i % 6
                                bank_i += 1
                                kb = ("ps", bank)
                                pb = ps[0:cw, bank * 512:(bank + 1) * 512]
                                for k in range(16):
                                    MM(P, pb, wv[:, k, c0:c0 + cw], hT[:, k, half * 512:(half + 1) * 512],
                                       k == 0, k == 15, [ks, ("hT", half)], [kb])
                                si = stg_i % 4
                                stg_i += 1
                                kst = ("stg", si)
                                dsb = stb[0:cw, si, :] if blk["dt"] == BF16 else stg[0:cw, si, :]
                                if blk.get("silu"):
                                    ACT(P, dsb, pb, AF.Silu, [], [kb, kst])
                                else:
                                    ev_i += 1
                                    CP(P, "act" if ev_i % 2 else "dve", dsb, pb, [], [kb, kst])
                                r0 = blk["off"] + g0 + c0
                                tt0 = tile * TA + half * 512
                                DMA(P, "sp", blk["dst"][r0:r0 + cw, tt0:tt0 + 512], dsb, [kst], [])
                    else:
                        for sub in range(NSUB):
                            bank = bank_i % 6
                            bank_i += 1
                            kb = ("ps", bank)
                            pb = ps[:, bank * 512:bank * 512 + gw]
                            for k in range(16):
                                MM(P, pb, hT[:, k, sub * 128:(sub + 1) * 128], wv[:, k, 0:gw],
                                   k == 0, k == 15, [ks, ("hT", sub // 4)], [kb])
                            si = stg_i % 4
                            stg_i += 1
                            kst = ("stg", si)
                            dsb = stb[:, si, 0:gw] if blk["dt"] == BF16 else stg[:, si, 0:gw]
                            ev_i += 1
                            CP(P, "act" if ev_i % 2 else "dve", dsb, pb, [], [kb, kst])
                            tt0 = tile * TA + sub * 128
                            cc0 = blk["off"] + g0
                            DMA(P, "sp", blk["dst"][tt0:tt0 + 128, cc0:cc0 + gw], dsb, [kst], [])


def phase_CD(P, C, MIXT, x_src, x_dst, g1row, g2T, g3row, Wout, Wg, Wu, Wd, T, do_ffn=True):
    nc = P.nc
    NKH = HID // 128
    with P.phase() as st:
        aT = sb(st, nc, "aT", [128, 16, 512], BF16)
        actT = sb(st, nc, "actT", [128, NKH, 512], BF16)
        xt = sb(st, nc, "xt", [128, 2, D], F32)
        f = sb(st, nc, "f", [128, 4, D], F32)
        gb = sb(st, nc, "gb", [128, D], F32)
        hbf = sb(st, nc, "hbfC", [128, 2, D], BF16)
        wsl = sb(st, nc, "wslC", [128, 4, 8192], BF16)
        sg = sb(st, nc, "sg", [128, 2, 512], F32)
        ss = sb(st, nc, "ss", [128, 4, 8], F32)
        col = sb(st, nc, "colC", [128, 2, 4], F32)
        ps = pst(st, nc, "psC", [128, 4096], F32)
        state = {"slot": 0}

        junk = sb(st, nc, "junkC", [128, 512], BF16)

        def tok_proj(src, nk, Wdram, srckey):
            for cg in range(4):
                for kg0 in range(0, nk, 16):
                    kn = min(16, nk - kg0)
                    s = state["slot"] % 4
                    state["slot"] += 1
                    ks = ("wsl", s)
                    wv = wsl[:, s, 0:kn * 512].rearrange("p (k n) -> p k n", k=kn)
                    DMA(P, "pool", wv,
                        Wdram[kg0 * 128:(kg0 + kn) * 128, cg * 512:(cg + 1) * 512].rearrange("(k p) n -> p k n", p=128),
                        [], [ks])
                    for k in range(kn):
                        kk = kg0 + k
                        for sub in range(4):
                            bk = (cg % 2) * 4 + sub
                            MM(P, ps[:, bk * 512:(bk + 1) * 512], src[:, kk, sub * 128:(sub + 1) * 128], wv[:, k, :],
                               kk == 0, kk == nk - 1, [ks, srckey], [("ps", bk)],
                               signal=(kk == nk - 1) or (k == kn - 1 and sub == 3))
                for sub in range(4):
                    bk = (cg % 2) * 4 + sub
                    pb = ps[:, bk * 512:(bk + 1) * 512]
                    ACT(P, junk[:], pb, AF.Square, [], [("ps", bk), ("ss", sub)],
                        accum=ss[:, sub, cg:cg + 1])
                    TT(P, "dve", f[:, sub, cg * 512:(cg + 1) * 512], pb, gb[:, cg * 512:(cg + 1) * 512], ALU.mult,
                       ["gb"], [("ps", bk), ("f", sub)])

        def finish_rstd(sub):
            k = ("ss", sub)
            P.op("dve", lambda e: e.tensor_reduce(out=ss[:, sub, 4:5], in_=ss[:, sub, 0:4], axis=mybir.AxisListType.X,
                                                   op=ALU.add), [k], [k])
            ACT(P, ss[:, sub, 5:6], ss[:, sub, 4:5], AF.Sqrt, [k], [k], bias=EPS, scale=1.0 / D)
            RECIP(P, ss[:, sub, 6:7], ss[:, sub, 5:6], [k], [k])

        for tile in range(T // 512):
            t0 = tile * 512
            DMA(P, "sp", aT[:], MIXT[:, t0:t0 + 512].rearrange("(k p) t -> p k t", p=128), [], ["aT"])
            DMA(P, "sp", gb[:], g1row.partition_broadcast(128), [], ["gb"])
            tok_proj(aT, 16, Wout, "aT")
            for sub in range(4):
                j = sub % 2
                kx, kh, kc = ("xt", j), ("hbf", j), ("col", j)
                tt0 = t0 + sub * 128
                kd = ("xd", tile, sub)
                finish_rstd(sub)
                DMA(P, "sp", xt[:, j, :], x_src[tt0:tt0 + 128, :], [kd], [kx])
                STT(P, xt[:, j, :], f[:, sub, :], ss[:, sub, 6:7], xt[:, j, :], ALU.mult, ALU.add,
                    [("f", sub), ("ss", sub), kx], [kx])
                DMA(P, "sp", x_dst[tt0:tt0 + 128, :], xt[:, j, :], [kx], [kd])
                if not do_ffn:
                    continue
                ACT(P, hbf[:, j, :], xt[:, j, :], AF.Square, [kx], [kh, kc], accum=col[:, j, 0:1])
                rstd_from_ss(P, col[:, j, :], D, kc)
                TS(P, "dve", hbf[:, j, :], xt[:, j, :], col[:, j, 2:3], None, ALU.mult, None, [kx, kc], [kh])
                for kg in range(4):
                    bank = 4 + 2 * (kg % 2)
                    kb = ("ps", bank)
                    ptb = ps[:, bank * 512:(bank + 1) * 512].bitcast(BF16)
                    for kk in range(4):
                        k = kg * 4 + kk
                        TR(P, ptb[:, kk * 128:(kk + 1) * 128], hbf[:, j, k * 128:(k + 1) * 128], C["ident"], [kh], [kb])
                    TT(P, "dve", aT[:, kg * 4:(kg + 1) * 4, sub * 128:(sub + 1) * 128],
                       ptb[:, 0:512].rearrange("p (a b) -> p a b", a=4),
                       g2T[:, kg * 4:(kg + 1) * 4].unsqueeze(2).to_broadcast([128, 4, 128]), ALU.mult,
                       [], [kb, "aT"])
            if not do_ffn:
                continue
            DMA(P, "sp", gb[:], g3row.partition_broadcast(128), [], ["gb"])
            for hg in range(HID // 512):
                sl = []
                for Wm in (Wg, Wu):
                    s = state["slot"] % 4
                    state["slot"] += 1
                    wv = wsl[:, s, 0:8192].rearrange("p (k n) -> p k n", k=16)
                    DMA(P, "pool", wv, Wm[:, hg * 512:(hg + 1) * 512].rearrange("(k p) n -> p k n", p=128), [], [("wsl", s)])
                    sl.append((wv, ("wsl", s)))
                for c in range(4):
                    hc = hg * 4 + c
                    bG = 4 + 2 * (hc % 2)
                    bU = bG + 1
                    for (wv, ks), b in zip(sl, (bG, bU)):
                        for k in range(16):
                            MM(P, ps[:, b * 512:(b + 1) * 512], wv[:, k, c * 128:(c + 1) * 128], aT[:, k, :],
                               k == 0, k == 15, [ks, "aT"], [("ps", b)])
                    ACT(P, sg[:, hc % 2, :], ps[:, bG * 512:(bG + 1) * 512], AF.Silu, [], [("ps", bG), ("sg", hc % 2)])
                    TT(P, "dve", actT[:, hc, :], sg[:, hc % 2, :], ps[:, bU * 512:(bU + 1) * 512], ALU.mult,
                       [("sg", hc % 2)], [("ps", bU), "actT"])
            tok_proj(actT, NKH, Wd, "actT")
            for sub in range(4):
                j = sub % 2
                kx = ("xt", j)
                tt0 = t0 + sub * 128
                kd = ("xd", tile, sub)
                finish_rstd(sub)
                DMA(P, "sp", xt[:, j, :], x_dst[tt0:tt0 + 128, :], [kd], [kx])
                STT(P, xt[:, j, :], f[:, sub, :], ss[:, sub, 6:7], xt[:, j, :], ALU.mult, ALU.add,
                    [("f", sub), ("ss", sub), kx], [kx])
                DMA(P, "sp", x_dst[tt0:tt0 + 128, :], xt[:, j, :], [kx], [kd])


def phase_SWA(P, C, OFM, OTM, MIXT, cosT_d, sinT_d, sink_row, T):
    nc = P.nc
    NB = T // 128
    SC = 128 ** -0.5
    with P.phase() as st:
        cosT = sb(st, nc, "cosT", [128, T], F32)
        sinT = sb(st, nc, "sinT", [128, T], F32)
        raw = sb(st, nc, "raw", [128, 2, 512], BF16)
        t1 = sb(st, nc, "t1", [128, 2, 512], F32)
        t2 = sb(st, nc, "t2", [128, 2, 512], F32)
        qrot = sb(st, nc, "qrot", [128, 4, T], BF16)
        krot = sb(st, nc, "krot", [128, T], BF16)
        V = sb(st, nc, "Vswa", [128, NB, 128], BF16)
        pT = sb(st, nc, "pT", [128, 6, 512], BF16)
        es = sb(st, nc, "es", [128, 8], F32)
        den = sb(st, nc, "den", [128, 2, 512], F32)
        ost = sb(st, nc, "ost", [128, 4, 512], BF16)
        ps = pst(st, nc, "psW", [128, 4096], F32)
        DMA(P, "sp", cosT[:], cosT_d[:, 0:T], [], ["cos"])
        DMA(P, "sp", sinT[:], sinT_d[:, 0:T], [], ["sin"])
        DMA(P, "sp", es[:], sink_row.partition_broadcast(128), [], ["es"])
        ACT(P, es[:], es[:], AF.Exp, ["es"], ["es"])
        ri = 0
        pi = 0

        def rope(src_rows, dst_ap_fn, dkey):
            nonlocal ri
            for tb in range(T // 512):
                j = ri % 2
                ri += 1
                bank = 6 + j
                kb = ("ps", bank)
                pb = ps[:, bank * 512:(bank + 1) * 512]
                tsl = slice(tb * 512, (tb + 1) * 512)
                DMA(P, "sp", raw[:, j, :], OFM[src_rows:src_rows + 128, tsl], [], [("raw", j)])
                MM(P, pb, C["rotm"], raw[:, j, :], True, True, [("raw", j)], [kb])
                TT(P, "dve", t1[:, j, :], raw[:, j, :], cosT[:, tsl], ALU.mult, [("raw", j), "cos"], [("t1", j)])
                TT(P, "dve", t2[:, j, :], pb, sinT[:, tsl], ALU.mult, ["sin"], [kb, ("t2", j)])
                TT(P, "pool", dst_ap_fn(tsl), t1[:, j, :], t2[:, j, :], ALU.add, [("t1", j), ("t2", j)], [dkey])

        for g in range(2):
            rope(3584 + g * 128, lambda tsl: krot[:, tsl], "krot")
            for h in range(4):
                rope(2560 + (g * 4 + h) * 128, (lambda hh: (lambda tsl: qrot[:, hh, tsl]))(h), "qrot")
            DMA(P, "sp", V[:], OTM[:, g * 128:(g + 1) * 128].rearrange("(n p) c -> p n c", p=128), [], ["V"])
            def jsof(n):
                return [j for j in (n - 1, n, n + 1) if 0 <= j < NB]

            def st1(n):
                nonlocal pi
                for idx, jb in enumerate(jsof(n)):
                    bS = pi % 2
                    pi += 1
                    pslot = (n % 2) * 3 + idx
                    pS = ps[:, bS * 512:(bS + 1) * 512]
                    kp = ("pT", pslot)
                    MM(P, pS, krot[:, jb * 128:(jb + 1) * 128], qrot[:, :, n * 128:(n + 1) * 128], True, True,
                       ["krot", "qrot"], [("ps", bS)])
                    ACT(P, pT[:, pslot, :], pS, AF.Exp, [], [("ps", bS), kp], scale=SC)
                    if jb != n:
                        mk = C["mprev"] if jb < n else C["mnext"]
                        TT(P, "dve", pT[:, pslot, :].rearrange("p (h q) -> p h q", h=4),
                           pT[:, pslot, :].rearrange("p (h q) -> p h q", h=4),
                           mk.unsqueeze(1).to_broadcast([128, 4, 128]), ALU.mult, [], [kp])

            def st2(n, g=g):
                bO = 2 + (n % 2)
                bD = 4 + (n % 2)
                pO = ps[:, bO * 512:(bO + 1) * 512]
                pD = ps[:, bD * 512:(bD + 1) * 512]
                js = jsof(n)
                for idx, jb in enumerate(js):
                    pslot = (n % 2) * 3 + idx
                    kp = ("pT", pslot)
                    MM(P, pO, V[:, jb, :], pT[:, pslot, :], idx == 0, idx == len(js) - 1, ["V", kp], [("ps", bO)])
                    MM(P, pD, C["ones"], pT[:, pslot, :], idx == 0, idx == len(js) - 1, [kp], [("ps", bD)])
                dj = n % 2
                TT(P, "dve", den[:, dj, :].rearrange("p (h q) -> p h q", h=4), pD.rearrange("p (h q) -> p h q", h=4),
                   es[:, g * 4:(g + 1) * 4].unsqueeze(2).to_broadcast([128, 4, 128]), ALU.add,
                   ["es"], [("ps", bD), ("den", dj)])
                RECIP(P, den[:, dj, :], den[:, dj, :], [], [("den", dj)])
                TT(P, "dve", ost[:, :, (n % 4) * 128:(n % 4 + 1) * 128], pO.rearrange("p (h q) -> p h q", h=4),
                   den[:, dj, :].rearrange("p (h q) -> p h q", h=4), ALU.mult, [("den", dj)], [("ps", bO), "ost"])
                if n % 4 == 3:
                    n0 = n - 3
                    r0 = 1024 + g * 512
                    DMA(P, "sp", MIXT[r0:r0 + 512, n0 * 128:n0 * 128 + 512].rearrange("(h d) t -> d h t", d=128),
                        ost[:], ["ost"], [])

            st1(0)
            for n in range(NB):
                if n + 1 < NB:
                    st1(n + 1)
                st2(n)


def na_pattern(m):
    return 0 if m == 0 else 1 if m == 1 else 3 if m == 30 else 4 if m == 31 else 2


def phase_NA(P, C, EFM, ETM, MIXT, nab, T):
    nc = P.nc
    NB = T // 128
    SC = 128 ** -0.5
    assert T == 4096
    with P.phase() as st:
        qT = sb(st, nc, "naq", [128, 2, T], BF16)
        kT = sb(st, nc, "nak", [128, 2, T], BF16)
        V = sb(st, nc, "nav", [128, 2, NB * 128], BF16)
        bias = sb(st, nc, "nabias", [128, 2, 5 * 640], F32)
        sbf = sb(st, nc, "nasb", [128, 2, 640], F32)
        pT = sb(st, nc, "napT", [128, 2, 640], BF16)
        rden = sb(st, nc, "narden", [128, 2, 128], F32)
        ost = sb(st, nc, "naost", [128, 2, 512], BF16)
        ps = pst(st, nc, "psN", [128, 4096], F32)
        def keys(h):
            hj = h % 2
            return hj, ("q", hj), ("k", hj), ("v", hj), ("b", hj)

        def loads(h):
            hj, kq, kk, kv, kbz = keys(h)
            DMA(P, "sp", qT[:, hj, :], EFM[2048 + h * 128:2048 + (h + 1) * 128, :], [], [kq])
            DMA(P, "sp", kT[:, hj, :], EFM[3072 + h * 128:3072 + (h + 1) * 128, :], [], [kk])
            DMA(P, "sp", V[:, hj, :].rearrange("p (n c) -> p n c", c=128),
                ETM[:, 1024 + h * 128:1024 + (h + 1) * 128].rearrange("(n p) c -> p n c", p=128), [], [kv])
            DMA(P, "sp", bias[:, hj, :], nab[h], [], [kbz])

        iters = [(h, m) for h in range(8) for m in range(NB)]

        def stage1(idx):
            h, m = iters[idx]
            hj, kq, kk, kv, kbz = keys(h)
            u0 = min(max(m - 2, 0), NB - 5)
            pat = na_pattern(m)
            sj = idx % 2
            b0 = 2 * sj
            kS = ("ps", b0)
            pS = ps[:, b0 * 512:b0 * 512 + 640]
            for j in range(5):
                MM(P, pS[:, j * 128:(j + 1) * 128], kT[:, hj, (u0 + j) * 128:(u0 + j + 1) * 128],
                   qT[:, hj, m * 128:(m + 1) * 128], True, True, [kq, kk], [kS if j < 4 else ("ps", b0 + 1)])
            STT(P, sbf[:, sj, 0:512], pS[:, 0:512], SC, bias[:, hj, pat * 640:pat * 640 + 512], ALU.mult, ALU.add,
                [kbz], [kS, ("sbf", sj)])
            STT(P, sbf[:, sj, 512:640], pS[:, 512:640], SC, bias[:, hj, pat * 640 + 512:(pat + 1) * 640], ALU.mult, ALU.add,
                [kbz], [("ps", b0 + 1), ("sbf", sj)])
            ACT(P, pT[:, sj, :], sbf[:, sj, :], AF.Exp, [("sbf", sj)], [("pT", sj)])

        def stage2(idx):
            h, m = iters[idx]
            hj, kq, kk, kv, kbz = keys(h)
            u0 = min(max(m - 2, 0), NB - 5)
            sj = idx % 2
            bO = 4 + sj
            pO = ps[:, bO * 512:bO * 512 + 128]
            pD = ps[:, bO * 512 + 128:bO * 512 + 256]
            for j in range(5):
                MM(P, pO, V[:, hj, (u0 + j) * 128:(u0 + j + 1) * 128], pT[:, sj, j * 128:(j + 1) * 128],
                   j == 0, j == 4, [kv, ("pT", sj)], [("ps", bO)])
            for j in range(5):
                MM(P, pD, C["ones"], pT[:, sj, j * 128:(j + 1) * 128], j == 0, j == 4, [("pT", sj)], [("ps", bO)])
            RECIP(P, rden[:, sj, :], pD, [], [("ps", bO), ("rden", sj)])
            oj = (m // 4) % 2
            TT(P, "dve", ost[:, oj, (m % 4) * 128:(m % 4 + 1) * 128], pO, rden[:, sj, :], ALU.mult,
               [("rden", sj)], [("ps", bO), ("ost", oj)])
            if m % 4 == 3:
                n0 = m - 3
                DMA(P, "sp", MIXT[1024 + h * 128:1024 + (h + 1) * 128, n0 * 128:n0 * 128 + 512], ost[:, oj, :],
                    [("ost", oj)], [])

        loads(0)
        stage1(0)
        for idx in range(len(iters)):
            h, m = iters[idx]
            if m == 0 and h + 1 < 8:
                loads(h + 1)
            if idx + 1 < len(iters):
                stage1(idx + 1)
            stage2(idx)


def make_consts(T=T_FULL):
    c = {}
    c["ident"] = np.eye(128, dtype=np.float32)
    c["ones"] = np.ones((128, 128), np.float32)
    rot = np.zeros((128, 128), np.float32)
    for m in range(64):
        rot[m + 64, m] = -1.0
        rot[m, m + 64] = 1.0
    c["rotm"] = rot
    jj = np.arange(128)[:, None]
    ii = np.arange(128)[None, :]
    c["mprev"] = (ii <= jj).astype(np.float32)
    c["mnext"] = (jj <= ii).astype(np.float32)
    half = 64
    inv = (10000.0 ** (-np.arange(half, dtype=np.float32) / half)).astype(np.float32)
    ang = np.arange(T, dtype=np.float32)[None, :] * np.concatenate([inv, inv])[:, None]
    c["cosT"] = np.cos(ang).astype(np.float32)
    c["sinT"] = np.sin(ang).astype(np.float32)
    return c


def na_bias_tables(rpb):
    H = rpb.shape[0]
    out = np.full((H, 128, 5, 5, 128), -30000.0, np.float32)
    reps = [0, 1, 5, 30, 31]
    for p, m in enumerate(reps):
        u0 = min(max(m - 2, 0), 27)
        ktok = (u0 * 128 + np.arange(640)).reshape(5, 128)
        kr, kc = ktok // 64, ktok % 64
        qtok = m * 128 + np.arange(128)
        r, cq = qtok // 64, qtok % 64
        rs = np.clip(r - 4, 0, 56)
        cs = np.clip(cq - 8, 0, 48)
        KR = kr[:, :, None]
        KC = kc[:, :, None]
        valid = (KR >= rs[None, None, :]) & (KR < rs[None, None, :] + 8) & (KC >= cs[None, None, :]) & (KC < cs[None, None, :] + 16)
        dr = np.clip(KR - r[None, None, :] + 7, 0, 14)
        dc = np.clip(KC - cq[None, None, :] + 15, 0, 30)
        for h in range(H):
            vals = rpb[h][dr, dc]
            tab = np.where(valid, vals, np.float32(-30000.0)).astype(np.float32)
            out[h, :, p, :, :] = tab.transpose(1, 0, 2)
    return out.reshape(H, 128, 5 * 640)


def phase_GLA(P, C, EFM, ETM, LRT, MIXT, wdec_d, bdecT_d, gnT_d, T):
    nc = P.nc
    NB = T // 128
    NT5 = T // 512
    with P.phase() as st:
        lrT = sb(st, nc, "lrT", [32, T], F32)
        wdec = sb(st, nc, "wdec", [32, 2 * 512], F32)
        negb = sb(st, nc, "negb", [128, 8], F32)
        gnT = sb(st, nc, "gnT", [128, 8], F32)
        rmask = sb(st, nc, "rmask", [128, 512], F32)
        mk2 = sb(st, nc, "mk2", [128, 2, 128], BF16)
        cs = sb(st, nc, "cs", [128, 2, T], F32)
        tmpe = sb(st, nc, "tmpe", [128, 2, 512], F32)
        tmps = sb(st, nc, "tmps", [128, 2, 512], F32)
        tmpp = sb(st, nc, "tmpp", [128, 2, 512], F32)
        qk = sb(st, nc, "qkraw", [128, 2, T], BF16)
        qkt = sb(st, nc, "qkt", [128, 4, T], BF16)
        ktok = sb(st, nc, "ktok", [128, 2, NB * 128], BF16)
        Vp = sb(st, nc, "Vp", [128, NB, 256], BF16)
        Sprev = sb(st, nc, "Sprev", [128, 2, NB * 128], BF16)
        S = sb(st, nc, "Sst", [128, 2, 128], F32)
        tS = sb(st, nc, "tS", [128, 2, 128], F32)
        dl = sb(st, nc, "dl", [128, 2, NB], F32)
        ATm = sb(st, nc, "ATm", [128, 2, 512], BF16)
        sq = sb(st, nc, "sq", [128, 2, 512], BF16)
        rt = sb(st, nc, "rt", [128, 2, 512], F32)
        o1 = sb(st, nc, "o1", [128, 2, 512], F32)
        sgt = sb(st, nc, "sgt", [128, 2, 512], BF16)
        ostg = sb(st, nc, "ostg", [128, 2, 512], BF16)
        ps = pst(st, nc, "psG", [128, 4096], F32)

        def bank(i, w=512):
            return ps[:, i * 512:i * 512 + w]

        DMA(P, "sp", lrT[:], LRT, [], ["lrT"])
        DMA(P, "sp", wdec[:], wdec_d.rearrange("r d c -> r (d c)"), [], ["wdec"])
        DMA(P, "sp", negb[:], bdecT_d, [], ["negb"])
        TS(P, "dve", negb[:], negb[:], -1.0, None, ALU.mult, None, ["negb"], ["negb"])
        DMA(P, "sp", gnT[:], gnT_d, [], ["gnT"])
        MEMSET(P, "dve", rmask[:], 1.0, [], ["rmask"])
        for c in range(4):
            MEMSET(P, "dve", rmask[:, c * 128:c * 128 + 1], 0.0, [], ["rmask"])
        CP(P, "dve", mk2[:, 0, :], C["mnext"], [], ["mk2"])
        CP(P, "dve", mk2[:, 1, :], C["mprev"], [], ["mk2"])
        ti = 0
        ei = 0
        for pr in range(4):
            DMA(P, "sp", qk[:, 0, :], EFM[pr * 128:(pr + 1) * 128, :], [], ["qraw"])
            DMA(P, "sp", qk[:, 1, :], EFM[512 + pr * 128:512 + (pr + 1) * 128, :], [], ["kraw"])
            DMA(P, "sp", Vp[:], ETM[:, pr * 256:(pr + 1) * 256].rearrange("(n p) c -> p n c", p=128), [], ["Vp"])
            for dr in range(2):
                for tb in range(NT5):
                    j = ti % 2
                    ti += 1
                    tsl = slice(tb * 512, (tb + 1) * 512)
                    MM(P, bank(0), wdec[0:32, dr * 512 + pr * 128:dr * 512 + (pr + 1) * 128], lrT[0:32, tsl], True, True,
                       ["wdec", "lrT"], [("ps", 0)])
                    ACT(P, tmpe[:, j, :], bank(0), AF.Exp, ["negb"], [("ps", 0), ("tmpe", j)],
                        bias=negb[:, dr * 4 + pr:dr * 4 + pr + 1], scale=-1.0)
                    ACT(P, tmps[:, j, :], tmpe[:, j, :], AF.Ln, [("tmpe", j)], [("tmps", j)], bias=1.0)
                    if dr == 0:
                        P.op("dve", (lambda o, d1: (lambda e: e.tensor_tensor_scan(out=o, data0=rmask[:], data1=d1, initial=0.0,
                                                                                    op0=ALU.mult, op1=ALU.add)))(cs[:, 0, tsl], tmps[:, j, :]),
                             [("tmps", j), "rmask"], [("cs", 0)])
                    else:
                        P.op("dve", (lambda o, d1: (lambda e: e.tensor_tensor_scan(out=o, data0=rmask[:], data1=d1, initial=0.0,
                                                                                    op0=ALU.mult, op1=ALU.add)))(tmpp[:, j, :], tmps[:, j, :]),
                             [("tmps", j), "rmask"], [("tmpp", j)])
                        TT(P, "dve", tmps[:, j, :], tmps[:, j, :], tmpp[:, j, :], ALU.subtract, [("tmpp", j)], [("tmps", j)])
                        TT(P, "dve", cs[:, 1, tsl].rearrange("p (c t) -> p c t", c=4),
                           tmps[:, j, :].rearrange("p (c t) -> p c t", c=4),
                           tmpp[:, j, :].rearrange("p (c t) -> p c t", c=4)[:, :, 127:128].to_broadcast([128, 4, 128]),
                           ALU.add, [("tmps", j), ("tmpp", j)], [("cs", 1)])
            for dr in range(2):
                for tb in range(NT5):
                    tsl = slice(tb * 512, (tb + 1) * 512)
                    j = ei % 2
                    ei += 1
                    ACT(P, tmpe[:, j, :], cs[:, dr, tsl], AF.Exp, [("cs", dr)], [("tmpe", j)], scale=-1.0 / 16)
                    STT(P, qkt[:, 2 * dr, tsl], qk[:, 0, tsl], 0.125, tmpe[:, j, :], ALU.mult, ALU.mult,
                        ["qraw", ("tmpe", j)], [("qkt", 2 * dr)])
                    ACT(P, tmpp[:, j, :], cs[:, dr, tsl], AF.Exp, [("cs", dr)], [("tmpp", j)], scale=1.0 / 16)
                    TT(P, "dve", qkt[:, 2 * dr + 1, tsl], qk[:, 1, tsl], tmpp[:, j, :], ALU.mult,
                       ["kraw", ("tmpp", j)], [("qkt", 2 * dr + 1)])
            cs3f = cs[:, 0, :].rearrange("p (n t) -> p n t", t=128)
            cs3b = cs[:, 1, :].rearrange("p (n t) -> p n t", t=128)
            ACT(P, dl[:, 0, :], cs3f[:, :, 127], AF.Exp, [("cs", 0)], ["dl"], scale=-1.0 / 16)
            ACT(P, dl[:, 1, :], cs3b[:, :, 0], AF.Exp, [("cs", 1)], ["dl"], scale=-1.0 / 16)
            for dr in range(2):
                for n0 in range(0, NB, 4):
                    ptb = bank(1).bitcast(BF16)
                    for c in range(4):
                        n = n0 + c
                        TR(P, ptb[:, c * 128:(c + 1) * 128], qkt[:, 2 * dr + 1, n * 128:(n + 1) * 128], C["ident"],
                           [("qkt", 2 * dr + 1)], [("ps", 1)])
                    CP(P, "act" if (n0 // 4) % 2 else "dve", ktok[:, dr, n0 * 128:(n0 + 4) * 128], ptb[:, 0:512], [],
                       [("ps", 1), ("ktok", dr)])
            for dr in range(2):
                MEMSET(P, "dve", S[:, dr, :], 0.0, [], [("S", dr)])
            for idx in range(NB):
                for dr in range(2):
                    kS = ("S", dr)
                    n = idx if dr == 0 else NB - 1 - idx
                    CP(P, "act", Sprev[:, dr, n * 128:(n + 1) * 128], S[:, dr, :], [kS], [("Sprev", dr)])
                    if idx == NB - 1:
                        continue
                    bU = 2 if dr == 0 else 5
                    pU = bank(bU, 128)
                    for hh in range(2):
                        MM(P, pU[hh * 64:(hh + 1) * 64, :], ktok[:, dr, n * 128 + hh * 64:n * 128 + (hh + 1) * 64],
                           Vp[:, n, hh * 128:(hh + 1) * 128], True, True, [("ktok", dr), "Vp"], [("ps", bU)])
                    TT(P, "dve", tS[:, dr, :], pU, S[:, dr, :], ALU.add, [kS], [("ps", bU), ("tS", dr)])
                    ACT(P, S[:, dr, :], tS[:, dr, :], AF.Copy, [("tS", dr), "dl"], [kS], scale=dl[:, dr, n:n + 1])
            def stageA(n):
                bAs = (3, 4) if n % 2 == 0 else (0, 1)
                aj = n % 2
                for dr in range(2):
                    for hh in range(2):
                        pA = bank(bAs[hh])
                        MM(P, pA[:, dr * 128:(dr + 1) * 128], qkt[hh * 64:(hh + 1) * 64, 2 * dr + 1, n * 128:(n + 1) * 128],
                           qkt[hh * 64:(hh + 1) * 64, 2 * dr, n * 128:(n + 1) * 128], True, True,
                           [("qkt", 2 * dr + 1), ("qkt", 2 * dr)], [("ps", bAs[hh])])
                for hh in range(2):
                    TT(P, "dve", ATm[:, aj, hh * 256:(hh + 1) * 256].rearrange("p (d t) -> p d t", d=2),
                       bank(bAs[hh])[:, 0:256].rearrange("p (d t) -> p d t", d=2),
                       mk2[:], ALU.mult, ["mk2"], [("ps", bAs[hh]), ("ATm", aj)])

            def stageB(n, pr=pr):
                aj = n % 2
                for hh in range(2):
                    pO = bank(5 + hh)[:, (n % 4) * 128:(n % 4 + 1) * 128]
                    kO = ("ps", 5 + hh)
                    for dr in range(2):
                        MM(P, pO, Vp[:, n, hh * 128:(hh + 1) * 128], ATm[:, aj, (hh * 2 + dr) * 128:(hh * 2 + dr + 1) * 128],
                           dr == 0, False, ["Vp", ("ATm", aj)], [kO])
                        MM(P, pO, Sprev[hh * 64:(hh + 1) * 64, dr, n * 128:(n + 1) * 128],
                           qkt[hh * 64:(hh + 1) * 64, 2 * dr, n * 128:(n + 1) * 128], False, dr == 1,
                           [("Sprev", dr), ("qkt", 2 * dr)], [kO])
                if n % 4 == 3:
                    tsl = slice((n - 3) * 128, (n + 1) * 128)
                    for hh in range(2):
                        h = pr * 2 + hh
                        kO = ("ps", 5 + hh)
                        pO = bank(5 + hh)
                        DMA(P, "sp", sgt[:, hh, :], EFM[1024 + h * 128:1024 + (h + 1) * 128, tsl], [], [("sgt", hh)])
                        ACT(P, sq[:, hh, :], pO, AF.Square, [], [kO, ("sq", hh)])
                        MM(P, bank(7), C["ones"], sq[:, hh, :], True, True, [("sq", hh)], [("ps", 7)])
                        ACT(P, rt[:, hh, :], bank(7), AF.Sqrt, [], [("ps", 7), ("rt", hh)], bias=EPS, scale=1.0 / 128)
                        RECIP(P, rt[:, hh, :], rt[:, hh, :], [], [("rt", hh)])
                        TT(P, "dve", o1[:, hh, :], pO, rt[:, hh, :], ALU.mult, [("rt", hh)], [kO, ("o1", hh)])
                        STT(P, ostg[:, hh, :], o1[:, hh, :], gnT[:, h:h + 1], sgt[:, hh, :], ALU.mult, ALU.mult,
                            [("o1", hh), "gnT", ("sgt", hh)], [("ostg", hh)])
                        DMA(P, "sp", MIXT[h * 128:(h + 1) * 128, tsl], ostg[:, hh, :], [("ostg", hh)], [])

            stageA(0)
            for n in range(NB):
                if n + 1 < NB:
                    stageA(n + 1)
                stageB(n)


def phase_SSD(P, C, OFM, DTR, MIXT, XBCS, STATES, PREV, prm, T):
    nc = P.nc
    NB = T // 128
    NT5 = T // 512
    with contextlib.ExitStack() as sst:
        _phase_SSD(P, C, OFM, DTR, MIXT, XBCS, STATES, PREV, prm, T, sst)


def _phase_SSD(P, C, OFM, DTR, MIXT, XBCS, STATES, PREV, prm, T, sst):
    nc = P.nc
    NB = T // 128
    NT5 = T // 512
    dt = sb(sst, nc, "ssd_dt", [128, NB, 32], F32)
    acum = sb(sst, nc, "ssd_acum", [128, NB, 32], F32)
    nacum = sb(sst, nc, "ssd_nacum", [128, NB, 32], F32)
    EA = sb(sst, nc, "ssd_EA", [128, NB, 32], F32)
    acT = sb(sst, nc, "ssd_acT", [16, 2, T], F32)
    with P.phase() as st:
        cw = sb(st, nc, "convw", [128, 60], F32)
        cb = sb(st, nc, "convb", [128, 12], F32)
        diagw = sb(st, nc, "diagw", [128, 60, 128], BF16)
        dtb = sb(st, nc, "dtb", [128, 32], F32)
        Aneg = sb(st, nc, "Aneg", [128, 32], F32)
        da = sb(st, nc, "ssd_da", [128, NB, 32], F32)
        dtdec = sb(st, nc, "ssd_dtdec", [128, NB, 32], F32)
        rawx = sb(st, nc, "rawx", [128, 2, 12 * 516], BF16)
        xbc = sb(st, nc, "xbc", [128, 2, 12 * 512], BF16)
        xdd = sb(st, nc, "xdd", [128, 2, 2 * 1024], BF16)
        btok = sb(st, nc, "btok", [128, 2, 256], BF16)
        stsb = sb(st, nc, "stsb", [128, 2, 2 * 1024], F32)
        ps = pst(st, nc, "psS1", [128, 4096], F32)

        def bank(i, w=512):
            return ps[:, i * 512:i * 512 + w]

        DMA(P, "sp", cw[:], prm["convwT"], [], ["cw"])
        DMA(P, "sp", cb[:], prm["convbT"], [], ["cb"])
        DMA(P, "sp", dtb[:], prm["dtb"].partition_broadcast(128), [], ["dtb"])
        DMA(P, "sp", Aneg[:], prm["alog"].partition_broadcast(128), [], ["Aneg"])
        ACT(P, Aneg[:], Aneg[:], AF.Exp, ["Aneg"], ["Aneg"])
        TS(P, "dve", Aneg[:], Aneg[:], -1.0, None, ALU.mult, None, ["Aneg"], ["Aneg"])
        for i in range(60):
            TS(P, "dve" if i % 2 else "pool", diagw[:, i, :], C["ident"], cw[:, i:i + 1], None, ALU.mult, None, ["cw"], ["diagw"])
        DMA(P, "sp", dt[:], DTR.rearrange("(n p) c -> p n c", p=128), [], ["dt"])
        TT(P, "dve", dt[:], dt[:], dtb[:].unsqueeze(1).to_broadcast([128, NB, 32]), ALU.add, ["dtb"], ["dt"])
        ACT(P, dt[:], dt[:], AF.Exp, ["dt"], ["dt"])
        ACT(P, dt[:], dt[:], AF.Ln, ["dt"], ["dt"], bias=1.0)
        TT(P, "dve", da[:], dt[:], Aneg[:].unsqueeze(1).to_broadcast([128, NB, 32]), ALU.mult, ["dt", "Aneg"], ["da"])
        for dr in range(2):
            tri = C["mnext32"] if dr == 0 else C["mprev32"]
            MM(P, bank(0).rearrange("p (n h) -> p n h", h=16), tri, da[:, :, dr * 16:(dr + 1) * 16], True, True,
               ["da"], [("ps", 0)])
            CP(P, "dve", acum[:, :, dr * 16:(dr + 1) * 16], bank(0).rearrange("p (n h) -> p n h", h=16), [],
               [("ps", 0), "acum"])
        TS(P, "dve", nacum[:], acum[:], -1.0, None, ALU.mult, None, ["acum"], ["nacum"])
        for hf in range(2):
            nsl = slice(hf * (NB // 2), (hf + 1) * (NB // 2))
            MM(P, bank(1).rearrange("p (n h) -> p n h", h=32), C["ones32"], da[:, nsl, :], True, True, ["da"], [("ps", 1)])
            TT(P, "dve", dtdec[:, nsl, :], bank(1).rearrange("p (n h) -> p n h", h=32), acum[:, nsl, :], ALU.subtract,
               ["acum"], [("ps", 1), "dtdec"])
            ACT(P, EA[:, nsl, :], bank(1).rearrange("p (n h) -> p n h", h=32), AF.Exp, [], [("ps", 1), "EA"])
        ACT(P, dtdec[:], dtdec[:], AF.Exp, ["dtdec"], ["dtdec"])
        TT(P, "dve", dtdec[:], dtdec[:], dt[:], ALU.mult, ["dt"], ["dtdec"])
        for dr in range(2):
            tri = C["mnext32"] if dr == 0 else C["mprev32"]
            for n0 in range(0, NB, 4):
                for c in range(4):
                    n = n0 + c
                    MM(P, ps[0:16, 1024 + c * 128:1024 + (c + 1) * 128], da[:, n, dr * 16:(dr + 1) * 16], tri, True, True,
                       ["da"], [("ps", 2)])
                CP(P, "act", acT[:, dr, n0 * 128:(n0 + 4) * 128], ps[0:16, 1024:1536], [], [("ps", 2), ("acT", dr)])
        for tb in range(NT5):
            j = tb % 2
            t0 = tb * 512
            kr, kx = ("rawx", j), ("xbc", j)
            rv = rawx[:, j, :].rearrange("p (c t) -> p c t", c=12)
            xv = xbc[:, j, :].rearrange("p (c t) -> p c t", c=12)
            lo = max(t0 - 2, 0)
            hi = min(t0 + 514, T)
            if tb == 0:
                MEMSET(P, "dve", rv[:, :, 0:2], 0.0, [], [kr])
            if tb == NT5 - 1:
                MEMSET(P, "dve", rv[:, :, 514:516], 0.0, [], [kr])
            DMA(P, "sp", rv[:, :, lo - (t0 - 2):hi - (t0 - 2)], OFM[1024:2560, lo:hi].rearrange("(c p) t -> p c t", p=128), [], [kr])
            for ch in range(12):
                b = 4 + (ch % 2)
                for jj in range(5):
                    MM(P, bank(b), diagw[:, ch * 5 + jj, :], rv[:, ch, jj:jj + 512], jj == 0, jj == 4, [kr, "diagw"], [("ps", b)])
                ACT(P, xv[:, ch, :], bank(b), AF.Silu, ["cb"], [("ps", b), kx], bias=cb[:, ch:ch + 1])
            DMA(P, "sp", XBCS[:, t0:t0 + 512].rearrange("(c p) t -> p c t", p=128), xv, [kx], [])
            def s1(c, tb=tb, xv=xv, kx=kx):
                n = tb * 4 + c
                cj = n % 2
                tsl = slice(c * 128, (c + 1) * 128)
                pX = bank(6).bitcast(BF16)
                pB = bank(7).bitcast(BF16)
                for k in range(8):
                    TR(P, pX[:, k * 128:(k + 1) * 128], xv[:, k, tsl], C["ident"], [kx], [("ps", 6)])
                for g in range(2):
                    TR(P, pB[:, g * 128:(g + 1) * 128], xv[:, 8 + g, tsl], C["ident"], [kx], [("ps", 7)])
                for dr in range(2):
                    TT(P, "dve", xdd[:, cj, dr * 1024:(dr + 1) * 1024].rearrange("p (h q) -> p h q", h=16),
                       pX[:, 0:1024].rearrange("p (h q) -> p h q", h=16),
                       dtdec[:, n, dr * 16:(dr + 1) * 16].unsqueeze(2).to_broadcast([128, 16, 64]), ALU.mult,
                       ["dtdec"], [("ps", 6), ("xdd", cj)])
                CP(P, "act", btok[:, cj, :], pB[:, 0:256], [], [("ps", 7), ("btok", cj)])

            def s2(c, tb=tb):
                n = tb * 4 + c
                cj = n % 2
                for dr in range(2):
                    for g in range(2):
                        b = dr * 2 + g
                        MM(P, bank(b), btok[:, cj, g * 128:(g + 1) * 128],
                           xdd[:, cj, dr * 1024 + g * 512:dr * 1024 + (g + 1) * 512], True, True,
                           [("btok", cj), ("xdd", cj)], [("ps", b)])
                        CP(P, "act" if g else "dve", stsb[:, cj, dr * 1024 + g * 512:dr * 1024 + (g + 1) * 512], bank(b), [],
                           [("ps", b), ("stsb", cj)])
                for dr in range(2):
                    DMA(P, "sp", STATES[dr, n], stsb[:, cj, dr * 1024:(dr + 1) * 1024], [("stsb", cj)], [])

            s1(0)
            for c in range(4):
                if c + 1 < 4:
                    s1(c + 1)
                s2(c)
    with P.phase() as st:
        prev = sb(st, nc, "prevS", [128, 2, 2 * 1024], F32)
        tmpm = sb(st, nc, "tmpm", [128, 2, 1024], F32)
        stl = sb(st, nc, "stl", [128, 4, 1024], F32)
        pvb = sb(st, nc, "pvb", [128, 4, 1024], BF16)
        for dr in range(2):
            MEMSET(P, "dve", prev[:, dr, 0:1024], 0.0, [], [("prev", dr, 0)])
        for idx in range(NB):
            for dr in range(2):
                n = idx if dr == 0 else NB - 1 - idx
                a_, b_ = idx % 2, (idx + 1) % 2
                j = (idx % 2) + 2 * dr
                pa = prev[:, dr, a_ * 1024:(a_ + 1) * 1024]
                pb = prev[:, dr, b_ * 1024:(b_ + 1) * 1024]
                CP(P, "act", pvb[:, j, :], pa, [("prev", dr, a_)], [("pvb", j)])
                DMA(P, "sp", PREV[dr, n], pvb[:, j, :], [("pvb", j)], [])
                if idx == NB - 1:
                    continue
                DMA(P, "sp", stl[:, j, :], STATES[dr, n], [], [("stl", j)])
                TT(P, "dve", tmpm[:, dr, :].rearrange("p (h q) -> p h q", h=16), pa.rearrange("p (h q) -> p h q", h=16),
                   EA[:, n, dr * 16:(dr + 1) * 16].unsqueeze(2).to_broadcast([128, 16, 64]), ALU.mult,
                   [("prev", dr, a_)], [("tmpm", dr)])
                TT(P, "dve", pb, tmpm[:, dr, :], stl[:, j, :], ALU.add, [("tmpm", dr), ("stl", j)], [("prev", dr, b_)])
    with P.phase() as st:
        dsk = sb(st, nc, "dsk", [128, 8], F32)
        snT = sb(st, nc, "snT", [128, 8], F32)
        rsel = sb(st, nc, "rsel", [16, 16 * 128], F32)
        negm = sb(st, nc, "negm", [128, 2, 128], F32)
        xbc = sb(st, nc, "xbc2", [128, 2, 12 * 512], BF16)
        zs = sb(st, nc, "zs", [128, 2, 8 * 512], BF16)
        pv = sb(st, nc, "pv", [128, 2, 2 * 1024], BF16)
        xdt = sb(st, nc, "xdt", [128, 2, 2 * 1024], BF16)
        cbt = sb(st, nc, "cbt", [128, 2, 256], F32)
        tmp = sb(st, nc, "tmpL", [128, 2, 512], F32)
        tmp2 = sb(st, nc, "tmpL2", [128, 2, 512], F32)
        LT = sb(st, nc, "LT", [128, 2, 512], F32)
        Eh = sb(st, nc, "Eh", [128, 2, 512], F32)
        MT = sb(st, nc, "MT", [128, 2, 2 * 512], BF16)
        Cp = sb(st, nc, "Cp", [128, 2, 2 * 512], BF16)
        ysb = sb(st, nc, "ysb", [128, 2, 1024], F32)
        sq = sb(st, nc, "sq2", [128, 2, 1024], BF16)
        rt = sb(st, nc, "rt2", [128, 2, 128], F32)
        ost = sb(st, nc, "ost2", [128, 2, 8 * 512], BF16)
        ps = pst(st, nc, "psS2", [128, 4096], F32)

        def bank(i, w=512):
            return ps[:, i * 512:i * 512 + w]

        DMA(P, "sp", dsk[:], prm["dskT"], [], ["dsk"])
        DMA(P, "sp", snT[:], prm["snT"], [], ["snT"])
        DMA(P, "sp", rsel[:], prm["rsel"], [], ["rsel"])
        TS(P, "dve", negm[:, 0, :], C["mnext32"], 30000.0, -30000.0, ALU.mult, ALU.add, [], ["negm"])
        TS(P, "dve", negm[:, 1, :], C["mprev32"], 30000.0, -30000.0, ALU.mult, ALU.add, [], ["negm"])
        li = 0
        for tb in range(NT5):
            j = tb % 2
            t0 = tb * 512
            kx, kz, ko = ("xbc", j), ("zs", j), ("ost", j)
            xv = xbc[:, j, :].rearrange("p (c t) -> p c t", c=12)
            zv = zs[:, j, :].rearrange("p (c t) -> p c t", c=8)
            ov = ost[:, j, :].rearrange("p (c t) -> p c t", c=8)
            DMA(P, "sp", xv, XBCS[:, t0:t0 + 512].rearrange("(c p) t -> p c t", p=128), [], [kx])
            DMA(P, "sp", zv, OFM[0:1024, t0:t0 + 512].rearrange("(c p) t -> p c t", p=128), [], [kz])
            for c in range(4):
                n = tb * 4 + c
                cj = n % 2
                tsl = slice(c * 128, (c + 1) * 128)
                kpv, kxd, kcb = ("pv", cj), ("xdt", cj), ("cbt", cj)
                for dr in range(2):
                    DMA(P, "sp", pv[:, cj, dr * 1024:(dr + 1) * 1024], PREV[dr, n], [], [kpv])
                for g in range(2):
                    MM(P, bank(0)[:, g * 128:(g + 1) * 128], xv[:, 8 + g, tsl], xv[:, 10 + g, tsl], True, True, [kx], [("ps", 0)])
                CP(P, "act", cbt[:, cj, :], bank(0)[:, 0:256], [], [("ps", 0), kcb])
                pX = bank(1).bitcast(BF16)
                for k in range(8):
                    TR(P, pX[:, k * 128:(k + 1) * 128], xv[:, k, tsl], C["ident"], [kx], [("ps", 1)])
                for dr in range(2):
                    TT(P, "dve", xdt[:, cj, dr * 1024:(dr + 1) * 1024].rearrange("p (h q) -> p h q", h=16),
                       pX[:, 0:1024].rearrange("p (h q) -> p h q", h=16),
                       dt[:, n, dr * 16:(dr + 1) * 16].unsqueeze(2).to_broadcast([128, 16, 64]), ALU.mult,
                       [], [("ps", 1), kxd])
                def chain_(hq, li0):
                    g = hq // 2
                    qj = hq % 2
                    km, kc = ("MT", qj), ("Cp", qj)
                    v3 = lambda ap: ap.rearrange("p (h l) -> p h l", h=4)
                    for dr in range(2):
                        lj = (li0 + dr) % 2
                        bA = 2 + lj
                        kt, ke = ("tmp", lj), ("Eh", lj)
                        pA = bank(bA)
                        for i4 in range(4):
                            h = hq * 4 + i4
                            MM(P, pA[:, i4 * 128:(i4 + 1) * 128], rsel[0:16, h * 128:(h + 1) * 128],
                               acT[0:16, dr, n * 128:(n + 1) * 128], True, True, ["rsel"], [("ps", bA)])
                        TT(P, "dve", v3(tmp[:, lj, :]), v3(pA),
                           negm[:, dr, :].unsqueeze(1).to_broadcast([128, 4, 128]),
                           ALU.add, ["negm"], [("ps", bA), kt])
                        ACT(P, Eh[:, lj, :], pA, AF.Exp, [], [("ps", bA), ke])
                    for dr in range(2):
                        lj = (li0 + dr) % 2
                        kt, kl = ("tmp", lj), ("LT", lj)
                        for i4 in range(4):
                            cc = dr * 16 + hq * 4 + i4
                            ACT(P, LT[:, lj, i4 * 128:(i4 + 1) * 128], tmp[:, lj, i4 * 128:(i4 + 1) * 128], AF.Exp,
                                [kt], [kl], bias=nacum[:, n, cc:cc + 1])
                    for dr in range(2):
                        lj = (li0 + dr) % 2
                        kl, ke = ("LT", lj), ("Eh", lj)
                        TT(P, "pool", v3(Cp[:, qj, dr * 512:(dr + 1) * 512]), v3(Eh[:, lj, :]),
                           xv[:, 10 + g, tsl].unsqueeze(1).to_broadcast([128, 4, 128]), ALU.mult, [kx, ke], [kc])
                        TT(P, "dve", v3(MT[:, qj, dr * 512:(dr + 1) * 512]), v3(LT[:, lj, :]),
                           cbt[:, cj, g * 128:(g + 1) * 128].unsqueeze(1).to_broadcast([128, 4, 128]), ALU.mult,
                           [kl, kcb], [km])
                def mm_(hq):
                    qj = hq % 2
                    km, kc = ("MT", qj), ("Cp", qj)
                    for i4 in range(4):
                        h = hq * 4 + i4
                        bY = 4 + (h // 8)
                        pY = bank(bY)[(h % 2) * 64:(h % 2 + 1) * 64, ((h // 2) % 4) * 128:((h // 2) % 4 + 1) * 128]
                        for dr in range(2):
                            o = dr * 512 + i4 * 128
                            MM(P, pY, xdt[:, cj, dr * 1024 + h * 64:dr * 1024 + (h + 1) * 64], MT[:, qj, o:o + 128], dr == 0, False,
                               [kxd, km], [("ps", bY)])
                            MM(P, pY, pv[:, cj, dr * 1024 + h * 64:dr * 1024 + (h + 1) * 64], Cp[:, qj, o:o + 128], False, dr == 1,
                               [kpv, kc], [("ps", bY)])
                chain_(0, li)
                for hq in range(4):
                    if hq + 1 < 4:
                        chain_(hq + 1, li + 2 * (hq + 1))
                    mm_(hq)
                li += 8
                yv = ysb[:, cj, :].rearrange("p (k t) -> p k t", k=8)
                ky = ("ysb", cj)
                for k in range(8):
                    pYk = bank(4 + k // 4)[:, (k % 4) * 128:(k % 4 + 1) * 128]
                    STT(P, yv[:, k, :], xv[:, k, tsl], dsk[:, k:k + 1], pYk, ALU.mult, ALU.add, [kx, "dsk"], [("ps", 4 + k // 4), ky])
                TT(P, "dve", yv, yv, zv[:, :, tsl], ALU.mult, [kz], [ky])
                ACT(P, sq[:, cj, :], ysb[:, cj, :], AF.Square, [ky], [("sq", cj)])
                for k in range(8):
                    MM(P, bank(6, 128), C["ones"], sq[:, cj, k * 128:(k + 1) * 128], k == 0, k == 7, [("sq", cj)], [("ps", 6)])
                ACT(P, rt[:, cj, :], bank(6, 128), AF.Sqrt, [], [("ps", 6), ("rt", cj)], bias=EPS, scale=1.0 / 1024)
                RECIP(P, rt[:, cj, :], rt[:, cj, :], [], [("rt", cj)])
                TT(P, "dve", yv, yv, rt[:, cj, :].unsqueeze(1).to_broadcast([128, 8, 128]), ALU.mult, [("rt", cj)], [ky])
                TT(P, "dve", ov[:, :, tsl], yv, snT[:].unsqueeze(2).to_broadcast([128, 8, 128]), ALU.mult, ["snT"], [ky, ko])
            DMA(P, "sp", MIXT[0:1024, t0:t0 + 512].rearrange("(c p) t -> p c t", p=128), ov, [ko], [])


def ssd_host_params(conv_w, conv_b, dt_bias, a_log, d_skip, ssd_norm):
    p = {}
    p["convwT"] = np.ascontiguousarray(conv_w.reshape(5, 12, 128).transpose(2, 1, 0).reshape(128, 60))
    p["convbT"] = np.ascontiguousarray(conv_b.reshape(12, 128).T)
    p["dtb"] = np.ascontiguousarray(dt_bias.reshape(1, 32))
    p["alog"] = np.ascontiguousarray(a_log.reshape(1, 32))
    p["dskT"] = np.ascontiguousarray(np.repeat(d_skip, 64).reshape(8, 128).T)
    p["snT"] = np.ascontiguousarray(ssd_norm.reshape(8, 128).T)
    return p


def make_rsel():
    r = np.zeros((16, 16, 128), np.float32)
    for i in range(16):
        r[i, i, :] = 1.0
    return r.reshape(16, 16 * 128)


NCORES = 4
EVEN_BLOCKS = [
    dict(col0=0, width=512, mode="fm", dst="EFM", off=0, dt=BF16),
    dict(col0=512, width=512, mode="fm", dst="EFM", off=512, dt=BF16),
    dict(col0=1024, width=1024, mode="tm", dst="ETM", off=0, dt=BF16),
    dict(col0=2048, width=1024, mode="fm", dst="EFM", off=1024, dt=BF16, silu=True),
    dict(col0=3072, width=32, mode="fm", dst="LRT", off=0, dt=F32),
    dict(col0=3104, width=1024, mode="fm", dst="EFM", off=2048, dt=BF16),
    dict(col0=4128, width=1024, mode="fm", dst="EFM", off=3072, dt=BF16),
    dict(col0=5152, width=1024, mode="tm", dst="ETM", off=1024, dt=BF16),
]
ODD_BLOCKS = [
    dict(col0=0, width=1024, mode="fm", dst="OFM", off=0, dt=BF16, silu=True),
    dict(col0=1024, width=1536, mode="fm", dst="OFM", off=1024, dt=BF16),
    dict(col0=2560, width=32, mode="tm", dst="DTR", off=0, dt=F32),
    dict(col0=2592, width=1024, mode="fm", dst="OFM", off=2560, dt=BF16),
    dict(col0=3616, width=256, mode="fm", dst="OFM", off=3584, dt=BF16),
    dict(col0=3872, width=256, mode="tm", dst="OTM", off=0, dt=BF16),
]

INPUT_SHAPES = {
    "x": [T_FULL, D], "norm_gains": [4, 4, D], "gainsT": [128, 256],
    "ffn_w_gate": [4, D, HID], "ffn_w_up": [4, D, HID], "ffn_w_down": [4, HID, D],
    "even_w_in": [2, D, EVEN_IN], "even_w_out": [2, D, D], "odd_w_in": [2, D, ODD_IN], "odd_w_out": [2, D, D],
    "wdec": [2, 32, 2, 512], "bdecT": [2, 128, 8], "gnT": [2, 128, 8], "nab": [2, 8, 128, 3200],
    "convwT": [2, 128, 60], "convbT": [2, 128, 12], "dtb": [2, 1, 32], "alog": [2, 1, 32], "dskT": [2, 128, 8],
    "snT": [2, 128, 8], "sink": [2, 1, 8],
    "c_ident": [128, 128], "c_ones": [128, 128], "c_rotm": [128, 128], "c_mprev": [128, 128], "c_mnext": [128, 128],
    "c_cosT": [128, T_FULL], "c_sinT": [128, T_FULL], "c_rsel": [16, 2048],
}


def build_program(layers=(0, 1, 2, 3)):
    T = T_FULL
    NB = T // 128
    nc = bass.Bass("TRN2", target_bir_lowering=False)
    P = Prog(nc)
    I = {k: nc.dram_tensor(k, v, F32, kind="ExternalInput").ap() for k, v in INPUT_SHAPES.items()}
    out = nc.dram_tensor("out", [T, D], F32, kind="ExternalOutput").ap()
    S = {
        "XS": nc.dram_tensor("XS", [T, D], F32, kind="Internal").ap(),
        "EFM": nc.dram_tensor("EFM", [4096, T], BF16, kind="Internal").ap(),
        "ETM": nc.dram_tensor("ETM", [T, 2048], BF16, kind="Internal").ap(),
        "LRT": nc.dram_tensor("LRT", [32, T], F32, kind="Internal").ap(),
        "OFM": nc.dram_tensor("OFM", [3840, T], BF16, kind="Internal").ap(),
        "OTM": nc.dram_tensor("OTM", [T, 256], BF16, kind="Internal").ap(),
        "DTR": nc.dram_tensor("DTR", [T, 32], F32, kind="Internal").ap(),
        "MIXT": nc.dram_tensor("MIXT", [D, T], BF16, kind="Internal").ap(),
        "XBCS": nc.dram_tensor("XBCS", [1536, T], BF16, kind="Internal").ap(),
        "STATES": nc.dram_tensor("STATES", [2, NB, 128, 1024], F32, kind="Internal").ap(),
        "PREV": nc.dram_tensor("PREV", [2, NB, 128, 1024], BF16, kind="Internal").ap(),
    }
    C = {}
    gT = sb(P.outer, nc, "gTall", [128, 256], F32)
    with P.phase() as st:
        for k in ["ident", "ones", "rotm", "mprev", "mnext"]:
            t = sb(P.outer, nc, "cb_" + k, [128, 128], BF16)
            DMA(P, "pool", t[:], I["c_" + k], [], [k])
            C[k] = t[:]
        for k in ["ones", "mprev", "mnext"]:
            t = sb(P.outer, nc, "cf_" + k, [128, 128], F32)
            DMA(P, "sp", t[:], I["c_" + k], [], [k + "32"])
            C[k + "32"] = t[:]
        DMA(P, "sp", gT[:], I["gainsT"], [], ["gT"])
    first, last = layers[0], layers[-1]
    for L in layers:
        i = L // 2
        x_src = I["x"] if L == first else S["XS"]
        x_dst = out if L == last else S["XS"]
        g = lambda a: gT[:, (L * 4 + a) * 16:(L * 4 + a + 1) * 16]
        if L % 2 == 0:
            blocks = [dict(b, dst=S[b["dst"]]) for b in EVEN_BLOCKS]
            phase_A(P, C, x_src, g(0), I["even_w_in"][i], blocks, T)
            phase_GLA(P, C, S["EFM"], S["ETM"], S["LRT"], S["MIXT"], I["wdec"][i], I["bdecT"][i], I["gnT"][i], T)
            phase_NA(P, C, S["EFM"], S["ETM"], S["MIXT"], I["nab"][i], T)
            Wout = I["even_w_out"][i]
        else:
            blocks = [dict(b, dst=S[b["dst"]]) for b in ODD_BLOCKS]
            phase_A(P, C, x_src, g(0), I["odd_w_in"][i], blocks, T)
            prm = {k: I[k][i] for k in ["convwT", "convbT", "dtb", "alog", "dskT", "snT"]}
            prm["rsel"] = I["c_rsel"]
            phase_SSD(P, C, S["OFM"], S["DTR"], S["MIXT"], S["XBCS"], S["STATES"], S["PREV"], prm, T)
            phase_SWA(P, C, S["OFM"], S["OTM"], S["MIXT"], I["c_cosT"], I["c_sinT"], I["sink"][i], T)
            Wout = I["odd_w_out"][i]
        phase_CD(P, C, S["MIXT"], x_src, x_dst, I["norm_gains"][L, 1:2, :], g(2), I["norm_gains"][L, 3:4, :],
                 Wout, I["ffn_w_gate"][L], I["ffn_w_up"][L], I["ffn_w_down"][L], T)
    n_ops = P.n_ops
    P.close()
    return nc, n_ops


def host_inputs(inp):
    f = lambda a: np.ascontiguousarray(np.asarray(a, dtype=np.float32))
    h = {}
    for k in ["norm_gains", "ffn_w_gate", "ffn_w_up", "ffn_w_down", "even_w_in", "even_w_out", "odd_w_in", "odd_w_out"]:
        h[k] = f(inp[k])
    g = f(inp["norm_gains"])
    h["gainsT"] = f(g.reshape(4, 4, 16, 128).transpose(3, 0, 1, 2).reshape(128, 256))
    wd = f(inp["gla_w_decay"])
    wdec = np.zeros((2, 32, 2, 512), np.float32)
    wdec[:, 0:16, 0, :] = wd[:, 0]
    wdec[:, 16:32, 1, :] = wd[:, 1]
    h["wdec"] = wdec
    bd = f(inp["gla_b_decay"])
    h["bdecT"] = f(bd.reshape(2, 2, 4, 128).transpose(0, 3, 1, 2).reshape(2, 128, 8))
    h["gnT"] = f(f(inp["gla_norm"]).reshape(2, 8, 128).transpose(0, 2, 1))
    rpb = f(inp["na_rpb"])
    h["nab"] = np.stack([na_bias_tables(rpb[i]) for i in range(2)])
    sp = [ssd_host_params(f(inp["ssd_conv_w"])[i], f(inp["ssd_conv_b"])[i], f(inp["ssd_dt_bias"])[i],
                          f(inp["ssd_a_log"])[i], f(inp["ssd_d"])[i], f(inp["ssd_norm"])[i]) for i in range(2)]
    for k in ["convwT", "convbT", "dtb", "alog", "dskT", "snT"]:
        h[k] = np.stack([sp[i][k] for i in range(2)])
    h["sink"] = f(inp["swa_sink"]).reshape(2, 1, 8)
    for k, v in make_consts().items():
        h["c_" + k] = v
    h["c_rsel"] = make_rsel()
    return h


_CACHE = {}


def kernel(**inputs):
    x = np.asarray(inputs["x"], dtype=np.float32)
    h = host_inputs(inputs)
    if "nc" not in _CACHE:
        _CACHE["nc"] = build_program()[0]
    nc = _CACHE["nc"]
    in_maps = []
    for c in range(NCORES):
        m = dict(h)
        m["x"] = np.ascontiguousarray(x[c % 4])
        in_maps.append(m)
    res = run_bass_kernel_spmd(nc, in_maps, core_ids=list(range(NCORES)))
    return np.stack([np.asarray(res.results[c]["out"], dtype=np.float32) for c in range(4)], axis=0)
```
